# Optimizing a Trainium2 kernel written in Bass

```python
import math
import jax, jax.numpy as jnp
from jax import lax
import numpy as np

D_MODEL = 2048
BATCH = 2
SEQ = 8192
DEPTH = 1

N_META = 16
CONV_W = D_MODEL // 2
N_CONV_GROUPS = 16
CONV_K = 3
RWKV_W = D_MODEL // 2
HEAD_N = 64
N_RWKV_HEADS = RWKV_W // HEAD_N
DECAY_LORA = 96
AAA_LORA = 96
GATE_LORA = 256
D_FF = 5632
FFN_K = 3
RMS_EPS = 1e-6
GN_EPS = 64e-5
DECAY_SCALE = math.exp(-0.5)

CONV_COLS = 3 * CONV_W
RWKV_COLS = 3 * RWKV_W + DECAY_LORA + AAA_LORA + GATE_LORA
IN_COLS = CONV_COLS + RWKV_COLS

kernel_name = "hymba_conv_rwkv7_hybrid_layer"


def rmsnorm(x, g):
    xf = x.astype(jnp.float32)
    y = xf * lax.rsqrt(jnp.mean(xf * xf, axis=-1, keepdims=True) + RMS_EPS)
    return (y * g.astype(jnp.float32)).astype(x.dtype)


def causal_dwconv(x, w):
    k = w.shape[0]
    seq_len = x.shape[1]
    xp = jnp.pad(x, ((0, 0), (k - 1, 0), (0, 0)))
    y = xp[:, 0:seq_len] * w[0]
    for i in range(1, k):
        y = y + xp[:, i:i + seq_len] * w[i]
    return y


def token_shift(z, mu):
    z_prev = jnp.pad(z, ((0, 0), (1, 0), (0, 0)))[:, :-1]
    return z + (z_prev - z) * mu


def rwkv7_scan(r, w, k, v, a_vec, b_vec):
    bsz, _, h, n = r.shape
    xs = tuple(jnp.swapaxes(t, 0, 1) for t in (r, w, k, v, a_vec, b_vec))

    def step(s, inp):
        r_t, w_t, k_t, v_t, a_t, b_t = inp
        sa = jnp.einsum('bhvk,bhk->bhv', s, a_t)
        s = s * w_t[:, :, None, :] + sa[..., None] * b_t[:, :, None, :] + v_t[..., None] * k_t[:, :, None, :]
        y_t = jnp.einsum('bhvk,bhk->bhv', s, r_t)
        return s, y_t

    s0 = jnp.zeros((bsz, h, n, n), jnp.float32)
    _, ys = lax.scan(step, s0, xs)
    return jnp.swapaxes(ys, 0, 1)


def short_conv_group(zc, conv_w):
    gate_b, gate_c, hx = jnp.split(zc, 3, axis=-1)
    return gate_b * causal_dwconv(gate_c * hx, conv_w)


def rwkv7_group(zr, mu_rwkv, w_decay_up, w0, a_up, a0, g_up, k_k, k_a, r_k, ln_x_w, ln_x_b):
    bsz, seq_len, _ = zr.shape
    zr = token_shift(zr, mu_rwkv)
    o1 = RWKV_W; o2 = 2 * RWKV_W; o3 = 3 * RWKV_W
    o4 = o3 + DECAY_LORA; o5 = o4 + AAA_LORA
    r = zr[..., :o1]
    k = zr[..., o1:o2]
    v = zr[..., o2:o3]
    wd = zr[..., o3:o4]
    ad = zr[..., o4:o5]
    gd = zr[..., o5:]

    f32 = jnp.float32
    log_w = -DECAY_SCALE * jax.nn.sigmoid((w0 + jnp.tanh(wd) @ w_decay_up).astype(f32))
    a = jax.nn.sigmoid(a0 + ad @ a_up)
    g = jax.nn.sigmoid(gd) @ g_up

    hs = (bsz, seq_len, N_RWKV_HEADS, HEAD_N)
    kk = (k * k_k).astype(f32).reshape(hs)
    kk = kk / jnp.maximum(jnp.sqrt(jnp.sum(kk * kk, axis=-1, keepdims=True)), 1e-12)
    k = k * (1.0 + (a - 1.0) * k_a)

    r_h = r.astype(f32).reshape(hs)
    k_h = k.astype(f32).reshape(hs)
    v_h = v.astype(f32).reshape(hs)
    a_h = a.astype(f32).reshape(hs)
    y = rwkv7_scan(r_h, jnp.exp(log_w).reshape(hs), k_h, v_h, -kk, kk * a_h)

    mean = jnp.mean(y, axis=-1, keepdims=True)
    var = jnp.mean(jnp.square(y - mean), axis=-1, keepdims=True)
    y = (y - mean) * lax.rsqrt(var + GN_EPS)
    y = y.reshape(bsz, seq_len, RWKV_W) * ln_x_w.astype(f32) + ln_x_b.astype(f32)
    bonus = jnp.sum(r_h * k_h * r_k.astype(f32), axis=-1, keepdims=True) * v_h
    y = y + bonus.reshape(bsz, seq_len, RWKV_W)
    return (y * g.astype(f32)).astype(zr.dtype)


def conv_glu(x, w_gate_up, conv_ffn, w_down):
    gu = x @ w_gate_up
    gate, up = jnp.split(gu, 2, axis=-1)
    gate = causal_dwconv(gate, conv_ffn)
    return (jax.nn.silu(gate) * up) @ w_down


def setup_inputs(seed: int = 0) -> dict:
    key = jax.random.key(seed)
    ks = jax.random.split(key, 24)
    f32 = jnp.float32
    nrm = lambda k, shape, s: jax.random.normal(k, shape, f32) * s
    d = D_MODEL
    return {
        "x": nrm(ks[0], (BATCH, SEQ, d), 1.0),
        "meta_tokens": nrm(ks[1], (N_META, d), 1.0),
        "g_pre_mix": 1.0 + nrm(ks[2], (d,), 0.05),
        "w_in": nrm(ks[3], (d, IN_COLS), d ** -0.5),
        "conv_w": nrm(ks[4], (CONV_K, CONV_W), CONV_K ** -0.5),
        "mu_rwkv": jax.random.uniform(ks[5], (RWKV_COLS,), f32, 0.1, 0.9),
        "w_decay_up": nrm(ks[6], (DECAY_LORA, RWKV_W), 0.1 * DECAY_LORA ** -0.5),
        "w0": nrm(ks[7], (RWKV_W,), 0.5) - 0.5,
        "a_up": nrm(ks[8], (AAA_LORA, RWKV_W), 0.1 * AAA_LORA ** -0.5),
        "a0": nrm(ks[9], (RWKV_W,), 0.1),
        "g_up": nrm(ks[10], (GATE_LORA, RWKV_W), GATE_LORA ** -0.5),
        "k_k": 0.85 + nrm(ks[11], (RWKV_W,), 0.05),
        "k_a": 1.0 + nrm(ks[12], (RWKV_W,), 0.05),
        "r_k": nrm(ks[13], (N_RWKV_HEADS, HEAD_N), 0.1),
        "ln_x_w": 1.0 + nrm(ks[14], (RWKV_W,), 0.05),
        "ln_x_b": nrm(ks[15], (RWKV_W,), 0.02),
        "w_out": nrm(ks[16], (CONV_W + RWKV_W, d), (CONV_W + RWKV_W) ** -0.5),
        "g_post_mix": 1.0 + nrm(ks[17], (d,), 0.05),
        "g_pre_ffn": 1.0 + nrm(ks[18], (d,), 0.05),
        "w_gate_up": nrm(ks[19], (d, 2 * D_FF), d ** -0.5),
        "conv_ffn": nrm(ks[20], (FFN_K, D_FF), FFN_K ** -0.5),
        "w_down": nrm(ks[21], (D_FF, d), D_FF ** -0.5),
        "g_post_ffn": 1.0 + nrm(ks[22], (d,), 0.05),
    }


def reference(x, meta_tokens, g_pre_mix, w_in, conv_w, mu_rwkv, w_decay_up, w0, a_up, a0,
              g_up, k_k, k_a, r_k, ln_x_w, ln_x_b, w_out, g_post_mix, g_pre_ffn,
              w_gate_up, conv_ffn, w_down, g_post_ffn):
    bsz = x.shape[0]
    meta = jnp.broadcast_to(meta_tokens.astype(x.dtype)[None], (bsz, N_META, x.shape[-1]))
    h = jnp.concatenate([meta, x], axis=1)

    for _ in range(DEPTH):
        n = rmsnorm(h, g_pre_mix)
        z = n @ w_in
        y_conv = short_conv_group(z[..., :CONV_COLS], conv_w)
        y_rwkv = rwkv7_group(z[..., CONV_COLS:], mu_rwkv, w_decay_up, w0, a_up, a0, g_up,
                             k_k, k_a, r_k, ln_x_w, ln_x_b)
        mix = jnp.concatenate([y_conv, y_rwkv], axis=-1) @ w_out
        h = h + rmsnorm(mix, g_post_mix)
        f = conv_glu(rmsnorm(h, g_pre_ffn), w_gate_up, conv_ffn, w_down)
        h = h + rmsnorm(f, g_post_ffn)

    return h[:, N_META:]
```

```python
import contextlib
import math
import numpy as np
import concourse.bass as bass
import concourse.mybir as mybir
from concourse.bass_utils import run_bass_kernel_spmd

F32 = mybir.dt.float32
BF16 = mybir.dt.bfloat16
ALU = mybir.AluOpType
AF = mybir.ActivationFunctionType
AX = mybir.AxisListType

D = 2048
KC = 16
TILE = 512
CH = 128
RW = 1024
DFF = 5632
NFB = DFF // 128
N_META = 16
RMS_EPS = 1e-6
GN_EPS = 64e-5
DECAY_C = math.exp(-0.5)
STC = 9

V_MU = 0
V_W0 = 28
V_A0 = 36
V_KK = 44
V_KA = 52
V_RK = 60
V_CW = 68
V_CF = 92
V_HK = 92 + 3 * NFB
NV = V_HK + 1
C_ID = 0
C_M4 = 128
C_ML = 640
C_BO = 1152
C_HS = 1280
NCONST = 1282


class Res:
    __slots__ = ("name", "w", "r", "dsem", "dcnt")

    def __init__(self, name):
        self.name = name
        self.w = None
        self.r = {}
        self.dsem = None
        self.dcnt = 0


class V:
    __slots__ = ("ap", "res")

    def __init__(self, ap, res):
        self.ap = ap
        self.res = res

    def __getitem__(self, idx):
        return V(self.ap[idx], self.res)

    def re(self, pat, **kw):
        return V(self.ap.rearrange(pat, **kw), self.res)

    def bitcast(self, dt):
        return V(self.ap.bitcast(dt), self.res)

    def bc(self, shape, axis):
        return V(self.ap.unsqueeze(axis).to_broadcast(list(shape)), self.res)


class Sched:
    ENG = ("pe", "act", "dve", "pool", "sp")

    def __init__(self, nc, stack):
        self.nc = nc
        self.stack = stack
        self.ops = {e: [] for e in self.ENG}
        self.sem = {e: stack.enter_context(nc.semaphore("s_" + e)) for e in self.ENG}
        self.cnt = {e: 0 for e in self.ENG}
        self.seen = {e: {} for e in self.ENG}
        self.allres = []

    def res(self, name):
        r = Res(name)
        self.allres.append(r)
        return r

    def _dsem(self, res):
        if res.dsem is None:
            res.dsem = self.stack.enter_context(self.nc.semaphore("d_" + res.name))
        return res.dsem

    def _deps(self, eng, reads, writes, seen_key=None):
        deps = {}

        def add(t, kind):
            if t is None:
                return
            key, val, en = t
            if en == eng and (kind != "raw" or eng == "pe"):
                return
            if deps.get(key, 0) < val:
                deps[key] = val

        for r in reads:
            add(r.w, "raw")
        for w in writes:
            add(w.w, "waw")
            for key, (val, en) in w.r.items():
                add((key, val, en), "war")
        out = []
        seen = self.seen[seen_key or eng]
        for key, val in deps.items():
            if seen.get(key, 0) >= val:
                continue
            seen[key] = val
            out.append((key, val))
        return out

    def op(self, eng, fn, reads=(), writes=()):
        if _HALT[0]:
            return
        waits = self._deps(eng, reads, writes)
        self.cnt[eng] += 1
        c = self.cnt[eng]
        key = ("c", eng)
        self.ops[eng].append((waits, fn, ("c", c)))
        for r in reads:
            r.r[key] = (c, eng)
        for w in writes:
            w.w = (key, c, eng)
            w.r = {}

    def dma(self, q, fn, reads=(), writes=()):
        if _HALT[0]:
            return
        waits = self._deps("dma", reads, writes, seen_key=q)
        anchor = writes[0] if writes else reads[0]
        sem = self._dsem(anchor)
        anchor.dcnt += 16
        c = anchor.dcnt
        key = ("d", sem)
        self.ops[q].append((waits, fn, ("d", sem)))
        for r in reads:
            r.r[key] = (c, "dma")
        for w in writes:
            w.w = (key, c, "dma")
            w.r = {}

    def barrier(self):
        for e in self.ENG:
            waits = self._deps(e + "_bar", self.allres, self.allres, seen_key=e)
            if waits:
                self.ops[e].append((waits, None, None))

    def emit(self):
        nc = self.nc
        ops = self.ops
        needed = {e: set() for e in self.ENG}
        for e in self.ENG:
            for waits, fn, inc in ops[e]:
                for key, val in waits:
                    if key[0] == "c":
                        needed[key[1]].add(val)
        rank = {e: {v: i + 1 for i, v in enumerate(sorted(needed[e]))} for e in self.ENG}
        sem = self.sem

        def run(engh, lst):
            for waits, fn, inc in lst:
                for key, val in waits:
                    if key[0] == "c":
                        engh.wait_ge(sem[key[1]], rank[key[1]][val])
                    else:
                        engh.wait_ge(key[1], val)
                if fn is not None:
                    ins = fn(engh)
                    if inc[0] == "d":
                        ins.then_inc(inc[1], 16)
                    elif inc[1] in rank[self._cur]:
                        ins.then_inc(sem[self._cur], 1)

        with nc.Block() as block:
            @block.tensor
            def _(e):
                self._cur = "pe"
                run(e, ops["pe"])

            @block.scalar
            def _(e):
                self._cur = "act"
                run(e, ops["act"])

            @block.vector
            def _(e):
                self._cur = "dve"
                run(e, ops["dve"])

            @block.gpsimd
            def _(e):
                self._cur = "pool"
                run(e, ops["pool"])

            @block.sync
            def _(e):
                self._cur = "sp"
                run(e, ops["sp"])


def _rw(reads, writes):
    rs, ws = [], []
    for v in reads:
        if isinstance(v, V) and v.res not in rs:
            rs.append(v.res)
    for v in writes:
        if v.res not in ws:
            ws.append(v.res)
    return rs, ws


def _a(x):
    return x.ap if isinstance(x, V) else x


class KB:
    def __init__(self, nc, S):
        self.nc = nc
        self.S = S

    def mm(self, out, lhsT, rhs, start=True, stop=True):
        rs, ws = _rw([lhsT, rhs], [out])
        self.S.op("pe", lambda e: e.matmul(out.ap, lhsT=lhsT.ap, rhs=rhs.ap, start=start, stop=stop), rs, ws)

    def tr(self, out, in_, ident):
        rs, ws = _rw([in_, ident], [out])
        self.S.op("pe", lambda e: e.transpose(out=out.ap, in_=in_.ap, identity=ident.ap), rs, ws)

    def act(self, out, in_, func, bias=None, scale=None, accum=None):
        rs, ws = _rw([in_, bias, scale], [out] + ([accum] if accum is not None else []))
        kw = {}
        if bias is not None:
            kw["bias"] = _a(bias)
        if scale is not None:
            kw["scale"] = _a(scale)
        if accum is not None:
            kw["accum_out"] = accum.ap
        self.S.op("act", lambda e: e.activation(out=out.ap, in_=in_.ap, func=func, **kw), rs, ws)

    def copy(self, eng, out, in_):
        rs, ws = _rw([in_], [out])
        if eng == "act":
            self.S.op("act", lambda e: e.activation(out=out.ap, in_=in_.ap, func=AF.Copy), rs, ws)
        else:
            self.S.op(eng, lambda e: e.tensor_copy(out=out.ap, in_=in_.ap), rs, ws)

    def tt(self, eng, out, a, b, op):
        rs, ws = _rw([a, b], [out])
        self.S.op(eng, lambda e: e.tensor_tensor(out=out.ap, in0=a.ap, in1=b.ap, op=op), rs, ws)

    def ts(self, eng, out, a, s1, s2=None, op0=ALU.mult, op1=None):
        rs, ws = _rw([a, s1, s2], [out])
        if op1 is None:
            self.S.op(eng, lambda e: e.tensor_scalar(out=out.ap, in0=a.ap, scalar1=_a(s1), scalar2=0.0, op0=op0, op1=ALU.add), rs, ws)
        else:
            self.S.op(eng, lambda e: e.tensor_scalar(out=out.ap, in0=a.ap, scalar1=_a(s1), scalar2=_a(s2), op0=op0, op1=op1), rs, ws)

    def stt(self, eng, out, a, scalar, b, op0, op1):
        rs, ws = _rw([a, scalar, b], [out])
        self.S.op(eng, lambda e: e.scalar_tensor_tensor(out=out.ap, in0=a.ap, scalar=_a(scalar), in1=b.ap, op0=op0, op1=op1), rs, ws)

    def red(self, eng, out, in_, op=ALU.add):
        rs, ws = _rw([in_], [out])
        self.S.op(eng, lambda e: e.tensor_reduce(out=out.ap, in_=in_.ap, axis=AX.X, op=op), rs, ws)

    def scan(self, out, d0, d1):
        rs, ws = _rw([d0, d1], [out])
        self.S.op("dve", lambda e: e.tensor_tensor_scan(out=out.ap, data0=d0.ap, data1=d1.ap, initial=0.0,
                                                        op0=ALU.mult, op1=ALU.add), rs, ws)

    def recip(self, out, in_):
        rs, ws = _rw([in_], [out])
        self.S.op("dve", lambda e: e.reciprocal(out=out.ap, in_=in_.ap), rs, ws)

    def memset(self, eng, out, val):
        rs, ws = _rw([], [out])
        self.S.op(eng, lambda e: e.memset(out.ap, val), rs, ws)

    def dma(self, q, out, in_):
        rs, ws = _rw([in_], [out])
        self.S.dma(q, lambda e: e.dma_start(out=out.ap, in_=in_.ap), rs, ws)


import os
_STOP = int(os.environ.get("KSTOP", "99"))


class _Stop(Exception):
    pass


_HALT = [False]


def _ck(k):
    if _STOP == k:
        _HALT[0] = True


def build_nc(NT, RANGE):
    SEQ = NT * TILE
    OWN0 = SEQ - RANGE - 2
    OC0 = OWN0 // CH
    NCHK = SEQ // CH
    NOC = NCHK - OC0
    HALO = OWN0 - OC0 * CH + 2
    NTOK2 = NOC * CH
    OT0 = OC0 // 4

    nc = bass.Bass("TRN2", target_bir_lowering=False)
    _HALT[0] = False
    dr = {}

    def din(name, shape):
        dr[name] = nc.dram_tensor(name, list(shape), F32, kind="ExternalInput").ap()
        return dr[name]

    xs = din("xs", [SEQ, D])
    win = din("win", [52, 128, KC * 128])
    wgu = din("wgu", [NFB * 2, 128, KC * 128])
    wdn = din("wdn", [NFB, 128, D])
    wout = din("wout", [128, KC * D])
    lora = din("lora", [128, 4 * RW])
    vecs_d = din("vecs", [128, NV])
    const_d = din("consts", [128, NCONST])
    bcv = {n: din(n, [1, w]) for n, w in (("g_pre_mix", D), ("g_post_mix", D), ("g_pre_ffn", D),
                                          ("g_post_ffn", D), ("ln_x_w", RW), ("ln_x_b", RW))}
    out_d = nc.dram_tensor("out", [NTOK2, D], F32, kind="ExternalOutput").ap()
    yT_d = nc.dram_tensor("yT_s", [KC, 128, NTOK2], BF16, kind="Internal").ap()
    n2T_d = nc.dram_tensor("n2T_s", [KC, 128, NTOK2], BF16, kind="Internal").ap()
    h_d = nc.dram_tensor("h_s", [NTOK2, D], F32, kind="Internal").ap()

    with contextlib.ExitStack() as top:
        S = Sched(nc, top)
        K = KB(nc, S)
        R = S.res
        xs_v = V(xs, R("xs"))
        win_v = V(win, R("win"))
        wgu_v = V(wgu, R("wgu"))
        wdn_v = V(wdn, R("wdn"))
        wout_v = V(wout, R("wout"))
        lora_v = V(lora, R("lora"))
        out_v = V(out_d, R("out"))
        yT_v = V(yT_d, R("yT"))
        n2T_v = V(n2T_d, R("n2T"))
        h_v = V(h_d, R("h"))

        banks = []
        for i in range(8):
            t = top.enter_context(nc.psum_tensor("pb%d" % i, [128, 512], F32))
            banks.append(V(t[:, :], R("pb%d" % i)))
        bank_i = [0]

        def psum():
            b = banks[bank_i[0] % 8]
            bank_i[0] += 1
            return b

        def sbt(stack, name, shape, dt):
            t = stack.enter_context(nc.sbuf_tensor("sb_" + name, list(shape), dt))
            return V(t[tuple(slice(None) for _ in shape)], R(name))

        vecs = sbt(top, "vecs", [128, NV], F32)
        vder = sbt(top, "vder", [128, 28 + 8], F32)
        cf = sbt(top, "cf", [128, NCONST], F32)
        cb = sbt(top, "cb", [128, NCONST], BF16)
        ones = sbt(top, "ones", [128, 128], F32)
        K.dma("sp", vecs, V(vecs_d, R("vecs_d")))
        K.dma("sp", cf, V(const_d, R("const_d")))
        K.dma("pool", cb, V(const_d, R("const_d2")))
        K.memset("dve", ones, 1.0)
        K.ts("dve", vder[:, 0:28], vecs[:, V_MU:V_MU + 28], -1.0, 1.0, ALU.mult, ALU.add)
        K.ts("dve", vder[:, 28:36], vecs[:, V_KA:V_KA + 8], -1.0, 1.0, ALU.mult, ALU.add)
        ident_b = cb[:, C_ID:C_ID + 128]
        mask4 = cb[:, C_M4:C_M4 + 512]
        maskL4 = cb[:, C_ML:C_ML + 512]
        blockones = cf[:, C_BO:C_BO + 128]
        headsel_b = cb[:, C_HS:C_HS + 2]
        hk = vecs[:, V_HK:V_HK + 1]

        def rms_rstd(src, rstd, junk, nbanks=None):
            pass

        try:
          _ck(1)
          with contextlib.ExitStack() as p1:
              gpre = sbt(p1, "gpre", [128, D], F32)
              lnw = sbt(p1, "lnw", [128, RW], F32)
              lnb = sbt(p1, "lnb", [128, RW], F32)
              K.dma("sp", gpre, V(bcv["g_pre_mix"].partition_broadcast(128), R("bc0")))
              K.dma("sp", lnw, V(bcv["ln_x_w"].partition_broadcast(128), R("bc1")))
              K.dma("sp", lnb, V(bcv["ln_x_b"].partition_broadcast(128), R("bc2")))
              lw_b = sbt(p1, "lw_b", [128, 4 * RW], BF16)
              K.dma("pool", lw_b, lora_v)
              wring = [sbt(p1, "wr%d" % i, [128, KC, 128], BF16) for i in range(4)]
              wr_i = [0]
              xin = sbt(p1, "xin", [128, D], F32)
              xn = sbt(p1, "xn", [128, D], BF16)
              st4 = sbt(p1, "st4", [128, 8], F32)
              nT = sbt(p1, "nT", [128, KC, TILE], BF16)
              carry = sbt(p1, "carry", [128, 28], F32)
              K.memset("dve", carry, 0.0)
              ccar = sbt(p1, "ccar", [128, 8, 2], F32)
              K.memset("dve", ccar, 0.0)
              tw = sbt(p1, "tw", [128, TILE], BF16)
              adb = sbt(p1, "adb", [128, TILE], BF16)
              sg2 = [sbt(p1, "sg%d" % i, [128, 2, TILE], BF16) for i in range(2)]
              ztmp = sbt(p1, "ztmp", [128, TILE], F32)
              zraw = sbt(p1, "zraw", [128, TILE], F32)
              zmu = sbt(p1, "zmu", [128, TILE + 1], F32)
              zr = sbt(p1, "zr", [128, TILE], F32)
              zk = sbt(p1, "zk", [128, TILE], F32)
              zv = sbt(p1, "zv", [128, TILE], F32)
              fA = [sbt(p1, "fA%d" % i, [128, TILE], F32) for i in range(8)]
              AR2 = [sbt(p1, "AR%d" % i, [128, 2, 4, 256], BF16) for i in range(2)]
              BK2 = [sbt(p1, "BK%d" % i, [128, 2, 4, 256], BF16) for i in range(2)]
              TF2 = [sbt(p1, "TF%d" % i, [128, 2, 3, TILE], BF16) for i in range(2)]
              rkb2 = [sbt(p1, "rkb%d" % i, [128, 2, TILE], BF16) for i in range(2)]
              PC2 = [sbt(p1, "PCd%d" % i, [128, 2, 4], F32) for i in range(2)]
              TOK = [sbt(p1, "TOK%d" % i, [128, 2, 4, 128], BF16) for i in range(2)]
              SA = [sbt(p1, "SA%d" % i, [128, 4, 512], BF16) for i in range(2)]
              SN = [[sbt(p1, "SN%d_%d" % (i, j), [128, 4, 128], BF16) for j in range(2)] for i in range(2)]
              SL = [[sbt(p1, "SL%d_%d" % (i, j), [128, 4, 128], BF16) for j in range(2)] for i in range(2)]
              WW = [[sbt(p1, "WW%d_%d" % (i, j), [128, 4, 128], BF16) for j in range(2)] for i in range(2)]
              GT = [sbt(p1, "GT%d" % i, [128, 2, 64], BF16) for i in range(2)]
              RH = [sbt(p1, "RH%d" % i, [128, 2, 128], BF16) for i in range(2)]
              Tst = sbt(p1, "Tst", [128, 8, 64], F32)
              Tbf = [sbt(p1, "Tbf%d" % i, [128, 8, 64], BF16) for i in range(2)]
              K.memset("dve", Tst, 0.0)
              K.memset("dve", Tbf[0], 0.0)
              K.memset("dve", Tbf[1], 0.0)
              tb_i = [[0] * 8]
              ysb = sbt(p1, "ysb", [128, 256], F32)
              ysq = sbt(p1, "ysq", [128, 256], F32)
              yn = sbt(p1, "yn", [128, 256], F32)
              bon = sbt(p1, "bon", [128, 256], F32)
              gst = sbt(p1, "gst", [128, 24], F32)
              ytb = sbt(p1, "ytb", [128, 256], BF16)
              yTs = sbt(p1, "yTs", [128, 2, TILE], BF16)
              csb = sbt(p1, "csb", [128, TILE], F32)
              chb = sbt(p1, "chb", [128, TILE + 2], F32)
              ycv = sbt(p1, "ycv", [128, TILE], BF16)

              def load_w(cbi):
                  w = wring[wr_i[0] % 4]
                  wr_i[0] += 1
                  K.dma("pool", w.re("p k j -> p (k j)"), win_v[cbi])
                  return w

              def project(w, t_cols=TILE):
                  pz = psum()
                  for kc in range(KC):
                      K.mm(pz[:, 0:t_cols], w[:, kc, :], nT[:, kc, 0:t_cols], start=(kc == 0), stop=(kc == KC - 1))
                  return pz

              def shift(pz, col, out):
                  mu = vecs[:, V_MU + col:V_MU + col + 1]
                  omu = vder[:, col:col + 1]
                  K.copy("act", zraw, pz)
                  K.ts("pool", ztmp, zraw, omu)
                  K.copy("pool", zmu[:, 0:1], carry[:, col:col + 1])
                  K.ts("pool", zmu[:, 1:TILE + 1], zraw, mu)
                  K.tt("pool", out, ztmp, zmu[:, 0:TILE], ALU.add)
                  K.copy("pool", carry[:, col:col + 1], zmu[:, TILE:TILE + 1])

              def front(ti):
                  t0 = ti * TILE
                  sg = sg2[ti % 2]
                  for j in range(4):
                      K.dma("sp", xin, xs_v[t0 + j * CH:t0 + (j + 1) * CH, :])
                      K.memset("dve", st4[:, 0:1], 0.0)
                      K.act(xn, xin, AF.Square, accum=st4[:, 0:1])
                      K.ts("dve", st4[:, 1:2], st4[:, 0:1], 1.0 / D, RMS_EPS, ALU.mult, ALU.add)
                      K.act(st4[:, 2:3], st4[:, 1:2], AF.Sqrt)
                      K.recip(st4[:, 3:4], st4[:, 2:3])
                      K.ts("pool", xin, xin, st4[:, 3:4])
                      K.tt("pool", xn, xin, gpre, ALU.mult)
                      yield
                      for a in range(2):
                          pb = psum()
                          pbb = pb.bitcast(BF16)
                          for k8 in range(8):
                              kc = a * 8 + k8
                              K.tr(pbb[:, k8 * 128:(k8 + 1) * 128], xn[:, kc * 128:(kc + 1) * 128], ident_b)
                          K.copy("act" if a == 0 else "dve", nT[:, a * 8:(a + 1) * 8, j * CH:(j + 1) * CH],
                                 pbb.re("p (k t) -> p k t", k=8))
                          yield
                  pz = project(load_w(0))
                  shift(pz, 0, zr)
                  K.act(tw, zr, AF.Tanh)
                  yield
                  pz = project(load_w(1))
                  shift(pz, 1, zr)
                  K.copy("act", adb, zr)
                  yield
                  for gi in range(2):
                      pz = project(load_w(2 + gi))
                      shift(pz, 2 + gi, zr)
                      K.act(sg[:, gi, :], zr, AF.Sigmoid)
                      yield

              def prep(ti, hg):
                  out_tile = ti >= OT0
                  par = (ti * 4 + hg) % 2
                  AR, BK, TF, rkb, PC = AR2[par], BK2[par], TF2[par], rkb2[par], PC2[par]
                  for s in range(2):
                      hp = hg * 2 + s
                      for qi, dst in enumerate((zr, zk, zv)):
                          pz = project(load_w(4 + hp * 3 + qi))
                          shift(pz, 4 + hp * 3 + qi, dst)
                          yield
                      plw = psum()
                      K.mm(plw, lw_b[:, hp * 128:(hp + 1) * 128], tw)
                      pa = psum()
                      K.mm(pa, lw_b[:, RW + hp * 128:RW + (hp + 1) * 128], adb)
                      sw, aa, cs, kk, t1, t2, t3, t4 = fA
                      K.act(sw, plw, AF.Sigmoid, bias=vecs[:, V_W0 + hp:V_W0 + hp + 1])
                      K.act(aa, pa, AF.Sigmoid, bias=vecs[:, V_A0 + hp:V_A0 + hp + 1])
                      yield
                      for c in range(4):
                          K.scan(cs[:, c * CH:(c + 1) * CH], ones, sw[:, c * CH:(c + 1) * CH])
                      K.ts("pool", kk, zk, vecs[:, V_KK + hp:V_KK + hp + 1])
                      K.tt("pool", t1, kk, kk, ALU.mult)
                      yield
                      pss = psum()
                      K.mm(pss, blockones, t1)
                      K.act(t1, pss, AF.Sqrt)
                      K.ts("dve", t1, t1, 1e-12, None, ALU.max)
                      K.recip(t2, t1)
                      K.tt("pool", kk, kk, t2, ALU.mult)
                      yield
                      K.ts("pool", t1, aa, vecs[:, V_KA + hp:V_KA + hp + 1], vder[:, 28 + hp:29 + hp], ALU.mult, ALU.add)
                      K.tt("pool", t1, zk, t1, ALU.mult)
                      K.tt("pool", aa, kk, aa, ALU.mult)
                      yield
                      K.tt("pool", t2, cs, sw, ALU.subtract)
                      K.act(t2, t2, AF.Exp, scale=-DECAY_C)
                      K.stt("dve", AR[:, s, :, 0:128], kk.re("p (c t) -> p c t", c=4), -1.0,
                            t2.re("p (c t) -> p c t", c=4), ALU.mult, ALU.mult)
                      yield
                      K.act(t3, cs, AF.Exp, scale=-DECAY_C)
                      K.tt("pool", AR[:, s, :, 128:256], zr.re("p (c t) -> p c t", c=4),
                           t3.re("p (c t) -> p c t", c=4), ALU.mult)
                      yield
                      K.act(t3, cs, AF.Exp, scale=DECAY_C)
                      K.tt("pool", BK[:, s, :, 0:128], aa.re("p (c t) -> p c t", c=4),
                           t3.re("p (c t) -> p c t", c=4), ALU.mult)
                      K.tt("pool", BK[:, s, :, 128:256], t1.re("p (c t) -> p c t", c=4),
                           t3.re("p (c t) -> p c t", c=4), ALU.mult)
                      yield
                      K.ts("dve", st4[:, 4:8], cs.re("p (c t) -> p c t", c=4)[:, :, CH - 1], -DECAY_C)
                      K.act(PC[:, s, :], st4[:, 4:8], AF.Exp)
                      for c in range(4):
                          K.act(t4[:, c * CH:(c + 1) * CH], cs[:, c * CH:(c + 1) * CH], AF.Exp,
                                scale=DECAY_C, bias=st4[:, 4 + c:5 + c])
                      yield
                      K.tt("pool", TF[:, s, 1, :], aa, t4, ALU.mult)
                      K.tt("pool", TF[:, s, 2, :], t1, t4, ALU.mult)
                      K.copy("act", TF[:, s, 0, :], zv)
                      if out_tile:
                          K.tt("pool", t2, zr, t1, ALU.mult)
                          K.ts("pool", rkb[:, s, :], t2, vecs[:, V_RK + hp:V_RK + hp + 1])
                      yield

              def scan(ti, hg):
                  out_tile = ti >= OT0
                  par = (ti * 4 + hg) % 2
                  AR, BK, TF, rkb, PC = AR2[par], BK2[par], TF2[par], rkb2[par], PC2[par]
                  sg = sg2[ti % 2]
                  for cp in range(2):
                      cl = [2 * cp, 2 * cp + 1]
                      for u, c in enumerate(cl):
                          pb = psum()
                          pbb = pb.bitcast(BF16).re("p (s q t) -> p s q t", s=2, q=4)
                          for s in range(2):
                              for q in range(3):
                                  K.tr(pbb[:, s, q, :], TF[:, s, q, c * CH:(c + 1) * CH], ident_b)
                              K.tr(pbb[:, s, 3, :], AR[:, s, c, 0:128], ident_b)
                          K.copy("act", TOK[u], pbb)
                          yield
                      for u, c in enumerate(cl):
                          for hl in range(4):
                              s, e = hl // 2, hl % 2
                              ps_ = slice(64 * e, 64 * e + 64)
                              pa_ = psum()
                              K.mm(pa_[:, 0:256], BK[ps_, s, c, 0:128], AR[ps_, s, c, :])
                              K.mm(pa_[:, 256:512], BK[ps_, s, c, 128:256], AR[ps_, s, c, :])
                              K.tt("dve", SA[u][:, hl, :], pa_, mask4, ALU.mult)
                          yield
                          ple = [psum(), psum()]
                          for hl in range(4):
                              s, e = hl // 2, hl % 2
                              ps_ = slice(64 * e, 64 * e + 64)
                              K.mm(ple[e][:, s * 128:(s + 1) * 128], AR[ps_, s, c, 0:128], BK[ps_, s, c, 0:128])
                          for e in range(2):
                              K.tt("dve", SL[u][0].re("p (s e) t -> p s e t", e=2)[:, :, e, :],
                                   ple[e][:, 0:256].re("p (s t) -> p s t", s=2),
                                   maskL4[:, 0:256].re("p (s t) -> p s t", s=2), ALU.mult)
                          px = psum()
                          for hl in range(4):
                              s, e = hl // 2, hl % 2
                              K.mm(px[:, hl * 64:(hl + 1) * 64], SA[u][:, hl, 256:384], TOK[u][:, s, 0, 64 * e:64 * e + 64])
                          K.copy("act", WW[u][0][:, :, 64:128], px[:, 0:256].re("p (h v) -> p h v", h=4))
                          K.copy("act", WW[u][0].re("p (s e) k -> p s e k", s=2)[:, :, :, 0:64],
                                 TOK[u][:, :, 3, :].re("p s (e k) -> p s e k", e=2))
                          yield
                      for i in range(7):
                          for u, c in enumerate(cl):
                              n_cur = SA[u][:, :, 0:128] if i == 0 else SN[u][i % 2]
                              l_cur = SL[u][i % 2]
                              w_cur = WW[u][i % 2]
                              pw = psum()
                              for hl in range(4):
                                  K.mm(pw[:, hl * 128:(hl + 1) * 128], n_cur[:, hl, :], w_cur[:, hl, :])
                              K.tt("dve", WW[u][(i + 1) % 2].re("p h t -> p (h t)"), pw,
                                   w_cur.re("p h t -> p (h t)"), ALU.add)
                              if i < 6:
                                  pn = psum()
                                  for hl in range(4):
                                      K.mm(pn[:, hl * 128:(hl + 1) * 128], l_cur[:, hl, :], n_cur[:, hl, :])
                                  K.copy("act", SN[u][(i + 1) % 2].re("p h t -> p (h t)"), pn)
                              if i < 5:
                                  pl_ = psum()
                                  for hl in range(4):
                                      K.mm(pl_[:, hl * 128:(hl + 1) * 128], n_cur[:, hl, :], l_cur[:, hl, :])
                                  K.copy("act", SL[u][(i + 1) % 2].re("p h t -> p (h t)"), pl_)
                              yield
                      for u, c in enumerate(cl):
                          Wf = WW[u][1]
                          chunk_g = ti * 4 + c
                          want_y = chunk_g >= OC0
                          pg = psum()
                          for hl in range(4):
                              s, e = hl // 2, hl % 2
                              K.mm(pg[64 * e:64 * e + 64, s * 64:(s + 1) * 64], Wf[:, hl, 0:64],
                                   TOK[u][:, s, 1, 64 * e:64 * e + 64])
                          K.copy("act", GT[u].re("p s k -> p (s k)"), pg[:, 0:128])
                          tcur = Tbf[tb_i[0][hg] % 2]
                          tnew = Tbf[(tb_i[0][hg] + 1) % 2]
                          if want_y:
                              pr = psum()
                              for hl in range(4):
                                  s, e = hl // 2, hl % 2
                                  K.mm(pr[64 * e:64 * e + 64, s * 128:(s + 1) * 128], Wf[:, hl, 0:64], SA[u][:, hl, 128:256])
                              K.tt("dve", RH[u], pr[:, 0:256].re("p (s t) -> p s t", s=2), AR[:, :, c, 128:256], ALU.add)
                              py = psum()
                              for hl in range(4):
                                  s, e = hl // 2, hl % 2
                                  ps_ = slice(64 * e, 64 * e + 64)
                                  o = py[:, hl * 64:(hl + 1) * 64]
                                  K.mm(o, SA[u][:, hl, 128:256], Wf[:, hl, 64:128], start=True, stop=False)
                                  K.mm(o, SA[u][:, hl, 384:512], TOK[u][:, s, 0, 64 * e:64 * e + 64], start=False, stop=False)
                                  K.mm(o, RH[u][ps_, s, :], tcur[ps_, hg * 2 + s, :], start=False, stop=True)
                          pt = psum()
                          for hl in range(4):
                              s, e = hl // 2, hl % 2
                              ps_ = slice(64 * e, 64 * e + 64)
                              o = pt[ps_, s * 64:(s + 1) * 64]
                              K.mm(o, TOK[u][:, s, 1, 64 * e:64 * e + 64], Wf[:, hl, 64:128], start=True, stop=False)
                              K.mm(o, TOK[u][:, s, 2, 64 * e:64 * e + 64], TOK[u][:, s, 0, 64 * e:64 * e + 64], start=False, stop=False)
                              K.mm(o, GT[u][ps_, s, :], tcur[ps_, hg * 2 + s, :], start=False, stop=True)
                          for s in range(2):
                              K.stt("dve", Tst[:, hg * 2 + s, :], Tst[:, hg * 2 + s, :], PC[:, s, c:c + 1],
                                    pt[:, s * 64:(s + 1) * 64], ALU.mult, ALU.add)
                          K.copy("act", tnew[:, hg * 2:hg * 2 + 2, :], Tst[:, hg * 2:hg * 2 + 2, :])
                          tb_i[0][hg] += 1
                          if want_y:
                              K.copy("act", ysb, py[:, 0:256])
                          yield
                          if not want_y:
                              continue
                          pgt = psum()
                          K.mm(pgt[:, 0:256], sg[:, 0, c * CH:(c + 1) * CH], lw_b[:, 2 * RW + hg * 256:2 * RW + (hg + 1) * 256],
                               start=True, stop=False)
                          K.mm(pgt[:, 0:256], sg[:, 1, c * CH:(c + 1) * CH], lw_b[:, 3 * RW + hg * 256:3 * RW + (hg + 1) * 256],
                               start=False, stop=True)
                          for s in range(2):
                              K.mm(pgt[:, 256 + 2 * s:258 + 2 * s], rkb[:, s, c * CH:(c + 1) * CH], headsel_b)
                          K.copy("act", gst[:, 16:20], pgt[:, 256:260])
                          y3 = ysb.re("p (h v) -> p h v", h=4)
                          K.red("dve", gst[:, 0:4], y3)
                          K.tt("dve", ysq, ysb, ysb, ALU.mult)
                          K.red("dve", gst[:, 4:8], ysq.re("p (h v) -> p h v", h=4))
                          K.ts("dve", gst[:, 0:4], gst[:, 0:4], 1.0 / 64)
                          K.tt("dve", gst[:, 8:12], gst[:, 0:4], gst[:, 0:4], ALU.mult)
                          K.stt("dve", gst[:, 4:8], gst[:, 4:8], 1.0 / 64, gst[:, 8:12], ALU.mult, ALU.subtract)
                          K.ts("dve", gst[:, 4:8], gst[:, 4:8], GN_EPS, None, ALU.add)
                          K.act(gst[:, 8:12], gst[:, 4:8], AF.Sqrt)
                          K.recip(gst[:, 12:16], gst[:, 8:12])
                          yn3 = yn.re("p (h v) -> p h v", h=4)
                          K.tt("dve", yn3, y3, gst[:, 0:4].bc([128, 4, 64], 2), ALU.subtract)
                          K.tt("dve", yn3, yn3, gst[:, 12:16].bc([128, 4, 64], 2), ALU.mult)
                          K.tt("dve", yn, yn, lnw[:, hg * 256:(hg + 1) * 256], ALU.mult)
                          K.tt("dve", yn, yn, lnb[:, hg * 256:(hg + 1) * 256], ALU.add)
                          K.tt("dve", bon.re("p (s e v) -> p s e v", s=2, e=2),
                               TOK[u][:, :, 0, :].re("p s (e v) -> p s e v", e=2),
                               gst[:, 16:20].re("p (s e) -> p s e", s=2).bc([128, 2, 2, 64], 3), ALU.mult)
                          K.tt("dve", yn, yn, bon, ALU.add)
                          K.tt("dve", ytb, yn, pgt[:, 0:256], ALU.mult)
                          pb = psum()
                          pbb = pb.bitcast(BF16)
                          for s in range(2):
                              K.tr(pbb[:, s * 128:(s + 1) * 128], ytb[:, s * 128:(s + 1) * 128], ident_b)
                          K.copy("act", yTs[:, :, c * CH:(c + 1) * CH], pbb[:, 0:256].re("p (s t) -> p s t", s=2))
                          yield
                  if out_tile:
                      c_lo = max(0, OC0 - ti * 4)
                      tk0 = (ti * 4 + c_lo - OC0) * CH
                      ncol = (4 - c_lo) * CH
                      for s in range(2):
                          K.dma("sp", yT_v[8 + hg * 2 + s, :, tk0:tk0 + ncol], yTs[:, s, c_lo * CH:4 * CH])
                  yield

              def conv(ti):
                  c_lo = max(0, OC0 - ti * 4)
                  tk0 = (ti * 4 + c_lo - OC0) * CH
                  ncol = (4 - c_lo) * CH
                  for i in range(8):
                      pB = project(load_w(28 + i * 3 + 0))
                      pC = project(load_w(28 + i * 3 + 1))
                      pH = project(load_w(28 + i * 3 + 2))
                      K.copy("act", csb, pC)
                      K.copy("act", chb[:, 0:2], ccar[:, i, :])
                      K.tt("dve", chb[:, 2:TILE + 2], csb, pH, ALU.mult)
                      K.copy("act", ccar[:, i, :], chb[:, TILE:TILE + 2])
                      K.ts("dve", csb, chb[:, 0:TILE], vecs[:, V_CW + i:V_CW + i + 1])
                      K.stt("dve", csb, chb[:, 1:TILE + 1], vecs[:, V_CW + 8 + i:V_CW + 9 + i], csb, ALU.mult, ALU.add)
                      K.stt("dve", csb, chb[:, 2:TILE + 2], vecs[:, V_CW + 16 + i:V_CW + 17 + i], csb, ALU.mult, ALU.add)
                      K.tt("dve", ycv, csb, pB, ALU.mult)
                      K.dma("sp", yT_v[i, :, tk0:tk0 + ncol], ycv[:, c_lo * CH:4 * CH])
                      yield

              def chain(*gens):
                  for g in gens:
                      yield from g

              def interleave(ga, gb):
                  la, lb = True, True
                  while la or lb:
                      if la:
                          try:
                              next(ga)
                          except StopIteration:
                              la = False
                      if lb:
                          try:
                              next(gb)
                          except StopIteration:
                              lb = False

              PIPE = os.environ.get("KPIPE", "1") == "1"
              units = [(ti, hg) for ti in range(NT) for hg in range(4)]

              def p_stream(ti, hg):
                  gens = []
                  if hg == 0:
                      gens.append(front(ti))
                  gens.append(prep(ti, hg))
                  return gens

              for g in chain(*p_stream(0, 0)):
                  pass
              for ui, (ti, hg) in enumerate(units):
                  nxt = []
                  if hg == 3 and ti >= OT0:
                      nxt.append(conv(ti))
                  if ui + 1 < len(units):
                      nxt += p_stream(*units[ui + 1])
                  if PIPE:
                      interleave(scan(ti, hg), chain(*nxt))
                  else:
                      for g in scan(ti, hg):
                          pass
                      for g in chain(*nxt):
                          pass
          _ck(8)
          S.barrier()

          with contextlib.ExitStack() as p15:
              wo_b = sbt(p15, "wo_b", [128, KC, D], BF16)
              K.dma("pool", wo_b.re("p k d -> p (k d)"), wout_v)
              gpm = sbt(p15, "gpm", [128, D], F32)
              gpf = sbt(p15, "gpf", [128, D], F32)
              K.dma("sp", gpm, V(bcv["g_post_mix"].partition_broadcast(128), R("bc3")))
              K.dma("sp", gpf, V(bcv["g_pre_ffn"].partition_broadcast(128), R("bc4")))
              ych = [sbt(p15, "ych%d" % i, [128, KC, CH], BF16) for i in range(2)]
              xin2 = [sbt(p15, "xin2_%d" % i, [128, D], F32) for i in range(2)]
              hsb = [sbt(p15, "hsb%d" % i, [128, D], F32) for i in range(2)]
              n2 = sbt(p15, "n2", [128, D], BF16)
              n2T = [sbt(p15, "n2T%d" % i, [128, KC, CH], BF16) for i in range(2)]
              st5 = sbt(p15, "st5", [128, 16], F32)
              for oc in range(NOC):
                  yc = ych[oc % 2]
                  xi = xin2[oc % 2]
                  hs = hsb[oc % 2]
                  nt2 = n2T[oc % 2]
                  K.dma("sp", yc, yT_v[:, :, oc * CH:(oc + 1) * CH].re("c p t -> p c t"))
                  K.dma("sp", xi, xs_v[(OC0 + oc) * CH:(OC0 + oc + 1) * CH, :])
                  pm = [psum() for _ in range(4)]
                  for blk in range(4):
                      for chc in range(KC):
                          K.mm(pm[blk], yc[:, chc, :], wo_b[:, chc, blk * 512:(blk + 1) * 512],
                               start=(chc == 0), stop=(chc == KC - 1))
                  K.memset("dve", st5[:, 0:4], 0.0)
                  for blk in range(4):
                      K.act(n2[:, blk * 512:(blk + 1) * 512], pm[blk], AF.Square, accum=st5[:, blk:blk + 1])
                  K.red("dve", st5[:, 4:5], st5[:, 0:4])
                  K.ts("dve", st5[:, 4:5], st5[:, 4:5], 1.0 / D, RMS_EPS, ALU.mult, ALU.add)
                  K.act(st5[:, 5:6], st5[:, 4:5], AF.Sqrt)
                  K.recip(st5[:, 6:7], st5[:, 5:6])
                  for blk in range(4):
                      sl = slice(blk * 512, (blk + 1) * 512)
                      K.stt("dve", hs[:, sl], pm[blk], st5[:, 6:7], gpm[:, sl], ALU.mult, ALU.mult)
                  K.tt("dve", hs, hs, xi, ALU.add)
                  K.dma("sp", h_v[oc * CH:(oc + 1) * CH, :], hs)
                  K.memset("dve", st5[:, 8:9], 0.0)
                  K.act(n2, hs, AF.Square, accum=st5[:, 8:9])
                  K.ts("dve", st5[:, 9:10], st5[:, 8:9], 1.0 / D, RMS_EPS, ALU.mult, ALU.add)
                  K.act(st5[:, 10:11], st5[:, 9:10], AF.Sqrt)
                  K.recip(st5[:, 11:12], st5[:, 10:11])
                  K.stt("dve", n2, hs, st5[:, 11:12], gpf, ALU.mult, ALU.mult)
                  for a in range(2):
                      pb = psum()
                      pbb = pb.bitcast(BF16)
                      for k8 in range(8):
                          kc = a * 8 + k8
                          K.tr(pbb[:, k8 * 128:(k8 + 1) * 128], n2[:, kc * 128:(kc + 1) * 128], ident_b)
                      K.copy("act", nt2[:, a * 8:(a + 1) * 8, :], pbb.re("p (k t) -> p k t", k=8))
                  if oc == 0:
                      K.ts("dve", nt2[:, :, 0:HALO], nt2[:, :, 0:HALO], hk)
                  K.dma("sp", n2T_v[:, :, oc * CH:(oc + 1) * CH].re("c p t -> p c t"), nt2)
          S.barrier()

          with contextlib.ExitStack() as p2:
              gpo = sbt(p2, "gpo", [128, D], F32)
              K.dma("sp", gpo, V(bcv["g_post_ffn"].partition_broadcast(128), R("bc5")))
              facc = sbt(p2, "facc", [128, STC, D], F32)
              n2s = sbt(p2, "n2s", [128, KC, STC * CH], BF16)
              wg_r = [sbt(p2, "wg%d" % i, [128, 2, KC, 128], BF16) for i in range(2)]
              wd_r = [sbt(p2, "wd%d" % i, [128, D], BF16) for i in range(2)]
              gbuf = sbt(p2, "gbuf", [128, STC * CH + 2], F32)
              cvt = sbt(p2, "cvt", [128, TILE], F32)
              slu = sbt(p2, "slu", [128, TILE], F32)
              actT = [sbt(p2, "actT%d" % i, [128, STC * CH], BF16) for i in range(2)]
              hld = sbt(p2, "hld", [128, D], F32)
              osb = sbt(p2, "osb", [128, D], F32)
              st6 = sbt(p2, "st6", [128, 8], F32)
              K.memset("dve", gbuf[:, 0:2], 0.0)
              st_starts = []
              s0 = 0
              while True:
                  st_starts.append(s0)
                  if s0 + STC >= NOC:
                      break
                  s0 += STC - 1
              for sti, cs0 in enumerate(st_starts):
                  nch = min(STC, NOC - cs0)
                  ntk = nch * CH
                  first_out = 0 if sti == 0 else 1
                  K.dma("sp", n2s[:, :, 0:ntk], n2T_v[:, :, cs0 * CH:cs0 * CH + ntk].re("c p t -> p c t"))
                  ttiles = [(a, min(TILE, ntk - a)) for a in range(0, ntk, TILE)]
                  for fb in range(NFB):
                      wgb = wg_r[fb % 2]
                      wdb = wd_r[fb % 2]
                      at = actT[fb % 2]
                      K.dma("pool", wgb.re("p g k j -> p g (k j)"), wgu_v[2 * fb:2 * fb + 2].re("g p f -> p g f"))
                      K.dma("pool", wdb, wdn_v[fb])
                      for (a0, tn) in ttiles:
                          pgg = psum()
                          pu = psum()
                          for kc in range(KC):
                              K.mm(pgg[:, 0:tn], wgb[:, 0, kc, :], n2s[:, kc, a0:a0 + tn], start=(kc == 0), stop=(kc == KC - 1))
                          for kc in range(KC):
                              K.mm(pu[:, 0:tn], wgb[:, 1, kc, :], n2s[:, kc, a0:a0 + tn], start=(kc == 0), stop=(kc == KC - 1))
                          K.copy("act", gbuf[:, 2 + a0:2 + a0 + tn], pgg[:, 0:tn])
                          K.ts("dve", cvt[:, 0:tn], gbuf[:, a0:a0 + tn], vecs[:, V_CF + fb:V_CF + fb + 1])
                          K.stt("dve", cvt[:, 0:tn], gbuf[:, a0 + 1:a0 + 1 + tn], vecs[:, V_CF + NFB + fb:V_CF + NFB + fb + 1],
                                cvt[:, 0:tn], ALU.mult, ALU.add)
                          K.stt("dve", cvt[:, 0:tn], gbuf[:, a0 + 2:a0 + 2 + tn], vecs[:, V_CF + 2 * NFB + fb:V_CF + 2 * NFB + fb + 1],
                                cvt[:, 0:tn], ALU.mult, ALU.add)
                          K.act(slu[:, 0:tn], cvt[:, 0:tn], AF.Silu)
                          K.tt("dve", at[:, a0:a0 + tn], slu[:, 0:tn], pu[:, 0:tn], ALU.mult)
                      for ci in range(nch):
                          for blk in range(4):
                              pf = psum()
                              K.mm(pf, at[:, ci * CH:(ci + 1) * CH], wdb[:, blk * 512:(blk + 1) * 512])
                              sl = slice(blk * 512, (blk + 1) * 512)
                              if fb == 0:
                                  K.copy("act", facc[:, ci, sl], pf)
                              else:
                                  K.tt("dve", facc[:, ci, sl], facc[:, ci, sl], pf, ALU.add)
                  for ci in range(first_out, nch):
                      oc = cs0 + ci
                      K.dma("sp", hld, h_v[oc * CH:(oc + 1) * CH, :])
                      K.memset("dve", st6[:, 0:1], 0.0)
                      K.act(osb, facc[:, ci, :], AF.Square, accum=st6[:, 0:1])
                      K.ts("dve", st6[:, 1:2], st6[:, 0:1], 1.0 / D, RMS_EPS, ALU.mult, ALU.add)
                      K.act(st6[:, 2:3], st6[:, 1:2], AF.Sqrt)
                      K.recip(st6[:, 3:4], st6[:, 2:3])
                      K.stt("dve", osb, facc[:, ci, :], st6[:, 3:4], gpo, ALU.mult, ALU.mult)
                      K.tt("dve", osb, osb, hld, ALU.add)
                      K.dma("sp", out_v[oc * CH:(oc + 1) * CH, :], osb)
          S.barrier()
        except _Stop:
            pass
        S.barrier()
        S.emit()
    return nc


def _blk(w):
    return np.ascontiguousarray(w.reshape(KC, 128, 128).transpose(1, 0, 2)).reshape(128, KC * 128)


def _pad_cols(w, n):
    out = np.zeros((w.shape[0], n), w.dtype)
    out[:, :w.shape[1]] = w
    return out


def _consts():
    c = np.zeros((128, NCONST), np.float32)
    j = np.arange(128)[:, None]
    t = np.arange(128)[None, :]
    c[:, C_ID:C_ID + 128] = np.eye(128)
    strict = (t > j).astype(np.float32)
    incl = (t >= j).astype(np.float32)
    c[:, C_M4:C_M4 + 512] = np.concatenate([strict, incl, strict, incl], 1)
    low = (j > t).astype(np.float32)
    c[:, C_ML:C_ML + 512] = np.concatenate([low] * 4, 1)
    c[:, C_BO:C_BO + 128] = (j // 64 == t // 64).astype(np.float32)
    c[:, C_HS:C_HS + 2] = (j // 64 == np.arange(2)[None, :]).astype(np.float32)
    return c


def prepare(inputs, NT, RANGE):
    f = lambda k: np.asarray(inputs[k], np.float32)
    w_in = f("w_in")
    RB = 3072
    cols = []
    cols.append(_pad_cols(w_in[:, RB + 3072:RB + 3168], 128))
    cols.append(_pad_cols(w_in[:, RB + 3168:RB + 3264], 128))
    cols.append(w_in[:, RB + 3264:RB + 3392])
    cols.append(w_in[:, RB + 3392:RB + 3520])
    for i in range(8):
        for base in (0, 1024, 2048):
            cols.append(w_in[:, RB + base + i * 128:RB + base + (i + 1) * 128])
    for i in range(8):
        for base in (0, 1024, 2048):
            cols.append(w_in[:, base + i * 128:base + (i + 1) * 128])
    win = np.stack([_blk(c) for c in cols], 0)
    wgu_full = f("w_gate_up")
    wgu = np.stack([_blk(wgu_full[:, g * DFF + fb * 128:g * DFF + (fb + 1) * 128]) for fb in range(NFB) for g in range(2)], 0)
    wdn = np.ascontiguousarray(f("w_down").reshape(NFB, 128, D))
    wout = np.ascontiguousarray(f("w_out").reshape(KC, 128, D).transpose(1, 0, 2)).reshape(128, KC * D)
    lora = np.zeros((128, 4 * RW), np.float32)
    lora[:96, 0:RW] = f("w_decay_up")
    lora[:96, RW:2 * RW] = f("a_up")
    gup = f("g_up")
    lora[:, 2 * RW:3 * RW] = gup[0:128]
    lora[:, 3 * RW:4 * RW] = gup[128:256]
    mu = f("mu_rwkv")
    vecs = np.zeros((128, NV), np.float32)
    vecs[:96, V_MU + 0] = mu[3072:3168]
    vecs[:96, V_MU + 1] = mu[3168:3264]
    vecs[:, V_MU + 2] = mu[3264:3392]
    vecs[:, V_MU + 3] = mu[3392:3520]
    for i in range(8):
        for qi, base in enumerate((0, 1024, 2048)):
            vecs[:, V_MU + 4 + i * 3 + qi] = mu[base + i * 128:base + (i + 1) * 128]
    for name, col in (("w0", V_W0), ("a0", V_A0), ("k_k", V_KK), ("k_a", V_KA)):
        vecs[:, col:col + 8] = f(name).reshape(8, 128).T
    vecs[:, V_RK:V_RK + 8] = f("r_k").reshape(8, 128).T
    cw = f("conv_w")
    for tap in range(3):
        vecs[:, V_CW + tap * 8:V_CW + tap * 8 + 8] = cw[tap].reshape(8, 128).T
    cfw = f("conv_ffn")
    for tap in range(3):
        vecs[:, V_CF + tap * NFB:V_CF + (tap + 1) * NFB] = cfw[tap].reshape(NFB, 128).T
    shared = {"win": win, "wgu": wgu, "wdn": wdn, "wout": wout, "lora": lora, "consts": _consts()}
    for n in ("g_pre_mix", "g_post_mix", "g_pre_ffn", "g_post_ffn", "ln_x_w", "ln_x_b"):
        shared[n] = f(n).reshape(1, -1)
    x = f("x")
    meta = f("meta_tokens")
    SEQ = NT * TILE
    in_maps = []
    for c in range(8):
        b, q = c // 4, c % 4
        seq = np.concatenate([meta, x[b]], 0)
        n_real = (q + 1) * RANGE
        xs = np.zeros((SEQ, D), np.float32)
        xs[SEQ - n_real:] = seq[:n_real]
        v = vecs.copy()
        v[:, V_HK] = 0.0 if q == 0 else 1.0
        m = dict(shared)
        m["xs"] = xs
        m["vecs"] = v
        in_maps.append(m)
    return in_maps


def kernel(**inputs):
    x = np.asarray(inputs["x"])
    B, L, _ = x.shape
    tot = N_META + L
    RANGE = tot // 4
    NT = -(-(tot + 2) // TILE)
    nc = build_nc(NT, RANGE)
    in_maps = prepare(inputs, NT, RANGE)
    dbg = os.environ.get("KCORES")
    if dbg is not None:
        sel = [int(t) for t in dbg.split(",")]
        res1 = run_bass_kernel_spmd(nc, [in_maps[c] for c in sel], core_ids=list(range(len(sel))))
        return {c: {k: np.asarray(v) for k, v in res1.results[i].items()} for i, c in enumerate(sel)}
    res = run_bass_kernel_spmd(nc, in_maps, core_ids=list(range(8)))
    SEQ = NT * TILE
    OC0 = (SEQ - RANGE - 2) // CH
    off = SEQ - RANGE - OC0 * CH
    full = np.zeros((B, tot, D), np.float32)
    for c in range(8):
        b, q = c // 4, c % 4
        o = np.asarray(res.results[c]["out"])
        full[b, q * RANGE:(q + 1) * RANGE] = o[off:off + RANGE]
    return np.ascontiguousarray(full[:, N_META:])
```

```python
import contextlib
import math
import numpy as np
import concourse.bass as bass
import concourse.mybir as mybir
from concourse.bass_utils import run_bass_kernel_spmd

F32 = mybir.dt.float32
BF16 = mybir.dt.bfloat16
ALU = mybir.AluOpType
AF = mybir.ActivationFunctionType
AX = mybir.AxisListType

D = 2048
KC = 16
TILE = 512
CH = 128
RW = 1024
DFF = 5632
NFB = DFF // 128
N_META = 16
RMS_EPS = 1e-6
GN_EPS = 64e-5
DECAY_C = math.exp(-0.5)
STC = 9

V_MU = 0
V_W0 = 28
V_A0 = 36
V_KK = 44
V_KA = 52
V_RK = 60
V_CW = 68
V_CF = 92
V_HK = 92 + 3 * NFB
NV = V_HK + 1
C_ID = 0
C_M4 = 128
C_ML = 640
C_BO = 1152
C_HS = 1280
NCONST = 1282


class Res:
    __slots__ = ("name", "w", "r", "dsem", "dcnt")

    def __init__(self, name):
        self.name = name
        self.w = None
        self.r = {}
        self.dsem = None
        self.dcnt = 0


class V:
    __slots__ = ("ap", "res")

    def __init__(self, ap, res):
        self.ap = ap
        self.res = res

    def __getitem__(self, idx):
        return V(self.ap[idx], self.res)

    def re(self, pat, **kw):
        return V(self.ap.rearrange(pat, **kw), self.res)

    def bitcast(self, dt):
        return V(self.ap.bitcast(dt), self.res)

    def bc(self, shape, axis):
        return V(self.ap.unsqueeze(axis).to_broadcast(list(shape)), self.res)


class Sched:
    ENG = ("pe", "act", "dve", "pool", "sp")

    def __init__(self, nc, stack):
        self.nc = nc
        self.stack = stack
        self.ops = {e: [] for e in self.ENG}
        self.sem = {e: stack.enter_context(nc.semaphore("s_" + e)) for e in self.ENG}
        self.cnt = {e: 0 for e in self.ENG}
        self.seen = {e: {} for e in self.ENG}
        self.allres = []

    def res(self, name):
        r = Res(name)
        self.allres.append(r)
        return r

    def _dsem(self, res):
        if res.dsem is None:
            res.dsem = self.stack.enter_context(self.nc.semaphore("d_" + res.name))
        return res.dsem

    def _deps(self, eng, reads, writes, seen_key=None):
        deps = {}

        def add(t, kind):
            if t is None:
                return
            key, val, en = t
            if en == eng and (kind != "raw" or eng == "pe"):
                return
            if deps.get(key, 0) < val:
                deps[key] = val

        for r in reads:
            add(r.w, "raw")
        for w in writes:
            add(w.w, "waw")
            for key, (val, en) in w.r.items():
                add((key, val, en), "war")
        out = []
        seen = self.seen[seen_key or eng]
        for key, val in deps.items():
            if seen.get(key, 0) >= val:
                continue
            seen[key] = val
            out.append((key, val))
        return out

    def op(self, eng, fn, reads=(), writes=()):
        if _HALT[0]:
            return
        waits = self._deps(eng, reads, writes)
        self.cnt[eng] += 1
        c = self.cnt[eng]
        key = ("c", eng)
        self.ops[eng].append((waits, fn, ("c", c)))
        for r in reads:
            r.r[key] = (c, eng)
        for w in writes:
            w.w = (key, c, eng)
            w.r = {}

    def dma(self, q, fn, reads=(), writes=()):
        if _HALT[0]:
            return
        waits = self._deps("dma", reads, writes, seen_key=q)
        anchor = writes[0] if writes else reads[0]
        sem = self._dsem(anchor)
        anchor.dcnt += 16
        c = anchor.dcnt
        key = ("d", sem)
        self.ops[q].append((waits, fn, ("d", sem)))
        for r in reads:
            r.r[key] = (c, "dma")
        for w in writes:
            w.w = (key, c, "dma")
            w.r = {}

    def barrier(self):
        for e in self.ENG:
            waits = self._deps(e + "_bar", self.allres, self.allres, seen_key=e)
            if waits:
                self.ops[e].append((waits, None, None))

    def emit(self):
        nc = self.nc
        ops = self.ops
        needed = {e: set() for e in self.ENG}
        for e in self.ENG:
            for waits, fn, inc in ops[e]:
                for key, val in waits:
                    if key[0] == "c":
                        needed[key[1]].add(val)
        rank = {e: {v: i + 1 for i, v in enumerate(sorted(needed[e]))} for e in self.ENG}
        sem = self.sem

        def run(engh, lst):
            for waits, fn, inc in lst:
                for key, val in waits:
                    if key[0] == "c":
                        engh.wait_ge(sem[key[1]], rank[key[1]][val])
                    else:
                        engh.wait_ge(key[1], val)
                if fn is not None:
                    ins = fn(engh)
                    if inc[0] == "d":
                        ins.then_inc(inc[1], 16)
                    elif inc[1] in rank[self._cur]:
                        ins.then_inc(sem[self._cur], 1)

        with nc.Block() as block:
            @block.tensor
            def _(e):
                self._cur = "pe"
                run(e, ops["pe"])

            @block.scalar
            def _(e):
                self._cur = "act"
                run(e, ops["act"])

            @block.vector
            def _(e):
                self._cur = "dve"
                run(e, ops["dve"])

            @block.gpsimd
            def _(e):
                self._cur = "pool"
                run(e, ops["pool"])

            @block.sync
            def _(e):
                self._cur = "sp"
                run(e, ops["sp"])


def _rw(reads, writes):
    rs, ws = [], []
    for v in reads:
        if isinstance(v, V) and v.res not in rs:
            rs.append(v.res)
    for v in writes:
        if v.res not in ws:
            ws.append(v.res)
    return rs, ws


def _a(x):
    return x.ap if isinstance(x, V) else x


class KB:
    def __init__(self, nc, S):
        self.nc = nc
        self.S = S

    def mm(self, out, lhsT, rhs, start=True, stop=True):
        rs, ws = _rw([lhsT, rhs], [out])
        self.S.op("pe", lambda e: e.matmul(out.ap, lhsT=lhsT.ap, rhs=rhs.ap, start=start, stop=stop), rs, ws)

    def tr(self, out, in_, ident):
        rs, ws = _rw([in_, ident], [out])
        self.S.op("pe", lambda e: e.transpose(out=out.ap, in_=in_.ap, identity=ident.ap), rs, ws)

    def act(self, out, in_, func, bias=None, scale=None, accum=None):
        rs, ws = _rw([in_, bias, scale], [out] + ([accum] if accum is not None else []))
        kw = {}
        if bias is not None:
            kw["bias"] = _a(bias)
        if scale is not None:
            kw["scale"] = _a(scale)
        if accum is not None:
            kw["accum_out"] = accum.ap
        self.S.op("act", lambda e: e.activation(out=out.ap, in_=in_.ap, func=func, **kw), rs, ws)

    def copy(self, eng, out, in_):
        rs, ws = _rw([in_], [out])
        if eng == "act":
            self.S.op("act", lambda e: e.activation(out=out.ap, in_=in_.ap, func=AF.Copy), rs, ws)
        else:
            self.S.op(eng, lambda e: e.tensor_copy(out=out.ap, in_=in_.ap), rs, ws)

    def tt(self, eng, out, a, b, op):
        rs, ws = _rw([a, b], [out])
        self.S.op(eng, lambda e: e.tensor_tensor(out=out.ap, in0=a.ap, in1=b.ap, op=op), rs, ws)

    def ts(self, eng, out, a, s1, s2=None, op0=ALU.mult, op1=None):
        rs, ws = _rw([a, s1, s2], [out])
        if op1 is None:
            self.S.op(eng, lambda e: e.tensor_scalar(out=out.ap, in0=a.ap, scalar1=_a(s1), scalar2=0.0, op0=op0, op1=ALU.add), rs, ws)
        else:
            self.S.op(eng, lambda e: e.tensor_scalar(out=out.ap, in0=a.ap, scalar1=_a(s1), scalar2=_a(s2), op0=op0, op1=op1), rs, ws)

    def stt(self, eng, out, a, scalar, b, op0, op1):
        rs, ws = _rw([a, scalar, b], [out])
        self.S.op(eng, lambda e: e.scalar_tensor_tensor(out=out.ap, in0=a.ap, scalar=_a(scalar), in1=b.ap, op0=op0, op1=op1), rs, ws)

    def red(self, eng, out, in_, op=ALU.add):
        rs, ws = _rw([in_], [out])
        self.S.op(eng, lambda e: e.tensor_reduce(out=out.ap, in_=in_.ap, axis=AX.X, op=op), rs, ws)

    def scan(self, out, d0, d1):
        rs, ws = _rw([d0, d1], [out])
        self.S.op("dve", lambda e: e.tensor_tensor_scan(out=out.ap, data0=d0.ap, data1=d1.ap, initial=0.0,
                                                        op0=ALU.mult, op1=ALU.add), rs, ws)

    def recip(self, out, in_):
        rs, ws = _rw([in_], [out])
        self.S.op("dve", lambda e: e.reciprocal(out=out.ap, in_=in_.ap), rs, ws)

    def memset(self, eng, out, val):
        rs, ws = _rw([], [out])
        self.S.op(eng, lambda e: e.memset(out.ap, val), rs, ws)

    def dma(self, q, out, in_):
        rs, ws = _rw([in_], [out])
        self.S.dma(q, lambda e: e.dma_start(out=out.ap, in_=in_.ap), rs, ws)


import os
_STOP = int(os.environ.get("KSTOP", "99"))


class _Stop(Exception):
    pass


_HALT = [False]


def _ck(k):
    if _STOP == k:
        _HALT[0] = True


def build_nc(NT, RANGE):
    SEQ = NT * TILE
    OWN0 = SEQ - RANGE - 2
    OC0 = OWN0 // CH
    NCHK = SEQ // CH
    NOC = NCHK - OC0
    HALO = OWN0 - OC0 * CH + 2
    NTOK2 = NOC * CH
    OT0 = OC0 // 4

    nc = bass.Bass("TRN2", target_bir_lowering=False)
    _HALT[0] = False
    dr = {}

    def din(name, shape):
        dr[name] = nc.dram_tensor(name, list(shape), F32, kind="ExternalInput").ap()
        return dr[name]

    xs = din("xs", [SEQ, D])
    win = din("win", [52, 128, KC * 128])
    wgu = din("wgu", [NFB * 2, 128, KC * 128])
    wdn = din("wdn", [NFB, 128, D])
    wout = din("wout", [128, KC * D])
    lora = din("lora", [128, 4 * RW])
    vecs_d = din("vecs", [128, NV])
    const_d = din("consts", [128, NCONST])
    bcv = {n: din(n, [1, w]) for n, w in (("g_pre_mix", D), ("g_post_mix", D), ("g_pre_ffn", D),
                                          ("g_post_ffn", D), ("ln_x_w", RW), ("ln_x_b", RW))}
    out_d = nc.dram_tensor("out", [NTOK2, D], F32, kind="ExternalOutput").ap()
    yT_d = nc.dram_tensor("yT_s", [KC, 128, NTOK2], BF16, kind="Internal").ap()
    n2T_d = nc.dram_tensor("n2T_s", [KC, 128, NTOK2], BF16, kind="Internal").ap()
    h_d = nc.dram_tensor("h_s", [NTOK2, D], F32, kind="Internal").ap()

    with contextlib.ExitStack() as top:
        S = Sched(nc, top)
        K = KB(nc, S)
        R = S.res
        xs_v = V(xs, R("xs"))
        win_v = V(win, R("win"))
        wgu_v = V(wgu, R("wgu"))
        wdn_v = V(wdn, R("wdn"))
        wout_v = V(wout, R("wout"))
        lora_v = V(lora, R("lora"))
        out_v = V(out_d, R("out"))
        yT_v = V(yT_d, R("yT"))
        n2T_v = V(n2T_d, R("n2T"))
        h_v = V(h_d, R("h"))

        banks = []
        for i in range(8):
            t = top.enter_context(nc.psum_tensor("pb%d" % i, [128, 512], F32))
            banks.append(V(t[:, :], R("pb%d" % i)))
        bank_i = [0]

        pools = {"a": list(range(8)), "p": [0, 1, 2], "s": [3, 4, 5, 6, 7]}
        pool_i = {"a": 0, "p": 0, "s": 0}

        def psum(pool="a"):
            lst = pools[pool]
            b = banks[lst[pool_i[pool] % len(lst)]]
            pool_i[pool] += 1
            return b

        def sbt(stack, name, shape, dt):
            t = stack.enter_context(nc.sbuf_tensor("sb_" + name, list(shape), dt))
            return V(t[tuple(slice(None) for _ in shape)], R(name))

        vecs = sbt(top, "vecs", [128, NV], F32)
        vder = sbt(top, "vder", [128, 28 + 8], F32)
        cf = sbt(top, "cf", [128, NCONST], F32)
        cb = sbt(top, "cb", [128, NCONST], BF16)
        ones = sbt(top, "ones", [128, 128], F32)
        K.dma("sp", vecs, V(vecs_d, R("vecs_d")))
        K.dma("sp", cf, V(const_d, R("const_d")))
        K.dma("pool", cb, V(const_d, R("const_d2")))
        K.memset("dve", ones, 1.0)
        K.ts("dve", vder[:, 0:28], vecs[:, V_MU:V_MU + 28], -1.0, 1.0, ALU.mult, ALU.add)
        K.ts("dve", vder[:, 28:36], vecs[:, V_KA:V_KA + 8], -1.0, 1.0, ALU.mult, ALU.add)
        ident_b = cb[:, C_ID:C_ID + 128]
        mask4 = cb[:, C_M4:C_M4 + 512]
        maskL4 = cb[:, C_ML:C_ML + 512]
        blockones = cf[:, C_BO:C_BO + 128]
        headsel_b = cb[:, C_HS:C_HS + 2]
        hk = vecs[:, V_HK:V_HK + 1]

        def rms_rstd(src, rstd, junk, nbanks=None):
            pass

        try:
          _ck(1)
          with contextlib.ExitStack() as p1:
              gpre = sbt(p1, "gpre", [128, D], F32)
              lnw = sbt(p1, "lnw", [128, RW], F32)
              lnb = sbt(p1, "lnb", [128, RW], F32)
              K.dma("sp", gpre, V(bcv["g_pre_mix"].partition_broadcast(128), R("bc0")))
              K.dma("sp", lnw, V(bcv["ln_x_w"].partition_broadcast(128), R("bc1")))
              K.dma("sp", lnb, V(bcv["ln_x_b"].partition_broadcast(128), R("bc2")))
              lw_b = sbt(p1, "lw_b", [128, 4 * RW], BF16)
              K.dma("pool", lw_b, lora_v)
              wring = [sbt(p1, "wr%d" % i, [128, KC, 128], BF16) for i in range(4)]
              wr_i = [0]
              xin = sbt(p1, "xin", [128, D], F32)
              xn = sbt(p1, "xn", [128, D], BF16)
              st4 = sbt(p1, "st4", [128, 8], F32)
              nT = sbt(p1, "nT", [128, KC, TILE], BF16)
              carry = sbt(p1, "carry", [128, 28], F32)
              K.memset("dve", carry, 0.0)
              ccar = sbt(p1, "ccar", [128, 8, 2], F32)
              K.memset("dve", ccar, 0.0)
              tw = sbt(p1, "tw", [128, TILE], BF16)
              adb = sbt(p1, "adb", [128, TILE], BF16)
              sg2 = [sbt(p1, "sg%d" % i, [128, 2, TILE], BF16) for i in range(2)]
              ztmp = sbt(p1, "ztmp", [128, TILE], F32)
              zr = sbt(p1, "zr", [128, TILE], F32)
              zk = sbt(p1, "zk", [128, TILE], F32)
              zv = sbt(p1, "zv", [128, TILE], F32)
              fA = [sbt(p1, "fA%d" % i, [128, TILE], F32) for i in range(8)]
              AR2 = [sbt(p1, "AR%d" % i, [128, 2, 4, 256], BF16) for i in range(2)]
              BK2 = [sbt(p1, "BK%d" % i, [128, 2, 4, 256], BF16) for i in range(2)]
              TF2 = [sbt(p1, "TF%d" % i, [128, 2, 3, TILE], BF16) for i in range(2)]
              rkb2 = [sbt(p1, "rkb%d" % i, [128, 2, TILE], BF16) for i in range(2)]
              PC2 = [sbt(p1, "PCd%d" % i, [128, 2, 4], F32) for i in range(2)]
              TOK = [sbt(p1, "TOK%d" % i, [128, 2, 4, 128], BF16) for i in range(2)]
              SA = [sbt(p1, "SA%d" % i, [128, 4, 512], BF16) for i in range(2)]
              SN = [[sbt(p1, "SN%d_%d" % (i, j), [128, 4, 128], BF16) for j in range(2)] for i in range(2)]
              SL = [[sbt(p1, "SL%d_%d" % (i, j), [128, 4, 128], BF16) for j in range(2)] for i in range(2)]
              WW = [[sbt(p1, "WW%d_%d" % (i, j), [128, 4, 128], BF16) for j in range(2)] for i in range(2)]
              GT = [sbt(p1, "GT%d" % i, [128, 2, 64], BF16) for i in range(2)]
              RH = [sbt(p1, "RH%d" % i, [128, 2, 128], BF16) for i in range(2)]
              Tst = sbt(p1, "Tst", [128, 8, 64], F32)
              Tbf = [sbt(p1, "Tbf%d" % i, [128, 8, 64], BF16) for i in range(2)]
              K.memset("dve", Tst, 0.0)
              K.memset("dve", Tbf[0], 0.0)
              K.memset("dve", Tbf[1], 0.0)
              tb_i = [[0] * 8]
              ysb = sbt(p1, "ysb", [128, 256], F32)
              ysq = sbt(p1, "ysq", [128, 256], F32)
              yn = sbt(p1, "yn", [128, 256], F32)
              bon = sbt(p1, "bon", [128, 256], F32)
              gst = sbt(p1, "gst", [128, 24], F32)
              ytb = sbt(p1, "ytb", [128, 256], BF16)
              yTs = sbt(p1, "yTs", [128, 2, TILE], BF16)
              csb = sbt(p1, "csb", [128, TILE], F32)
              chb = sbt(p1, "chb", [128, TILE + 2], F32)
              ycv = sbt(p1, "ycv", [128, TILE], BF16)

              def load_w(cbi):
                  w = wring[wr_i[0] % 4]
                  wr_i[0] += 1
                  K.dma("pool", w.re("p k j -> p (k j)"), win_v[cbi])
                  return w

              def project(w, t_cols=TILE):
                  pz = psum("p")
                  for kc in range(KC):
                      K.mm(pz[:, 0:t_cols], w[:, kc, :], nT[:, kc, 0:t_cols], start=(kc == 0), stop=(kc == KC - 1))
                      if kc % 4 == 3:
                          yield
                  return pz

              def shift(pz, col, out):
                  mu = vecs[:, V_MU + col:V_MU + col + 1]
                  omu = vder[:, col:col + 1]
                  K.act(ztmp, pz, AF.Copy, scale=omu)
                  K.stt("dve", out[:, 1:TILE], pz[:, 0:TILE - 1], mu, ztmp[:, 1:TILE], ALU.mult, ALU.add)
                  K.stt("dve", out[:, 0:1], carry[:, col:col + 1], mu, ztmp[:, 0:1], ALU.mult, ALU.add)
                  K.copy("act", carry[:, col:col + 1], pz[:, TILE - 1:TILE])

              def front(ti):
                  t0 = ti * TILE
                  sg = sg2[ti % 2]
                  for j in range(4):
                      K.dma("sp", xin, xs_v[t0 + j * CH:t0 + (j + 1) * CH, :])
                      K.memset("dve", st4[:, 0:1], 0.0)
                      K.act(xn, xin, AF.Square, accum=st4[:, 0:1])
                      K.ts("dve", st4[:, 1:2], st4[:, 0:1], 1.0 / D, RMS_EPS, ALU.mult, ALU.add)
                      K.act(st4[:, 2:3], st4[:, 1:2], AF.Sqrt)
                      K.recip(st4[:, 3:4], st4[:, 2:3])
                      K.stt("dve", xn, xin, st4[:, 3:4], gpre, ALU.mult, ALU.mult)
                      yield
                      for a in range(2):
                          pb = psum("p")
                          pbb = pb.bitcast(BF16)
                          for k8 in range(8):
                              kc = a * 8 + k8
                              K.tr(pbb[:, k8 * 128:(k8 + 1) * 128], xn[:, kc * 128:(kc + 1) * 128], ident_b)
                          K.copy("act" if a == 0 else "dve", nT[:, a * 8:(a + 1) * 8, j * CH:(j + 1) * CH],
                                 pbb.re("p (k t) -> p k t", k=8))
                          yield
                  pz = yield from project(load_w(0))
                  shift(pz, 0, zr)
                  K.act(tw, zr, AF.Tanh)
                  yield
                  pz = yield from project(load_w(1))
                  shift(pz, 1, zr)
                  K.copy("act", adb, zr)
                  yield
                  for gi in range(2):
                      pz = yield from project(load_w(2 + gi))
                      shift(pz, 2 + gi, zr)
                      K.act(sg[:, gi, :], zr, AF.Sigmoid)
                      yield

              def prep(ti, hg):
                  out_tile = ti >= OT0
                  par = (ti * 4 + hg) % 2
                  AR, BK, TF, rkb, PC = AR2[par], BK2[par], TF2[par], rkb2[par], PC2[par]
                  for s in range(2):
                      hp = hg * 2 + s
                      for qi, dst in enumerate((zr, zk, zv)):
                          pz = yield from project(load_w(4 + hp * 3 + qi))
                          shift(pz, 4 + hp * 3 + qi, dst)
                          yield
                      plw = psum("p")
                      K.mm(plw, lw_b[:, hp * 128:(hp + 1) * 128], tw)
                      pa = psum("p")
                      K.mm(pa, lw_b[:, RW + hp * 128:RW + (hp + 1) * 128], adb)
                      sw, aa, cs, kk, t1, t2, t3, t4 = fA
                      K.act(sw, plw, AF.Sigmoid, bias=vecs[:, V_W0 + hp:V_W0 + hp + 1])
                      yield
                      K.act(aa, pa, AF.Sigmoid, bias=vecs[:, V_A0 + hp:V_A0 + hp + 1])
                      yield
                      yield
                      for c in range(4):
                          K.scan(cs[:, c * CH:(c + 1) * CH], ones, sw[:, c * CH:(c + 1) * CH])
                      K.ts("dve", kk, zk, vecs[:, V_KK + hp:V_KK + hp + 1])
                      yield
                      K.tt("dve", t1, kk, kk, ALU.mult)
                      yield
                      yield
                      pss = psum("p")
                      K.mm(pss, blockones, t1)
                      K.act(t1, pss, AF.Sqrt)
                      yield
                      K.ts("dve", t1, t1, 1e-12, None, ALU.max)
                      yield
                      K.recip(t2, t1)
                      yield
                      K.tt("dve", kk, kk, t2, ALU.mult)
                      yield
                      yield
                      K.ts("dve", t1, aa, vecs[:, V_KA + hp:V_KA + hp + 1], vder[:, 28 + hp:29 + hp], ALU.mult, ALU.add)
                      yield
                      K.tt("dve", t1, zk, t1, ALU.mult)
                      yield
                      K.tt("dve", aa, kk, aa, ALU.mult)
                      yield
                      yield
                      K.tt("dve", t2, cs, sw, ALU.subtract)
                      yield
                      K.act(t2, t2, AF.Exp, scale=-DECAY_C)
                      yield
                      K.stt("dve", AR[:, s, :, 0:128], kk.re("p (c t) -> p c t", c=4), -1.0,
                            t2.re("p (c t) -> p c t", c=4), ALU.mult, ALU.mult)
                      yield
                      K.act(t3, cs, AF.Exp, scale=-DECAY_C)
                      yield
                      K.tt("dve", AR[:, s, :, 128:256], zr.re("p (c t) -> p c t", c=4),
                           t3.re("p (c t) -> p c t", c=4), ALU.mult)
                      yield
                      K.act(t3, cs, AF.Exp, scale=DECAY_C)
                      yield
                      K.tt("dve", BK[:, s, :, 0:128], aa.re("p (c t) -> p c t", c=4),
                           t3.re("p (c t) -> p c t", c=4), ALU.mult)
                      K.tt("dve", BK[:, s, :, 128:256], t1.re("p (c t) -> p c t", c=4),
                           t3.re("p (c t) -> p c t", c=4), ALU.mult)
                      yield
                      K.ts("dve", st4[:, 4:8], cs.re("p (c t) -> p c t", c=4)[:, :, CH - 1], -DECAY_C)
                      yield
                      K.act(PC[:, s, :], st4[:, 4:8], AF.Exp)
                      yield
                      for c in range(4):
                          K.act(t4[:, c * CH:(c + 1) * CH], cs[:, c * CH:(c + 1) * CH], AF.Exp,
                                scale=DECAY_C, bias=st4[:, 4 + c:5 + c])
                      yield
                      K.tt("dve", TF[:, s, 1, :], aa, t4, ALU.mult)
                      yield
                      K.tt("dve", TF[:, s, 2, :], t1, t4, ALU.mult)
                      yield
                      K.copy("act", TF[:, s, 0, :], zv)
                      yield
                      if out_tile:
                          K.tt("dve", t2, zr, t1, ALU.mult)
                          yield
                          K.ts("dve", rkb[:, s, :], t2, vecs[:, V_RK + hp:V_RK + hp + 1])
                          yield
                      yield

              def scan(ti, hg):
                  out_tile = ti >= OT0
                  par = (ti * 4 + hg) % 2
                  AR, BK, TF, rkb, PC = AR2[par], BK2[par], TF2[par], rkb2[par], PC2[par]
                  sg = sg2[ti % 2]
                  for cp in range(2):
                      cl = [2 * cp, 2 * cp + 1]
                      for u, c in enumerate(cl):
                          pb = psum("s")
                          pbb = pb.bitcast(BF16).re("p (s q t) -> p s q t", s=2, q=4)
                          for s in range(2):
                              for q in range(3):
                                  K.tr(pbb[:, s, q, :], TF[:, s, q, c * CH:(c + 1) * CH], ident_b)
                              K.tr(pbb[:, s, 3, :], AR[:, s, c, 0:128], ident_b)
                          K.copy("act", TOK[u], pbb)
                          yield
                      for u, c in enumerate(cl):
                          for hl in range(4):
                              s, e = hl // 2, hl % 2
                              ps_ = slice(64 * e, 64 * e + 64)
                              pa_ = psum("s")
                              K.mm(pa_[:, 0:256], BK[ps_, s, c, 0:128], AR[ps_, s, c, :])
                              K.mm(pa_[:, 256:512], BK[ps_, s, c, 128:256], AR[ps_, s, c, :])
                              K.tt("dve", SA[u][:, hl, :], pa_, mask4, ALU.mult)
                          yield
                          ple = [psum("s"), psum("s")]
                          for hl in range(4):
                              s, e = hl // 2, hl % 2
                              ps_ = slice(64 * e, 64 * e + 64)
                              K.mm(ple[e][:, s * 128:(s + 1) * 128], AR[ps_, s, c, 0:128], BK[ps_, s, c, 0:128])
                          for e in range(2):
                              K.tt("dve", SL[u][0].re("p (s e) t -> p s e t", e=2)[:, :, e, :],
                                   ple[e][:, 0:256].re("p (s t) -> p s t", s=2),
                                   maskL4[:, 0:256].re("p (s t) -> p s t", s=2), ALU.mult)
                          px = psum("s")
                          for hl in range(4):
                              s, e = hl // 2, hl % 2
                              K.mm(px[:, hl * 64:(hl + 1) * 64], SA[u][:, hl, 256:384], TOK[u][:, s, 0, 64 * e:64 * e + 64])
                          K.copy("act", WW[u][0][:, :, 64:128], px[:, 0:256].re("p (h v) -> p h v", h=4))
                          K.copy("act", WW[u][0].re("p (s e) k -> p s e k", s=2)[:, :, :, 0:64],
                                 TOK[u][:, :, 3, :].re("p s (e k) -> p s e k", e=2))
                          yield
                      for i in range(7):
                          for u, c in enumerate(cl):
                              n_cur = SA[u][:, :, 0:128] if i == 0 else SN[u][i % 2]
                              l_cur = SL[u][i % 2]
                              w_cur = WW[u][i % 2]
                              pw = psum("s")
                              for hl in range(4):
                                  K.mm(pw[:, hl * 128:(hl + 1) * 128], n_cur[:, hl, :], w_cur[:, hl, :])
                              K.tt("dve", WW[u][(i + 1) % 2].re("p h t -> p (h t)"), pw,
                                   w_cur.re("p h t -> p (h t)"), ALU.add)
                              if i < 6:
                                  pn = psum("s")
                                  for hl in range(4):
                                      K.mm(pn[:, hl * 128:(hl + 1) * 128], l_cur[:, hl, :], n_cur[:, hl, :])
                                  K.copy("act", SN[u][(i + 1) % 2].re("p h t -> p (h t)"), pn)
                              if i < 5:
                                  pl_ = psum("s")
                                  for hl in range(4):
                                      K.mm(pl_[:, hl * 128:(hl + 1) * 128], n_cur[:, hl, :], l_cur[:, hl, :])
                                  K.copy("act", SL[u][(i + 1) % 2].re("p h t -> p (h t)"), pl_)
                              yield
                      for u, c in enumerate(cl):
                          Wf = WW[u][1]
                          chunk_g = ti * 4 + c
                          want_y = chunk_g >= OC0
                          pg = psum("s")
                          for hl in range(4):
                              s, e = hl // 2, hl % 2
                              K.mm(pg[64 * e:64 * e + 64, s * 64:(s + 1) * 64], Wf[:, hl, 0:64],
                                   TOK[u][:, s, 1, 64 * e:64 * e + 64])
                          K.copy("act", GT[u].re("p s k -> p (s k)"), pg[:, 0:128])
                          tcur = Tbf[tb_i[0][hg] % 2]
                          tnew = Tbf[(tb_i[0][hg] + 1) % 2]
                          if want_y:
                              pr = psum("s")
                              for hl in range(4):
                                  s, e = hl // 2, hl % 2
                                  K.mm(pr[64 * e:64 * e + 64, s * 128:(s + 1) * 128], Wf[:, hl, 0:64], SA[u][:, hl, 128:256])
                              K.tt("dve", RH[u], pr[:, 0:256].re("p (s t) -> p s t", s=2), AR[:, :, c, 128:256], ALU.add)
                              py = psum("s")
                              for hl in range(4):
                                  s, e = hl // 2, hl % 2
                                  ps_ = slice(64 * e, 64 * e + 64)
                                  o = py[:, hl * 64:(hl + 1) * 64]
                                  K.mm(o, SA[u][:, hl, 128:256], Wf[:, hl, 64:128], start=True, stop=False)
                                  K.mm(o, SA[u][:, hl, 384:512], TOK[u][:, s, 0, 64 * e:64 * e + 64], start=False, stop=False)
                                  K.mm(o, RH[u][ps_, s, :], tcur[ps_, hg * 2 + s, :], start=False, stop=True)
                          pt = psum("s")
                          for hl in range(4):
                              s, e = hl // 2, hl % 2
                              ps_ = slice(64 * e, 64 * e + 64)
                              o = pt[ps_, s * 64:(s + 1) * 64]
                              K.mm(o, TOK[u][:, s, 1, 64 * e:64 * e + 64], Wf[:, hl, 64:128], start=True, stop=False)
                              K.mm(o, TOK[u][:, s, 2, 64 * e:64 * e + 64], TOK[u][:, s, 0, 64 * e:64 * e + 64], start=False, stop=False)
                              K.mm(o, GT[u][ps_, s, :], tcur[ps_, hg * 2 + s, :], start=False, stop=True)
                          for s in range(2):
                              K.stt("dve", Tst[:, hg * 2 + s, :], Tst[:, hg * 2 + s, :], PC[:, s, c:c + 1],
                                    pt[:, s * 64:(s + 1) * 64], ALU.mult, ALU.add)
                          K.copy("act", tnew[:, hg * 2:hg * 2 + 2, :], Tst[:, hg * 2:hg * 2 + 2, :])
                          tb_i[0][hg] += 1
                          if want_y:
                              K.copy("act", ysb, py[:, 0:256])
                          yield
                          if not want_y:
                              continue
                          pgt = psum("s")
                          K.mm(pgt[:, 0:256], sg[:, 0, c * CH:(c + 1) * CH], lw_b[:, 2 * RW + hg * 256:2 * RW + (hg + 1) * 256],
                               start=True, stop=False)
                          K.mm(pgt[:, 0:256], sg[:, 1, c * CH:(c + 1) * CH], lw_b[:, 3 * RW + hg * 256:3 * RW + (hg + 1) * 256],
                               start=False, stop=True)
                          for s in range(2):
                              K.mm(pgt[:, 256 + 2 * s:258 + 2 * s], rkb[:, s, c * CH:(c + 1) * CH], headsel_b)
                          K.copy("act", gst[:, 16:20], pgt[:, 256:260])
                          y3 = ysb.re("p (h v) -> p h v", h=4)
                          K.red("dve", gst[:, 0:4], y3)
                          K.tt("dve", ysq, ysb, ysb, ALU.mult)
                          K.red("dve", gst[:, 4:8], ysq.re("p (h v) -> p h v", h=4))
                          K.ts("dve", gst[:, 0:4], gst[:, 0:4], 1.0 / 64)
                          K.tt("dve", gst[:, 8:12], gst[:, 0:4], gst[:, 0:4], ALU.mult)
                          K.stt("dve", gst[:, 4:8], gst[:, 4:8], 1.0 / 64, gst[:, 8:12], ALU.mult, ALU.subtract)
                          K.ts("dve", gst[:, 4:8], gst[:, 4:8], GN_EPS, None, ALU.add)
                          K.act(gst[:, 8:12], gst[:, 4:8], AF.Sqrt)
                          K.recip(gst[:, 12:16], gst[:, 8:12])
                          yn3 = yn.re("p (h v) -> p h v", h=4)
                          K.tt("dve", yn3, y3, gst[:, 0:4].bc([128, 4, 64], 2), ALU.subtract)
                          K.tt("dve", yn3, yn3, gst[:, 12:16].bc([128, 4, 64], 2), ALU.mult)
                          K.tt("dve", yn, yn, lnw[:, hg * 256:(hg + 1) * 256], ALU.mult)
                          K.tt("dve", yn, yn, lnb[:, hg * 256:(hg + 1) * 256], ALU.add)
                          K.tt("dve", bon.re("p (s e v) -> p s e v", s=2, e=2),
                               TOK[u][:, :, 0, :].re("p s (e v) -> p s e v", e=2),
                               gst[:, 16:20].re("p (s e) -> p s e", s=2).bc([128, 2, 2, 64], 3), ALU.mult)
                          K.tt("dve", yn, yn, bon, ALU.add)
                          K.tt("dve", ytb, yn, pgt[:, 0:256], ALU.mult)
                          pb = psum("s")
                          pbb = pb.bitcast(BF16)
                          for s in range(2):
                              K.tr(pbb[:, s * 128:(s + 1) * 128], ytb[:, s * 128:(s + 1) * 128], ident_b)
                          K.copy("act", yTs[:, :, c * CH:(c + 1) * CH], pbb[:, 0:256].re("p (s t) -> p s t", s=2))
                          yield
                  if out_tile:
                      c_lo = max(0, OC0 - ti * 4)
                      tk0 = (ti * 4 + c_lo - OC0) * CH
                      ncol = (4 - c_lo) * CH
                      for s in range(2):
                          K.dma("sp", yT_v[8 + hg * 2 + s, :, tk0:tk0 + ncol], yTs[:, s, c_lo * CH:4 * CH])
                  yield

              def conv(ti):
                  c_lo = max(0, OC0 - ti * 4)
                  tk0 = (ti * 4 + c_lo - OC0) * CH
                  ncol = (4 - c_lo) * CH
                  for i in range(8):
                      pB = yield from project(load_w(28 + i * 3 + 0))
                      pC = yield from project(load_w(28 + i * 3 + 1))
                      pH = yield from project(load_w(28 + i * 3 + 2))
                      K.copy("act", csb, pC)
                      K.copy("act", chb[:, 0:2], ccar[:, i, :])
                      K.tt("dve", chb[:, 2:TILE + 2], csb, pH, ALU.mult)
                      K.copy("act", ccar[:, i, :], chb[:, TILE:TILE + 2])
                      K.ts("dve", csb, chb[:, 0:TILE], vecs[:, V_CW + i:V_CW + i + 1])
                      K.stt("dve", csb, chb[:, 1:TILE + 1], vecs[:, V_CW + 8 + i:V_CW + 9 + i], csb, ALU.mult, ALU.add)
                      K.stt("dve", csb, chb[:, 2:TILE + 2], vecs[:, V_CW + 16 + i:V_CW + 17 + i], csb, ALU.mult, ALU.add)
                      K.tt("dve", ycv, csb, pB, ALU.mult)
                      K.dma("sp", yT_v[i, :, tk0:tk0 + ncol], ycv[:, c_lo * CH:4 * CH])
                      yield

              def chain(*gens):
                  for g in gens:
                      yield from g

              def interleave(ga, gb):
                  la, lb = True, True
                  while la or lb:
                      if la:
                          try:
                              next(ga)
                          except StopIteration:
                              la = False
                      if lb:
                          try:
                              next(gb)
                          except StopIteration:
                              lb = False

              PIPE = os.environ.get("KPIPE", "1") == "1"
              units = [(ti, hg) for ti in range(NT) for hg in range(4)]

              def p_stream(ti, hg):
                  gens = []
                  if hg == 0:
                      gens.append(front(ti))
                  gens.append(prep(ti, hg))
                  return gens

              for g in chain(*p_stream(0, 0)):
                  pass
              for ui, (ti, hg) in enumerate(units):
                  nxt = []
                  if hg == 3 and ti >= OT0:
                      nxt.append(conv(ti))
                  if ui + 1 < len(units):
                      nxt += p_stream(*units[ui + 1])
                  if PIPE:
                      interleave(scan(ti, hg), chain(*nxt))
                  else:
                      for g in scan(ti, hg):
                          pass
                      for g in chain(*nxt):
                          pass
          _ck(8)
          S.barrier()

          with contextlib.ExitStack() as p15:
              wo_b = sbt(p15, "wo_b", [128, KC, D], BF16)
              K.dma("pool", wo_b.re("p k d -> p (k d)"), wout_v)
              gpm = sbt(p15, "gpm", [128, D], F32)
              gpf = sbt(p15, "gpf", [128, D], F32)
              K.dma("sp", gpm, V(bcv["g_post_mix"].partition_broadcast(128), R("bc3")))
              K.dma("sp", gpf, V(bcv["g_pre_ffn"].partition_broadcast(128), R("bc4")))
              ych = [sbt(p15, "ych%d" % i, [128, KC, CH], BF16) for i in range(2)]
              xin2 = [sbt(p15, "xin2_%d" % i, [128, D], F32) for i in range(2)]
              hsb = [sbt(p15, "hsb%d" % i, [128, D], F32) for i in range(2)]
              n2 = sbt(p15, "n2", [128, D], BF16)
              n2T = [sbt(p15, "n2T%d" % i, [128, KC, CH], BF16) for i in range(2)]
              st5 = sbt(p15, "st5", [128, 16], F32)
              for oc in range(NOC):
                  yc = ych[oc % 2]
                  xi = xin2[oc % 2]
                  hs = hsb[oc % 2]
                  nt2 = n2T[oc % 2]
                  K.dma("sp", yc, yT_v[:, :, oc * CH:(oc + 1) * CH].re("c p t -> p c t"))
                  K.dma("sp", xi, xs_v[(OC0 + oc) * CH:(OC0 + oc + 1) * CH, :])
                  pm = [psum() for _ in range(4)]
                  for blk in range(4):
                      for chc in range(KC):
                          K.mm(pm[blk], yc[:, chc, :], wo_b[:, chc, blk * 512:(blk + 1) * 512],
                               start=(chc == 0), stop=(chc == KC - 1))
                  K.memset("dve", st5[:, 0:4], 0.0)
                  for blk in range(4):
                      K.act(n2[:, blk * 512:(blk + 1) * 512], pm[blk], AF.Square, accum=st5[:, blk:blk + 1])
                  K.red("dve", st5[:, 4:5], st5[:, 0:4])
                  K.ts("dve", st5[:, 4:5], st5[:, 4:5], 1.0 / D, RMS_EPS, ALU.mult, ALU.add)
                  K.act(st5[:, 5:6], st5[:, 4:5], AF.Sqrt)
                  K.recip(st5[:, 6:7], st5[:, 5:6])
                  for blk in range(4):
                      sl = slice(blk * 512, (blk + 1) * 512)
                      K.stt("dve", hs[:, sl], pm[blk], st5[:, 6:7], gpm[:, sl], ALU.mult, ALU.mult)
                  K.tt("dve", hs, hs, xi, ALU.add)
                  K.dma("sp", h_v[oc * CH:(oc + 1) * CH, :], hs)
                  K.memset("dve", st5[:, 8:9], 0.0)
                  K.act(n2, hs, AF.Square, accum=st5[:, 8:9])
                  K.ts("dve", st5[:, 9:10], st5[:, 8:9], 1.0 / D, RMS_EPS, ALU.mult, ALU.add)
                  K.act(st5[:, 10:11], st5[:, 9:10], AF.Sqrt)
                  K.recip(st5[:, 11:12], st5[:, 10:11])
                  K.stt("dve", n2, hs, st5[:, 11:12], gpf, ALU.mult, ALU.mult)
                  for a in range(2):
                      pb = psum()
                      pbb = pb.bitcast(BF16)
                      for k8 in range(8):
                          kc = a * 8 + k8
                          K.tr(pbb[:, k8 * 128:(k8 + 1) * 128], n2[:, kc * 128:(kc + 1) * 128], ident_b)
                      K.copy("act", nt2[:, a * 8:(a + 1) * 8, :], pbb.re("p (k t) -> p k t", k=8))
                  if oc == 0:
                      K.ts("dve", nt2[:, :, 0:HALO], nt2[:, :, 0:HALO], hk)
                  K.dma("sp", n2T_v[:, :, oc * CH:(oc + 1) * CH].re("c p t -> p c t"), nt2)
          S.barrier()

          with contextlib.ExitStack() as p2:
              gpo = sbt(p2, "gpo", [128, D], F32)
              K.dma("sp", gpo, V(bcv["g_post_ffn"].partition_broadcast(128), R("bc5")))
              facc = sbt(p2, "facc", [128, STC, D], F32)
              n2s = sbt(p2, "n2s", [128, KC, STC * CH], BF16)
              wg_r = [sbt(p2, "wg%d" % i, [128, 2, KC, 128], BF16) for i in range(2)]
              wd_r = [sbt(p2, "wd%d" % i, [128, D], BF16) for i in range(2)]
              gbuf = sbt(p2, "gbuf", [128, STC * CH + 2], F32)
              cvt = sbt(p2, "cvt", [128, TILE], F32)
              slu = sbt(p2, "slu", [128, TILE], F32)
              actT = [sbt(p2, "actT%d" % i, [128, STC * CH], BF16) for i in range(2)]
              hld = sbt(p2, "hld", [128, D], F32)
              osb = sbt(p2, "osb", [128, D], F32)
              st6 = sbt(p2, "st6", [128, 8], F32)
              K.memset("dve", gbuf[:, 0:2], 0.0)
              st_starts = []
              s0 = 0
              while True:
                  st_starts.append(s0)
                  if s0 + STC >= NOC:
                      break
                  s0 += STC - 1
              for sti, cs0 in enumerate(st_starts):
                  nch = min(STC, NOC - cs0)
                  ntk = nch * CH
                  first_out = 0 if sti == 0 else 1
                  K.dma("sp", n2s[:, :, 0:ntk], n2T_v[:, :, cs0 * CH:cs0 * CH + ntk].re("c p t -> p c t"))
                  ttiles = [(a, min(TILE, ntk - a)) for a in range(0, ntk, TILE)]
                  for fb in range(NFB):
                      wgb = wg_r[fb % 2]
                      wdb = wd_r[fb % 2]
                      at = actT[fb % 2]
                      K.dma("pool", wgb.re("p g k j -> p g (k j)"), wgu_v[2 * fb:2 * fb + 2].re("g p f -> p g f"))
                      K.dma("pool", wdb, wdn_v[fb])
                      for (a0, tn) in ttiles:
                          pgg = psum()
                          pu = psum()
                          for kc in range(KC):
                              K.mm(pgg[:, 0:tn], wgb[:, 0, kc, :], n2s[:, kc, a0:a0 + tn], start=(kc == 0), stop=(kc == KC - 1))
                          for kc in range(KC):
                              K.mm(pu[:, 0:tn], wgb[:, 1, kc, :], n2s[:, kc, a0:a0 + tn], start=(kc == 0), stop=(kc == KC - 1))
                          K.copy("act", gbuf[:, 2 + a0:2 + a0 + tn], pgg[:, 0:tn])
                          K.ts("dve", cvt[:, 0:tn], gbuf[:, a0:a0 + tn], vecs[:, V_CF + fb:V_CF + fb + 1])
                          K.stt("dve", cvt[:, 0:tn], gbuf[:, a0 + 1:a0 + 1 + tn], vecs[:, V_CF + NFB + fb:V_CF + NFB + fb + 1],
                                cvt[:, 0:tn], ALU.mult, ALU.add)
                          K.stt("dve", cvt[:, 0:tn], gbuf[:, a0 + 2:a0 + 2 + tn], vecs[:, V_CF + 2 * NFB + fb:V_CF + 2 * NFB + fb + 1],
                                cvt[:, 0:tn], ALU.mult, ALU.add)
                          K.act(slu[:, 0:tn], cvt[:, 0:tn], AF.Silu)
                          K.tt("dve", at[:, a0:a0 + tn], slu[:, 0:tn], pu[:, 0:tn], ALU.mult)
                      for ci in range(nch):
                          for blk in range(4):
                              pf = psum()
                              K.mm(pf, at[:, ci * CH:(ci + 1) * CH], wdb[:, blk * 512:(blk + 1) * 512])
                              sl = slice(blk * 512, (blk + 1) * 512)
                              if fb == 0:
                                  K.copy("act", facc[:, ci, sl], pf)
                              else:
                                  K.tt("dve", facc[:, ci, sl], facc[:, ci, sl], pf, ALU.add)
                  for ci in range(first_out, nch):
                      oc = cs0 + ci
                      K.dma("sp", hld, h_v[oc * CH:(oc + 1) * CH, :])
                      K.memset("dve", st6[:, 0:1], 0.0)
                      K.act(osb, facc[:, ci, :], AF.Square, accum=st6[:, 0:1])
                      K.ts("dve", st6[:, 1:2], st6[:, 0:1], 1.0 / D, RMS_EPS, ALU.mult, ALU.add)
                      K.act(st6[:, 2:3], st6[:, 1:2], AF.Sqrt)
                      K.recip(st6[:, 3:4], st6[:, 2:3])
                      K.stt("dve", osb, facc[:, ci, :], st6[:, 3:4], gpo, ALU.mult, ALU.mult)
                      K.tt("dve", osb, osb, hld, ALU.add)
                      K.dma("sp", out_v[oc * CH:(oc + 1) * CH, :], osb)
          S.barrier()
        except _Stop:
            pass
        S.barrier()
        S.emit()
    return nc


def _blk(w):
    return np.ascontiguousarray(w.reshape(KC, 128, 128).transpose(1, 0, 2)).reshape(128, KC * 128)


def _pad_cols(w, n):
    out = np.zeros((w.shape[0], n), w.dtype)
    out[:, :w.shape[1]] = w
    return out


def _consts():
    c = np.zeros((128, NCONST), np.float32)
    j = np.arange(128)[:, None]
    t = np.arange(128)[None, :]
    c[:, C_ID:C_ID + 128] = np.eye(128)
    strict = (t > j).astype(np.float32)
    incl = (t >= j).astype(np.float32)
    c[:, C_M4:C_M4 + 512] = np.concatenate([strict, incl, strict, incl], 1)
    low = (j > t).astype(np.float32)
    c[:, C_ML:C_ML + 512] = np.concatenate([low] * 4, 1)
    c[:, C_BO:C_BO + 128] = (j // 64 == t // 64).astype(np.float32)
    c[:, C_HS:C_HS + 2] = (j // 64 == np.arange(2)[None, :]).astype(np.float32)
    return c


def prepare(inputs, NT, RANGE):
    f = lambda k: np.asarray(inputs[k], np.float32)
    w_in = f("w_in")
    RB = 3072
    cols = []
    cols.append(_pad_cols(w_in[:, RB + 3072:RB + 3168], 128))
    cols.append(_pad_cols(w_in[:, RB + 3168:RB + 3264], 128))
    cols.append(w_in[:, RB + 3264:RB + 3392])
    cols.append(w_in[:, RB + 3392:RB + 3520])
    for i in range(8):
        for base in (0, 1024, 2048):
            cols.append(w_in[:, RB + base + i * 128:RB + base + (i + 1) * 128])
    for i in range(8):
        for base in (0, 1024, 2048):
            cols.append(w_in[:, base + i * 128:base + (i + 1) * 128])
    win = np.stack([_blk(c) for c in cols], 0)
    wgu_full = f("w_gate_up")
    wgu = np.stack([_blk(wgu_full[:, g * DFF + fb * 128:g * DFF + (fb + 1) * 128]) for fb in range(NFB) for g in range(2)], 0)
    wdn = np.ascontiguousarray(f("w_down").reshape(NFB, 128, D))
    wout = np.ascontiguousarray(f("w_out").reshape(KC, 128, D).transpose(1, 0, 2)).reshape(128, KC * D)
    lora = np.zeros((128, 4 * RW), np.float32)
    lora[:96, 0:RW] = f("w_decay_up")
    lora[:96, RW:2 * RW] = f("a_up")
    gup = f("g_up")
    lora[:, 2 * RW:3 * RW] = gup[0:128]
    lora[:, 3 * RW:4 * RW] = gup[128:256]
    mu = f("mu_rwkv")
    vecs = np.zeros((128, NV), np.float32)
    vecs[:96, V_MU + 0] = mu[3072:3168]
    vecs[:96, V_MU + 1] = mu[3168:3264]
    vecs[:, V_MU + 2] = mu[3264:3392]
    vecs[:, V_MU + 3] = mu[3392:3520]
    for i in range(8):
        for qi, base in enumerate((0, 1024, 2048)):
            vecs[:, V_MU + 4 + i * 3 + qi] = mu[base + i * 128:base + (i + 1) * 128]
    for name, col in (("w0", V_W0), ("a0", V_A0), ("k_k", V_KK), ("k_a", V_KA)):
        vecs[:, col:col + 8] = f(name).reshape(8, 128).T
    vecs[:, V_RK:V_RK + 8] = f("r_k").reshape(8, 128).T
    cw = f("conv_w")
    for tap in range(3):
        vecs[:, V_CW + tap * 8:V_CW + tap * 8 + 8] = cw[tap].reshape(8, 128).T
    cfw = f("conv_ffn")
    for tap in range(3):
        vecs[:, V_CF + tap * NFB:V_CF + (tap + 1) * NFB] = cfw[tap].reshape(NFB, 128).T
    shared = {"win": win, "wgu": wgu, "wdn": wdn, "wout": wout, "lora": lora, "consts": _consts()}
    for n in ("g_pre_mix", "g_post_mix", "g_pre_ffn", "g_post_ffn", "ln_x_w", "ln_x_b"):
        shared[n] = f(n).reshape(1, -1)
    x = f("x")
    meta = f("meta_tokens")
    SEQ = NT * TILE
    in_maps = []
    for c in range(8):
        b, q = c // 4, c % 4
        seq = np.concatenate([meta, x[b]], 0)
        n_real = (q + 1) * RANGE
        xs = np.zeros((SEQ, D), np.float32)
        xs[SEQ - n_real:] = seq[:n_real]
        v = vecs.copy()
        v[:, V_HK] = 0.0 if q == 0 else 1.0
        m = dict(shared)
        m["xs"] = xs
        m["vecs"] = v
        in_maps.append(m)
    return in_maps


def kernel(**inputs):
    x = np.asarray(inputs["x"])
    B, L, _ = x.shape
    tot = N_META + L
    RANGE = tot // 4
    NT = -(-(tot + 2) // TILE)
    nc = build_nc(NT, RANGE)
    in_maps = prepare(inputs, NT, RANGE)
    dbg = os.environ.get("KCORES")
    if dbg is not None:
        sel = [int(t) for t in dbg.split(",")]
        res1 = run_bass_kernel_spmd(nc, [in_maps[c] for c in sel], core_ids=list(range(len(sel))))
        return {c: {k: np.asarray(v) for k, v in res1.results[i].items()} for i, c in enumerate(sel)}
    res = run_bass_kernel_spmd(nc, in_maps, core_ids=list(range(8)))
    SEQ = NT * TILE
    OC0 = (SEQ - RANGE - 2) // CH
    off = SEQ - RANGE - OC0 * CH
    full = np.zeros((B, tot, D), np.float32)
    for c in range(8):
        b, q = c // 4, c % 4
        o = np.asarray(res.results[c]["out"])
        full[b, q * RANGE:(q + 1) * RANGE] = o[off:off + RANGE]
    return np.ascontiguousarray(full[:, N_META:])
```

```python
import contextlib
import math
import numpy as np
import concourse.bass as bass
import concourse.mybir as mybir
from concourse.bass_utils import run_bass_kernel_spmd

F32 = mybir.dt.float32
BF16 = mybir.dt.bfloat16
ALU = mybir.AluOpType
AF = mybir.ActivationFunctionType
AX = mybir.AxisListType

D = 2048
KC = 16
TILE = 512
CH = 128
RW = 1024
DFF = 5632
NFB = DFF // 128
N_META = 16
RMS_EPS = 1e-6
GN_EPS = 64e-5
DECAY_C = math.exp(-0.5)
STC = 9

V_MU = 0
V_W0 = 28
V_A0 = 36
V_KK = 44
V_KA = 52
V_RK = 60
V_CW = 68
V_CF = 92
V_HK = 92 + 3 * NFB
NV = V_HK + 1
C_ID = 0
C_M4 = 128
C_ML = 640
C_BO = 1152
C_HS = 1280
NCONST = 1282


class Res:
    __slots__ = ("name", "w", "r", "dsem", "dcnt")

    def __init__(self, name):
        self.name = name
        self.w = None
        self.r = {}
        self.dsem = None
        self.dcnt = 0


class V:
    __slots__ = ("ap", "res")

    def __init__(self, ap, res):
        self.ap = ap
        self.res = res

    def __getitem__(self, idx):
        return V(self.ap[idx], self.res)

    def re(self, pat, **kw):
        return V(self.ap.rearrange(pat, **kw), self.res)

    def bitcast(self, dt):
        return V(self.ap.bitcast(dt), self.res)

    def bc(self, shape, axis):
        return V(self.ap.unsqueeze(axis).to_broadcast(list(shape)), self.res)


class Sched:
    ENG = ("pe", "act", "dve", "pool", "sp")

    def __init__(self, nc, stack):
        self.nc = nc
        self.stack = stack
        self.ops = {e: [] for e in self.ENG}
        self.sem = {e: stack.enter_context(nc.semaphore("s_" + e)) for e in self.ENG}
        self.cnt = {e: 0 for e in self.ENG}
        self.seen = {e: {} for e in self.ENG}
        self.allres = []

    def res(self, name):
        r = Res(name)
        self.allres.append(r)
        return r

    def _dsem(self, res):
        if res.dsem is None:
            res.dsem = self.stack.enter_context(self.nc.semaphore("d_" + res.name))
        return res.dsem

    def _deps(self, eng, reads, writes, seen_key=None):
        deps = {}

        def add(t, kind):
            if t is None:
                return
            key, val, en = t
            if en == eng and (kind != "raw" or eng == "pe"):
                return
            if deps.get(key, 0) < val:
                deps[key] = val

        for r in reads:
            add(r.w, "raw")
        for w in writes:
            add(w.w, "waw")
            for key, (val, en) in w.r.items():
                add((key, val, en), "war")
        out = []
        seen = self.seen[seen_key or eng]
        for key, val in deps.items():
            if seen.get(key, 0) >= val:
                continue
            seen[key] = val
            out.append((key, val))
        return out

    def op(self, eng, fn, reads=(), writes=()):
        if _HALT[0]:
            return
        waits = self._deps(eng, reads, writes)
        self.cnt[eng] += 1
        c = self.cnt[eng]
        key = ("c", eng)
        self.ops[eng].append((waits, fn, ("c", c)))
        for r in reads:
            r.r[key] = (c, eng)
        for w in writes:
            w.w = (key, c, eng)
            w.r = {}

    def dma(self, q, fn, reads=(), writes=()):
        if _HALT[0]:
            return
        waits = self._deps("dma", reads, writes, seen_key=q)
        anchor = writes[0] if writes else reads[0]
        sem = self._dsem(anchor)
        anchor.dcnt += 16
        c = anchor.dcnt
        key = ("d", sem)
        self.ops[q].append((waits, fn, ("d", sem)))
        for r in reads:
            r.r[key] = (c, "dma")
        for w in writes:
            w.w = (key, c, "dma")
            w.r = {}

    def barrier(self):
        for e in self.ENG:
            waits = self._deps(e + "_bar", self.allres, self.allres, seen_key=e)
            if waits:
                self.ops[e].append((waits, None, None))

    def emit(self):
        nc = self.nc
        ops = self.ops
        needed = {e: set() for e in self.ENG}
        for e in self.ENG:
            for waits, fn, inc in ops[e]:
                for key, val in waits:
                    if key[0] == "c":
                        needed[key[1]].add(val)
        rank = {e: {v: i + 1 for i, v in enumerate(sorted(needed[e]))} for e in self.ENG}
        sem = self.sem

        def run(engh, lst):
            for waits, fn, inc in lst:
                for key, val in waits:
                    if key[0] == "c":
                        engh.wait_ge(sem[key[1]], rank[key[1]][val])
                    else:
                        engh.wait_ge(key[1], val)
                if fn is not None:
                    ins = fn(engh)
                    if inc[0] == "d":
                        ins.then_inc(inc[1], 16)
                    elif inc[1] in rank[self._cur]:
                        ins.then_inc(sem[self._cur], 1)

        with nc.Block() as block:
            @block.tensor
            def _(e):
                self._cur = "pe"
                run(e, ops["pe"])

            @block.scalar
            def _(e):
                self._cur = "act"
                run(e, ops["act"])

            @block.vector
            def _(e):
                self._cur = "dve"
                run(e, ops["dve"])

            @block.gpsimd
            def _(e):
                self._cur = "pool"
                run(e, ops["pool"])

            @block.sync
            def _(e):
                self._cur = "sp"
                run(e, ops["sp"])


def _rw(reads, writes):
    rs, ws = [], []
    for v in reads:
        if isinstance(v, V) and v.res not in rs:
            rs.append(v.res)
    for v in writes:
        if v.res not in ws:
            ws.append(v.res)
    return rs, ws


def _a(x):
    return x.ap if isinstance(x, V) else x


class KB:
    def __init__(self, nc, S):
        self.nc = nc
        self.S = S

    def mm(self, out, lhsT, rhs, start=True, stop=True):
        rs, ws = _rw([lhsT, rhs], [out])
        self.S.op("pe", lambda e: e.matmul(out.ap, lhsT=lhsT.ap, rhs=rhs.ap, start=start, stop=stop), rs, ws)

    def tr(self, out, in_, ident):
        rs, ws = _rw([in_, ident], [out])
        self.S.op("pe", lambda e: e.transpose(out=out.ap, in_=in_.ap, identity=ident.ap), rs, ws)

    def act(self, out, in_, func, bias=None, scale=None, accum=None):
        rs, ws = _rw([in_, bias, scale], [out] + ([accum] if accum is not None else []))
        kw = {}
        if bias is not None:
            kw["bias"] = _a(bias)
        if scale is not None:
            kw["scale"] = _a(scale)
        if accum is not None:
            kw["accum_out"] = accum.ap
        self.S.op("act", lambda e: e.activation(out=out.ap, in_=in_.ap, func=func, **kw), rs, ws)

    def copy(self, eng, out, in_):
        rs, ws = _rw([in_], [out])
        if eng == "act":
            self.S.op("act", lambda e: e.activation(out=out.ap, in_=in_.ap, func=AF.Copy), rs, ws)
        else:
            self.S.op(eng, lambda e: e.tensor_copy(out=out.ap, in_=in_.ap), rs, ws)

    def tt(self, eng, out, a, b, op):
        rs, ws = _rw([a, b], [out])
        self.S.op(eng, lambda e: e.tensor_tensor(out=out.ap, in0=a.ap, in1=b.ap, op=op), rs, ws)

    def ts(self, eng, out, a, s1, s2=None, op0=ALU.mult, op1=None):
        rs, ws = _rw([a, s1, s2], [out])
        if op1 is None:
            self.S.op(eng, lambda e: e.tensor_scalar(out=out.ap, in0=a.ap, scalar1=_a(s1), scalar2=0.0, op0=op0, op1=ALU.add), rs, ws)
        else:
            self.S.op(eng, lambda e: e.tensor_scalar(out=out.ap, in0=a.ap, scalar1=_a(s1), scalar2=_a(s2), op0=op0, op1=op1), rs, ws)

    def stt(self, eng, out, a, scalar, b, op0, op1):
        rs, ws = _rw([a, scalar, b], [out])
        self.S.op(eng, lambda e: e.scalar_tensor_tensor(out=out.ap, in0=a.ap, scalar=_a(scalar), in1=b.ap, op0=op0, op1=op1), rs, ws)

    def red(self, eng, out, in_, op=ALU.add):
        rs, ws = _rw([in_], [out])
        self.S.op(eng, lambda e: e.tensor_reduce(out=out.ap, in_=in_.ap, axis=AX.X, op=op), rs, ws)

    def scan(self, out, d0, d1):
        rs, ws = _rw([d0, d1], [out])
        self.S.op("dve", lambda e: e.tensor_tensor_scan(out=out.ap, data0=d0.ap, data1=d1.ap, initial=0.0,
                                                        op0=ALU.mult, op1=ALU.add), rs, ws)

    def recip(self, out, in_):
        rs, ws = _rw([in_], [out])
        self.S.op("dve", lambda e: e.reciprocal(out=out.ap, in_=in_.ap), rs, ws)

    def memset(self, eng, out, val):
        rs, ws = _rw([], [out])
        self.S.op(eng, lambda e: e.memset(out.ap, val), rs, ws)

    def dma(self, q, out, in_):
        rs, ws = _rw([in_], [out])
        self.S.dma(q, lambda e: e.dma_start(out=out.ap, in_=in_.ap), rs, ws)


import os
_STOP = int(os.environ.get("KSTOP", "99"))


class _Stop(Exception):
    pass


_HALT = [False]


def _ck(k):
    if _STOP == k:
        _HALT[0] = True


def build_nc(NT, RANGE):
    SEQ = NT * TILE
    OWN0 = SEQ - RANGE - 2
    OC0 = OWN0 // CH
    NCHK = SEQ // CH
    NOC = NCHK - OC0
    HALO = OWN0 - OC0 * CH + 2
    NTOK2 = NOC * CH
    OT0 = OC0 // 4

    nc = bass.Bass("TRN2", target_bir_lowering=False)
    _HALT[0] = False
    dr = {}

    def din(name, shape):
        dr[name] = nc.dram_tensor(name, list(shape), F32, kind="ExternalInput").ap()
        return dr[name]

    xs = din("xs", [SEQ, D])
    win = din("win", [52, 128, KC * 128])
    wgu = din("wgu", [NFB * 2, 128, KC * 128])
    wdn = din("wdn", [NFB, 128, D])
    wout = din("wout", [128, KC * D])
    lora = din("lora", [128, 4 * RW])
    vecs_d = din("vecs", [128, NV])
    const_d = din("consts", [128, NCONST])
    bcv = {n: din(n, [1, w]) for n, w in (("g_pre_mix", D), ("g_post_mix", D), ("g_pre_ffn", D),
                                          ("g_post_ffn", D), ("ln_x_w", RW), ("ln_x_b", RW))}
    out_d = nc.dram_tensor("out", [NTOK2, D], F32, kind="ExternalOutput").ap()
    yT_d = nc.dram_tensor("yT_s", [KC, 128, NTOK2], BF16, kind="Internal").ap()
    n2T_d = nc.dram_tensor("n2T_s", [KC, 128, NTOK2], BF16, kind="Internal").ap()
    h_d = nc.dram_tensor("h_s", [NTOK2, D], F32, kind="Internal").ap()

    with contextlib.ExitStack() as top:
        S = Sched(nc, top)
        K = KB(nc, S)
        R = S.res
        xs_v = V(xs, R("xs"))
        win_v = V(win, R("win"))
        wgu_v = V(wgu, R("wgu"))
        wdn_v = V(wdn, R("wdn"))
        wout_v = V(wout, R("wout"))
        lora_v = V(lora, R("lora"))
        out_v = V(out_d, R("out"))
        yT_v = V(yT_d, R("yT"))
        n2T_v = V(n2T_d, R("n2T"))
        h_v = V(h_d, R("h"))

        banks = []
        for i in range(8):
            t = top.enter_context(nc.psum_tensor("pb%d" % i, [128, 512], F32))
            banks.append(V(t[:, :], R("pb%d" % i)))
        bank_i = [0]

        def psum():
            b = banks[bank_i[0] % 8]
            bank_i[0] += 1
            return b

        def sbt(stack, name, shape, dt):
            t = stack.enter_context(nc.sbuf_tensor("sb_" + name, list(shape), dt))
            return V(t[tuple(slice(None) for _ in shape)], R(name))

        vecs = sbt(top, "vecs", [128, NV], F32)
        vder = sbt(top, "vder", [128, 28 + 8], F32)
        cf = sbt(top, "cf", [128, NCONST], F32)
        cb = sbt(top, "cb", [128, NCONST], BF16)
        ones = sbt(top, "ones", [128, 128], F32)
        K.dma("sp", vecs, V(vecs_d, R("vecs_d")))
        K.dma("sp", cf, V(const_d, R("const_d")))
        K.dma("pool", cb, V(const_d, R("const_d2")))
        K.memset("dve", ones, 1.0)
        K.ts("dve", vder[:, 0:28], vecs[:, V_MU:V_MU + 28], -1.0, 1.0, ALU.mult, ALU.add)
        K.ts("dve", vder[:, 28:36], vecs[:, V_KA:V_KA + 8], -1.0, 1.0, ALU.mult, ALU.add)
        ident_b = cb[:, C_ID:C_ID + 128]
        mask4 = cb[:, C_M4:C_M4 + 512]
        maskL4 = cb[:, C_ML:C_ML + 512]
        blockones = cf[:, C_BO:C_BO + 128]
        headsel_b = cb[:, C_HS:C_HS + 2]
        hk = vecs[:, V_HK:V_HK + 1]

        def rms_rstd(src, rstd, junk, nbanks=None):
            pass

        try:
          _ck(1)
          with contextlib.ExitStack() as p1:
              gpre = sbt(p1, "gpre", [128, D], F32)
              lnw = sbt(p1, "lnw", [128, RW], F32)
              lnb = sbt(p1, "lnb", [128, RW], F32)
              K.dma("sp", gpre, V(bcv["g_pre_mix"].partition_broadcast(128), R("bc0")))
              K.dma("sp", lnw, V(bcv["ln_x_w"].partition_broadcast(128), R("bc1")))
              K.dma("sp", lnb, V(bcv["ln_x_b"].partition_broadcast(128), R("bc2")))
              lw_b = sbt(p1, "lw_b", [128, 4 * RW], BF16)
              K.dma("pool", lw_b, lora_v)
              wring = [sbt(p1, "wr%d" % i, [128, KC, 128], BF16) for i in range(4)]
              wr_i = [0]
              xin = sbt(p1, "xin", [128, D], F32)
              xn = sbt(p1, "xn", [128, D], BF16)
              st4 = sbt(p1, "st4", [128, 8], F32)
              nT = sbt(p1, "nT", [128, KC, TILE], BF16)
              carry = sbt(p1, "carry", [128, 28], F32)
              K.memset("dve", carry, 0.0)
              ccar = sbt(p1, "ccar", [128, 8, 2], F32)
              K.memset("dve", ccar, 0.0)
              tw = sbt(p1, "tw", [128, TILE], BF16)
              adb = sbt(p1, "adb", [128, TILE], BF16)
              sg2 = [sbt(p1, "sg%d" % i, [128, 2, TILE], BF16) for i in range(2)]
              ztmp = sbt(p1, "ztmp", [128, TILE], F32)
              zr = sbt(p1, "zr", [128, TILE], F32)
              zk = sbt(p1, "zk", [128, TILE], F32)
              zv = sbt(p1, "zv", [128, TILE], F32)
              fA = [sbt(p1, "fA%d" % i, [128, TILE], F32) for i in range(8)]
              AR2 = [sbt(p1, "AR%d" % i, [128, 2, 4, 256], BF16) for i in range(2)]
              BK2 = [sbt(p1, "BK%d" % i, [128, 2, 4, 256], BF16) for i in range(2)]
              TF2 = [sbt(p1, "TF%d" % i, [128, 2, 3, TILE], BF16) for i in range(2)]
              rkb2 = [sbt(p1, "rkb%d" % i, [128, 2, TILE], BF16) for i in range(2)]
              PC2 = [sbt(p1, "PCd%d" % i, [128, 2, 4], F32) for i in range(2)]
              TOK = [sbt(p1, "TOK%d" % i, [128, 2, 4, 128], BF16) for i in range(2)]
              SA = [sbt(p1, "SA%d" % i, [128, 4, 512], BF16) for i in range(2)]
              SN = [[sbt(p1, "SN%d_%d" % (i, j), [128, 4, 128], BF16) for j in range(2)] for i in range(2)]
              SL = [[sbt(p1, "SL%d_%d" % (i, j), [128, 4, 128], BF16) for j in range(2)] for i in range(2)]
              WW = [[sbt(p1, "WW%d_%d" % (i, j), [128, 4, 128], BF16) for j in range(2)] for i in range(2)]
              GT = [sbt(p1, "GT%d" % i, [128, 2, 64], BF16) for i in range(2)]
              RH = [sbt(p1, "RH%d" % i, [128, 2, 128], BF16) for i in range(2)]
              Tst = sbt(p1, "Tst", [128, 8, 64], F32)
              Tbf = [sbt(p1, "Tbf%d" % i, [128, 8, 64], BF16) for i in range(2)]
              K.memset("dve", Tst, 0.0)
              K.memset("dve", Tbf[0], 0.0)
              K.memset("dve", Tbf[1], 0.0)
              tb_i = [[0] * 8]
              ysb = sbt(p1, "ysb", [128, 256], F32)
              ysq = sbt(p1, "ysq", [128, 256], F32)
              yn = sbt(p1, "yn", [128, 256], F32)
              bon = sbt(p1, "bon", [128, 256], F32)
              gst = sbt(p1, "gst", [128, 24], F32)
              ytb = sbt(p1, "ytb", [128, 256], BF16)
              yTs = sbt(p1, "yTs", [128, 2, TILE], BF16)
              csb = sbt(p1, "csb", [128, TILE], F32)
              chb = sbt(p1, "chb", [128, TILE + 2], F32)
              ycv = sbt(p1, "ycv", [128, TILE], BF16)

              def load_w(cbi):
                  w = wring[wr_i[0] % 4]
                  wr_i[0] += 1
                  K.dma("pool", w.re("p k j -> p (k j)"), win_v[cbi])
                  return w

              def project(w, t_cols=TILE):
                  pz = psum()
                  for kc in range(KC):
                      K.mm(pz[:, 0:t_cols], w[:, kc, :], nT[:, kc, 0:t_cols], start=(kc == 0), stop=(kc == KC - 1))
                  return pz

              def shift(pz, col, out):
                  mu = vecs[:, V_MU + col:V_MU + col + 1]
                  omu = vder[:, col:col + 1]
                  K.act(ztmp, pz, AF.Copy, scale=omu)
                  K.stt("dve", out[:, 1:TILE], pz[:, 0:TILE - 1], mu, ztmp[:, 1:TILE], ALU.mult, ALU.add)
                  K.stt("dve", out[:, 0:1], carry[:, col:col + 1], mu, ztmp[:, 0:1], ALU.mult, ALU.add)
                  K.copy("act", carry[:, col:col + 1], pz[:, TILE - 1:TILE])

              def front(ti):
                  t0 = ti * TILE
                  sg = sg2[ti % 2]
                  for j in range(4):
                      K.dma("sp", xin, xs_v[t0 + j * CH:t0 + (j + 1) * CH, :])
                      K.memset("dve", st4[:, 0:1], 0.0)
                      K.act(xn, xin, AF.Square, accum=st4[:, 0:1])
                      K.ts("dve", st4[:, 1:2], st4[:, 0:1], 1.0 / D, RMS_EPS, ALU.mult, ALU.add)
                      K.act(st4[:, 2:3], st4[:, 1:2], AF.Sqrt)
                      K.recip(st4[:, 3:4], st4[:, 2:3])
                      K.stt("dve", xn, xin, st4[:, 3:4], gpre, ALU.mult, ALU.mult)
                      yield
                      for a in range(2):
                          pb = psum()
                          pbb = pb.bitcast(BF16)
                          for k8 in range(8):
                              kc = a * 8 + k8
                              K.tr(pbb[:, k8 * 128:(k8 + 1) * 128], xn[:, kc * 128:(kc + 1) * 128], ident_b)
                          K.copy("act" if a == 0 else "dve", nT[:, a * 8:(a + 1) * 8, j * CH:(j + 1) * CH],
                                 pbb.re("p (k t) -> p k t", k=8))
                          yield
                  pz = project(load_w(0))
                  shift(pz, 0, zr)
                  K.act(tw, zr, AF.Tanh)
                  yield
                  pz = project(load_w(1))
                  shift(pz, 1, zr)
                  K.copy("act", adb, zr)
                  yield
                  for gi in range(2):
                      pz = project(load_w(2 + gi))
                      shift(pz, 2 + gi, zr)
                      K.act(sg[:, gi, :], zr, AF.Sigmoid)
                      yield

              def prep(ti, hg):
                  out_tile = ti >= OT0
                  par = (ti * 4 + hg) % 2
                  AR, BK, TF, rkb, PC = AR2[par], BK2[par], TF2[par], rkb2[par], PC2[par]
                  for s in range(2):
                      hp = hg * 2 + s
                      for qi, dst in enumerate((zr, zk, zv)):
                          pz = project(load_w(4 + hp * 3 + qi))
                          shift(pz, 4 + hp * 3 + qi, dst)
                          yield
                      plw = psum()
                      K.mm(plw, lw_b[:, hp * 128:(hp + 1) * 128], tw)
                      pa = psum()
                      K.mm(pa, lw_b[:, RW + hp * 128:RW + (hp + 1) * 128], adb)
                      sw, aa, cs, kk, t1, t2, t3, t4 = fA
                      K.act(sw, plw, AF.Sigmoid, bias=vecs[:, V_W0 + hp:V_W0 + hp + 1])
                      K.act(aa, pa, AF.Sigmoid, bias=vecs[:, V_A0 + hp:V_A0 + hp + 1])
                      yield
                      for c in range(4):
                          K.scan(cs[:, c * CH:(c + 1) * CH], ones, sw[:, c * CH:(c + 1) * CH])
                      K.ts("dve", kk, zk, vecs[:, V_KK + hp:V_KK + hp + 1])
                      K.tt("dve", t1, kk, kk, ALU.mult)
                      yield
                      pss = psum()
                      K.mm(pss, blockones, t1)
                      K.act(t1, pss, AF.Sqrt)
                      K.ts("dve", t1, t1, 1e-12, None, ALU.max)
                      K.recip(t2, t1)
                      K.tt("dve", kk, kk, t2, ALU.mult)
                      yield
                      K.ts("dve", t1, aa, vecs[:, V_KA + hp:V_KA + hp + 1], vder[:, 28 + hp:29 + hp], ALU.mult, ALU.add)
                      K.tt("dve", t1, zk, t1, ALU.mult)
                      K.tt("dve", aa, kk, aa, ALU.mult)
                      yield
                      K.tt("dve", t2, cs, sw, ALU.subtract)
                      K.act(t2, t2, AF.Exp, scale=-DECAY_C)
                      K.stt("dve", AR[:, s, :, 0:128], kk.re("p (c t) -> p c t", c=4), -1.0,
                            t2.re("p (c t) -> p c t", c=4), ALU.mult, ALU.mult)
                      yield
                      K.act(t3, cs, AF.Exp, scale=-DECAY_C)
                      K.tt("dve", AR[:, s, :, 128:256], zr.re("p (c t) -> p c t", c=4),
                           t3.re("p (c t) -> p c t", c=4), ALU.mult)
                      yield
                      K.act(t3, cs, AF.Exp, scale=DECAY_C)
                      K.tt("dve", BK[:, s, :, 0:128], aa.re("p (c t) -> p c t", c=4),
                           t3.re("p (c t) -> p c t", c=4), ALU.mult)
                      K.tt("dve", BK[:, s, :, 128:256], t1.re("p (c t) -> p c t", c=4),
                           t3.re("p (c t) -> p c t", c=4), ALU.mult)
                      yield
                      K.ts("dve", st4[:, 4:8], cs.re("p (c t) -> p c t", c=4)[:, :, CH - 1], -DECAY_C)
                      K.act(PC[:, s, :], st4[:, 4:8], AF.Exp)
                      for c in range(4):
                          K.act(t4[:, c * CH:(c + 1) * CH], cs[:, c * CH:(c + 1) * CH], AF.Exp,
                                scale=DECAY_C, bias=st4[:, 4 + c:5 + c])
                      yield
                      K.tt("dve", TF[:, s, 1, :], aa, t4, ALU.mult)
                      K.tt("dve", TF[:, s, 2, :], t1, t4, ALU.mult)
                      K.copy("act", TF[:, s, 0, :], zv)
                      if out_tile:
                          K.tt("dve", t2, zr, t1, ALU.mult)
                          K.ts("dve", rkb[:, s, :], t2, vecs[:, V_RK + hp:V_RK + hp + 1])
                      yield

              def scan(ti, hg):
                  out_tile = ti >= OT0
                  par = (ti * 4 + hg) % 2
                  AR, BK, TF, rkb, PC = AR2[par], BK2[par], TF2[par], rkb2[par], PC2[par]
                  sg = sg2[ti % 2]
                  for cp in range(2):
                      cl = [2 * cp, 2 * cp + 1]
                      for u, c in enumerate(cl):
                          pb = psum()
                          pbb = pb.bitcast(BF16).re("p (s q t) -> p s q t", s=2, q=4)
                          for s in range(2):
                              for q in range(3):
                                  K.tr(pbb[:, s, q, :], TF[:, s, q, c * CH:(c + 1) * CH], ident_b)
                              K.tr(pbb[:, s, 3, :], AR[:, s, c, 0:128], ident_b)
                          K.copy("act", TOK[u], pbb)
                          yield
                      for u, c in enumerate(cl):
                          for hl in range(4):
                              s, e = hl // 2, hl % 2
                              ps_ = slice(64 * e, 64 * e + 64)
                              pa_ = psum()
                              K.mm(pa_[:, 0:256], BK[ps_, s, c, 0:128], AR[ps_, s, c, :])
                              K.mm(pa_[:, 256:512], BK[ps_, s, c, 128:256], AR[ps_, s, c, :])
                              K.tt("dve", SA[u][:, hl, :], pa_, mask4, ALU.mult)
                          yield
                          ple = [psum(), psum()]
                          for hl in range(4):
                              s, e = hl // 2, hl % 2
                              ps_ = slice(64 * e, 64 * e + 64)
                              K.mm(ple[e][:, s * 128:(s + 1) * 128], AR[ps_, s, c, 0:128], BK[ps_, s, c, 0:128])
                          for e in range(2):
                              K.tt("dve", SL[u][0].re("p (s e) t -> p s e t", e=2)[:, :, e, :],
                                   ple[e][:, 0:256].re("p (s t) -> p s t", s=2),
                                   maskL4[:, 0:256].re("p (s t) -> p s t", s=2), ALU.mult)
                          px = psum()
                          for hl in range(4):
                              s, e = hl // 2, hl % 2
                              K.mm(px[:, hl * 64:(hl + 1) * 64], SA[u][:, hl, 256:384], TOK[u][:, s, 0, 64 * e:64 * e + 64])
                          K.copy("act", WW[u][0][:, :, 64:128], px[:, 0:256].re("p (h v) -> p h v", h=4))
                          K.copy("act", WW[u][0].re("p (s e) k -> p s e k", s=2)[:, :, :, 0:64],
                                 TOK[u][:, :, 3, :].re("p s (e k) -> p s e k", e=2))
                          yield
                      for i in range(7):
                          for u, c in enumerate(cl):
                              n_cur = SA[u][:, :, 0:128] if i == 0 else SN[u][i % 2]
                              l_cur = SL[u][i % 2]
                              w_cur = WW[u][i % 2]
                              pw = psum()
                              for hl in range(4):
                                  K.mm(pw[:, hl * 128:(hl + 1) * 128], n_cur[:, hl, :], w_cur[:, hl, :])
                              K.tt("dve", WW[u][(i + 1) % 2].re("p h t -> p (h t)"), pw,
                                   w_cur.re("p h t -> p (h t)"), ALU.add)
                              if i < 6:
                                  pn = psum()
                                  for hl in range(4):
                                      K.mm(pn[:, hl * 128:(hl + 1) * 128], l_cur[:, hl, :], n_cur[:, hl, :])
                                  K.copy("act", SN[u][(i + 1) % 2].re("p h t -> p (h t)"), pn)
                              if i < 5:
                                  pl_ = psum()
                                  for hl in range(4):
                                      K.mm(pl_[:, hl * 128:(hl + 1) * 128], n_cur[:, hl, :], l_cur[:, hl, :])
                                  K.copy("dve" if (i + u) % 2 == 0 else "act", SL[u][(i + 1) % 2].re("p h t -> p (h t)"), pl_)
                              yield
                      for u, c in enumerate(cl):
                          Wf = WW[u][1]
                          chunk_g = ti * 4 + c
                          want_y = chunk_g >= OC0
                          pg = psum()
                          for hl in range(4):
                              s, e = hl // 2, hl % 2
                              K.mm(pg[64 * e:64 * e + 64, s * 64:(s + 1) * 64], Wf[:, hl, 0:64],
                                   TOK[u][:, s, 1, 64 * e:64 * e + 64])
                          K.copy("act", GT[u].re("p s k -> p (s k)"), pg[:, 0:128])
                          tcur = Tbf[tb_i[0][hg] % 2]
                          tnew = Tbf[(tb_i[0][hg] + 1) % 2]
                          if want_y:
                              pr = psum()
                              for hl in range(4):
                                  s, e = hl // 2, hl % 2
                                  K.mm(pr[64 * e:64 * e + 64, s * 128:(s + 1) * 128], Wf[:, hl, 0:64], SA[u][:, hl, 128:256])
                              K.tt("dve", RH[u], pr[:, 0:256].re("p (s t) -> p s t", s=2), AR[:, :, c, 128:256], ALU.add)
                              py = psum()
                              for hl in range(4):
                                  s, e = hl // 2, hl % 2
                                  ps_ = slice(64 * e, 64 * e + 64)
                                  o = py[:, hl * 64:(hl + 1) * 64]
                                  K.mm(o, SA[u][:, hl, 128:256], Wf[:, hl, 64:128], start=True, stop=False)
                                  K.mm(o, SA[u][:, hl, 384:512], TOK[u][:, s, 0, 64 * e:64 * e + 64], start=False, stop=False)
                                  K.mm(o, RH[u][ps_, s, :], tcur[ps_, hg * 2 + s, :], start=False, stop=True)
                          pt = psum()
                          for hl in range(4):
                              s, e = hl // 2, hl % 2
                              ps_ = slice(64 * e, 64 * e + 64)
                              o = pt[ps_, s * 64:(s + 1) * 64]
                              K.mm(o, TOK[u][:, s, 1, 64 * e:64 * e + 64], Wf[:, hl, 64:128], start=True, stop=False)
                              K.mm(o, TOK[u][:, s, 2, 64 * e:64 * e + 64], TOK[u][:, s, 0, 64 * e:64 * e + 64], start=False, stop=False)
                              K.mm(o, GT[u][ps_, s, :], tcur[ps_, hg * 2 + s, :], start=False, stop=True)
                          for s in range(2):
                              K.stt("dve", Tst[:, hg * 2 + s, :], Tst[:, hg * 2 + s, :], PC[:, s, c:c + 1],
                                    pt[:, s * 64:(s + 1) * 64], ALU.mult, ALU.add)
                          K.copy("act", tnew[:, hg * 2:hg * 2 + 2, :], Tst[:, hg * 2:hg * 2 + 2, :])
                          tb_i[0][hg] += 1
                          if want_y:
                              K.copy("act", ysb, py[:, 0:256])
                          yield
                          if not want_y:
                              continue
                          pgt = psum()
                          K.mm(pgt[:, 0:256], sg[:, 0, c * CH:(c + 1) * CH], lw_b[:, 2 * RW + hg * 256:2 * RW + (hg + 1) * 256],
                               start=True, stop=False)
                          K.mm(pgt[:, 0:256], sg[:, 1, c * CH:(c + 1) * CH], lw_b[:, 3 * RW + hg * 256:3 * RW + (hg + 1) * 256],
                               start=False, stop=True)
                          for s in range(2):
                              K.mm(pgt[:, 256 + 2 * s:258 + 2 * s], rkb[:, s, c * CH:(c + 1) * CH], headsel_b)
                          K.copy("act", gst[:, 16:20], pgt[:, 256:260])
                          y3 = ysb.re("p (h v) -> p h v", h=4)
                          K.red("dve", gst[:, 0:4], y3)
                          K.tt("dve", ysq, ysb, ysb, ALU.mult)
                          K.red("dve", gst[:, 4:8], ysq.re("p (h v) -> p h v", h=4))
                          K.ts("dve", gst[:, 0:4], gst[:, 0:4], 1.0 / 64)
                          K.tt("dve", gst[:, 8:12], gst[:, 0:4], gst[:, 0:4], ALU.mult)
                          K.stt("dve", gst[:, 4:8], gst[:, 4:8], 1.0 / 64, gst[:, 8:12], ALU.mult, ALU.subtract)
                          K.ts("dve", gst[:, 4:8], gst[:, 4:8], GN_EPS, None, ALU.add)
                          K.act(gst[:, 8:12], gst[:, 4:8], AF.Sqrt)
                          K.recip(gst[:, 12:16], gst[:, 8:12])
                          yn3 = yn.re("p (h v) -> p h v", h=4)
                          K.tt("dve", yn3, y3, gst[:, 0:4].bc([128, 4, 64], 2), ALU.subtract)
                          K.tt("dve", yn3, yn3, gst[:, 12:16].bc([128, 4, 64], 2), ALU.mult)
                          K.tt("dve", yn, yn, lnw[:, hg * 256:(hg + 1) * 256], ALU.mult)
                          K.tt("dve", yn, yn, lnb[:, hg * 256:(hg + 1) * 256], ALU.add)
                          K.tt("dve", bon.re("p (s e v) -> p s e v", s=2, e=2),
                               TOK[u][:, :, 0, :].re("p s (e v) -> p s e v", e=2),
                               gst[:, 16:20].re("p (s e) -> p s e", s=2).bc([128, 2, 2, 64], 3), ALU.mult)
                          K.tt("dve", yn, yn, bon, ALU.add)
                          K.tt("dve", ytb, yn, pgt[:, 0:256], ALU.mult)
                          pb = psum()
                          pbb = pb.bitcast(BF16)
                          for s in range(2):
                              K.tr(pbb[:, s * 128:(s + 1) * 128], ytb[:, s * 128:(s + 1) * 128], ident_b)
                          K.copy("act", yTs[:, :, c * CH:(c + 1) * CH], pbb[:, 0:256].re("p (s t) -> p s t", s=2))
                          yield
                  if out_tile:
                      c_lo = max(0, OC0 - ti * 4)
                      tk0 = (ti * 4 + c_lo - OC0) * CH
                      ncol = (4 - c_lo) * CH
                      for s in range(2):
                          K.dma("sp", yT_v[8 + hg * 2 + s, :, tk0:tk0 + ncol], yTs[:, s, c_lo * CH:4 * CH])
                  yield

              def conv(ti):
                  c_lo = max(0, OC0 - ti * 4)
                  tk0 = (ti * 4 + c_lo - OC0) * CH
                  ncol = (4 - c_lo) * CH
                  for i in range(8):
                      pB = project(load_w(28 + i * 3 + 0))
                      pC = project(load_w(28 + i * 3 + 1))
                      pH = project(load_w(28 + i * 3 + 2))
                      K.copy("act", csb, pC)
                      K.copy("act", chb[:, 0:2], ccar[:, i, :])
                      K.tt("dve", chb[:, 2:TILE + 2], csb, pH, ALU.mult)
                      K.copy("act", ccar[:, i, :], chb[:, TILE:TILE + 2])
                      K.ts("dve", csb, chb[:, 0:TILE], vecs[:, V_CW + i:V_CW + i + 1])
                      K.stt("dve", csb, chb[:, 1:TILE + 1], vecs[:, V_CW + 8 + i:V_CW + 9 + i], csb, ALU.mult, ALU.add)
                      K.stt("dve", csb, chb[:, 2:TILE + 2], vecs[:, V_CW + 16 + i:V_CW + 17 + i], csb, ALU.mult, ALU.add)
                      K.tt("dve", ycv, csb, pB, ALU.mult)
                      K.dma("sp", yT_v[i, :, tk0:tk0 + ncol], ycv[:, c_lo * CH:4 * CH])
                      yield

              def chain(*gens):
                  for g in gens:
                      yield from g

              def interleave(ga, gb):
                  la, lb = True, True
                  while la or lb:
                      if la:
                          try:
                              next(ga)
                          except StopIteration:
                              la = False
                      if lb:
                          try:
                              next(gb)
                          except StopIteration:
                              lb = False

              PIPE = os.environ.get("KPIPE", "1") == "1"
              units = [(ti, hg) for ti in range(NT) for hg in range(4)]

              def p_stream(ti, hg):
                  gens = []
                  if hg == 0:
                      gens.append(front(ti))
                  gens.append(prep(ti, hg))
                  return gens

              for g in chain(*p_stream(0, 0)):
                  pass
              for ui, (ti, hg) in enumerate(units):
                  nxt = []
                  if hg == 3 and ti >= OT0:
                      nxt.append(conv(ti))
                  if ui + 1 < len(units):
                      nxt += p_stream(*units[ui + 1])
                  if PIPE:
                      interleave(scan(ti, hg), chain(*nxt))
                  else:
                      for g in scan(ti, hg):
                          pass
                      for g in chain(*nxt):
                          pass
          _ck(8)
          S.barrier()

          with contextlib.ExitStack() as p15:
              wo_b = sbt(p15, "wo_b", [128, KC, D], BF16)
              K.dma("pool", wo_b.re("p k d -> p (k d)"), wout_v)
              gpm = sbt(p15, "gpm", [128, D], F32)
              gpf = sbt(p15, "gpf", [128, D], F32)
              K.dma("sp", gpm, V(bcv["g_post_mix"].partition_broadcast(128), R("bc3")))
              K.dma("sp", gpf, V(bcv["g_pre_ffn"].partition_broadcast(128), R("bc4")))
              ych = [sbt(p15, "ych%d" % i, [128, KC, CH], BF16) for i in range(2)]
              xin2 = [sbt(p15, "xin2_%d" % i, [128, D], F32) for i in range(2)]
              hsb = [sbt(p15, "hsb%d" % i, [128, D], F32) for i in range(2)]
              n2 = sbt(p15, "n2", [128, D], BF16)
              n2T = [sbt(p15, "n2T%d" % i, [128, KC, CH], BF16) for i in range(2)]
              st5 = sbt(p15, "st5", [128, 16], F32)
              for oc in range(NOC):
                  yc = ych[oc % 2]
                  xi = xin2[oc % 2]
                  hs = hsb[oc % 2]
                  nt2 = n2T[oc % 2]
                  K.dma("sp", yc, yT_v[:, :, oc * CH:(oc + 1) * CH].re("c p t -> p c t"))
                  K.dma("sp", xi, xs_v[(OC0 + oc) * CH:(OC0 + oc + 1) * CH, :])
                  pm = [psum() for _ in range(4)]
                  for blk in range(4):
                      for chc in range(KC):
                          K.mm(pm[blk], yc[:, chc, :], wo_b[:, chc, blk * 512:(blk + 1) * 512],
                               start=(chc == 0), stop=(chc == KC - 1))
                  K.memset("dve", st5[:, 0:4], 0.0)
                  for blk in range(4):
                      K.act(n2[:, blk * 512:(blk + 1) * 512], pm[blk], AF.Square, accum=st5[:, blk:blk + 1])
                  K.red("dve", st5[:, 4:5], st5[:, 0:4])
                  K.ts("dve", st5[:, 4:5], st5[:, 4:5], 1.0 / D, RMS_EPS, ALU.mult, ALU.add)
                  K.act(st5[:, 5:6], st5[:, 4:5], AF.Sqrt)
                  K.recip(st5[:, 6:7], st5[:, 5:6])
                  for blk in range(4):
                      sl = slice(blk * 512, (blk + 1) * 512)
                      K.stt("dve", hs[:, sl], pm[blk], st5[:, 6:7], gpm[:, sl], ALU.mult, ALU.mult)
                  K.tt("dve", hs, hs, xi, ALU.add)
                  K.dma("sp", h_v[oc * CH:(oc + 1) * CH, :], hs)
                  K.memset("dve", st5[:, 8:9], 0.0)
                  K.act(n2, hs, AF.Square, accum=st5[:, 8:9])
                  K.ts("dve", st5[:, 9:10], st5[:, 8:9], 1.0 / D, RMS_EPS, ALU.mult, ALU.add)
                  K.act(st5[:, 10:11], st5[:, 9:10], AF.Sqrt)
                  K.recip(st5[:, 11:12], st5[:, 10:11])
                  K.stt("dve", n2, hs, st5[:, 11:12], gpf, ALU.mult, ALU.mult)
                  for a in range(2):
                      pb = psum()
                      pbb = pb.bitcast(BF16)
                      for k8 in range(8):
                          kc = a * 8 + k8
                          K.tr(pbb[:, k8 * 128:(k8 + 1) * 128], n2[:, kc * 128:(kc + 1) * 128], ident_b)
                      K.copy("act", nt2[:, a * 8:(a + 1) * 8, :], pbb.re("p (k t) -> p k t", k=8))
                  if oc == 0:
                      K.ts("dve", nt2[:, :, 0:HALO], nt2[:, :, 0:HALO], hk)
                  K.dma("sp", n2T_v[:, :, oc * CH:(oc + 1) * CH].re("c p t -> p c t"), nt2)
          S.barrier()

          with contextlib.ExitStack() as p2:
              gpo = sbt(p2, "gpo", [128, D], F32)
              K.dma("sp", gpo, V(bcv["g_post_ffn"].partition_broadcast(128), R("bc5")))
              facc = sbt(p2, "facc", [128, STC, D], F32)
              n2s = sbt(p2, "n2s", [128, KC, STC * CH], BF16)
              wg_r = [sbt(p2, "wg%d" % i, [128, 2, KC, 128], BF16) for i in range(2)]
              wd_r = [sbt(p2, "wd%d" % i, [128, D], BF16) for i in range(2)]
              gbuf = sbt(p2, "gbuf", [128, STC * CH + 2], F32)
              cvt = sbt(p2, "cvt", [128, TILE], F32)
              slu = sbt(p2, "slu", [128, TILE], F32)
              actT = [sbt(p2, "actT%d" % i, [128, STC * CH], BF16) for i in range(2)]
              hld = sbt(p2, "hld", [128, D], F32)
              osb = sbt(p2, "osb", [128, D], F32)
              st6 = sbt(p2, "st6", [128, 8], F32)
              K.memset("dve", gbuf[:, 0:2], 0.0)
              st_starts = []
              s0 = 0
              while True:
                  st_starts.append(s0)
                  if s0 + STC >= NOC:
                      break
                  s0 += STC - 1
              for sti, cs0 in enumerate(st_starts):
                  nch = min(STC, NOC - cs0)
                  ntk = nch * CH
                  first_out = 0 if sti == 0 else 1
                  K.dma("sp", n2s[:, :, 0:ntk], n2T_v[:, :, cs0 * CH:cs0 * CH + ntk].re("c p t -> p c t"))
                  ttiles = [(a, min(TILE, ntk - a)) for a in range(0, ntk, TILE)]
                  for fb in range(NFB):
                      wgb = wg_r[fb % 2]
                      wdb = wd_r[fb % 2]
                      at = actT[fb % 2]
                      K.dma("pool", wgb.re("p g k j -> p g (k j)"), wgu_v[2 * fb:2 * fb + 2].re("g p f -> p g f"))
                      K.dma("pool", wdb, wdn_v[fb])
                      for (a0, tn) in ttiles:
                          pgg = psum()
                          pu = psum()
                          for kc in range(KC):
                              K.mm(pgg[:, 0:tn], wgb[:, 0, kc, :], n2s[:, kc, a0:a0 + tn], start=(kc == 0), stop=(kc == KC - 1))
                          for kc in range(KC):
                              K.mm(pu[:, 0:tn], wgb[:, 1, kc, :], n2s[:, kc, a0:a0 + tn], start=(kc == 0), stop=(kc == KC - 1))
                          K.copy("act", gbuf[:, 2 + a0:2 + a0 + tn], pgg[:, 0:tn])
                          K.ts("dve", cvt[:, 0:tn], gbuf[:, a0:a0 + tn], vecs[:, V_CF + fb:V_CF + fb + 1])
                          K.stt("dve", cvt[:, 0:tn], gbuf[:, a0 + 1:a0 + 1 + tn], vecs[:, V_CF + NFB + fb:V_CF + NFB + fb + 1],
                                cvt[:, 0:tn], ALU.mult, ALU.add)
                          K.stt("dve", cvt[:, 0:tn], gbuf[:, a0 + 2:a0 + 2 + tn], vecs[:, V_CF + 2 * NFB + fb:V_CF + 2 * NFB + fb + 1],
                                cvt[:, 0:tn], ALU.mult, ALU.add)
                          K.act(slu[:, 0:tn], cvt[:, 0:tn], AF.Silu)
                          K.tt("dve", at[:, a0:a0 + tn], slu[:, 0:tn], pu[:, 0:tn], ALU.mult)
                      for ci in range(nch):
                          for blk in range(4):
                              pf = psum()
                              K.mm(pf, at[:, ci * CH:(ci + 1) * CH], wdb[:, blk * 512:(blk + 1) * 512])
                              sl = slice(blk * 512, (blk + 1) * 512)
                              if fb == 0:
                                  K.copy("act", facc[:, ci, sl], pf)
                              else:
                                  K.tt("dve", facc[:, ci, sl], facc[:, ci, sl], pf, ALU.add)
                  for ci in range(first_out, nch):
                      oc = cs0 + ci
                      K.dma("sp", hld, h_v[oc * CH:(oc + 1) * CH, :])
                      K.memset("dve", st6[:, 0:1], 0.0)
                      K.act(osb, facc[:, ci, :], AF.Square, accum=st6[:, 0:1])
                      K.ts("dve", st6[:, 1:2], st6[:, 0:1], 1.0 / D, RMS_EPS, ALU.mult, ALU.add)
                      K.act(st6[:, 2:3], st6[:, 1:2], AF.Sqrt)
                      K.recip(st6[:, 3:4], st6[:, 2:3])
                      K.stt("dve", osb, facc[:, ci, :], st6[:, 3:4], gpo, ALU.mult, ALU.mult)
                      K.tt("dve", osb, osb, hld, ALU.add)
                      K.dma("sp", out_v[oc * CH:(oc + 1) * CH, :], osb)
          S.barrier()
        except _Stop:
            pass
        S.barrier()
        S.emit()
    return nc


def _blk(w):
    return np.ascontiguousarray(w.reshape(KC, 128, 128).transpose(1, 0, 2)).reshape(128, KC * 128)


def _pad_cols(w, n):
    out = np.zeros((w.shape[0], n), w.dtype)
    out[:, :w.shape[1]] = w
    return out


def _consts():
    c = np.zeros((128, NCONST), np.float32)
    j = np.arange(128)[:, None]
    t = np.arange(128)[None, :]
    c[:, C_ID:C_ID + 128] = np.eye(128)
    strict = (t > j).astype(np.float32)
    incl = (t >= j).astype(np.float32)
    c[:, C_M4:C_M4 + 512] = np.concatenate([strict, incl, strict, incl], 1)
    low = (j > t).astype(np.float32)
    c[:, C_ML:C_ML + 512] = np.concatenate([low] * 4, 1)
    c[:, C_BO:C_BO + 128] = (j // 64 == t // 64).astype(np.float32)
    c[:, C_HS:C_HS + 2] = (j // 64 == np.arange(2)[None, :]).astype(np.float32)
    return c


def prepare(inputs, NT, RANGE):
    f = lambda k: np.asarray(inputs[k], np.float32)
    w_in = f("w_in")
    RB = 3072
    cols = []
    cols.append(_pad_cols(w_in[:, RB + 3072:RB + 3168], 128))
    cols.append(_pad_cols(w_in[:, RB + 3168:RB + 3264], 128))
    cols.append(w_in[:, RB + 3264:RB + 3392])
    cols.append(w_in[:, RB + 3392:RB + 3520])
    for i in range(8):
        for base in (0, 1024, 2048):
            cols.append(w_in[:, RB + base + i * 128:RB + base + (i + 1) * 128])
    for i in range(8):
        for base in (0, 1024, 2048):
            cols.append(w_in[:, base + i * 128:base + (i + 1) * 128])
    win = np.stack([_blk(c) for c in cols], 0)
    wgu_full = f("w_gate_up")
    wgu = np.stack([_blk(wgu_full[:, g * DFF + fb * 128:g * DFF + (fb + 1) * 128]) for fb in range(NFB) for g in range(2)], 0)
    wdn = np.ascontiguousarray(f("w_down").reshape(NFB, 128, D))
    wout = np.ascontiguousarray(f("w_out").reshape(KC, 128, D).transpose(1, 0, 2)).reshape(128, KC * D)
    lora = np.zeros((128, 4 * RW), np.float32)
    lora[:96, 0:RW] = f("w_decay_up")
    lora[:96, RW:2 * RW] = f("a_up")
    gup = f("g_up")
    lora[:, 2 * RW:3 * RW] = gup[0:128]
    lora[:, 3 * RW:4 * RW] = gup[128:256]
    mu = f("mu_rwkv")
    vecs = np.zeros((128, NV), np.float32)
    vecs[:96, V_MU + 0] = mu[3072:3168]
    vecs[:96, V_MU + 1] = mu[3168:3264]
    vecs[:, V_MU + 2] = mu[3264:3392]
    vecs[:, V_MU + 3] = mu[3392:3520]
    for i in range(8):
        for qi, base in enumerate((0, 1024, 2048)):
            vecs[:, V_MU + 4 + i * 3 + qi] = mu[base + i * 128:base + (i + 1) * 128]
    for name, col in (("w0", V_W0), ("a0", V_A0), ("k_k", V_KK), ("k_a", V_KA)):
        vecs[:, col:col + 8] = f(name).reshape(8, 128).T
    vecs[:, V_RK:V_RK + 8] = f("r_k").reshape(8, 128).T
    cw = f("conv_w")
    for tap in range(3):
        vecs[:, V_CW + tap * 8:V_CW + tap * 8 + 8] = cw[tap].reshape(8, 128).T
    cfw = f("conv_ffn")
    for tap in range(3):
        vecs[:, V_CF + tap * NFB:V_CF + (tap + 1) * NFB] = cfw[tap].reshape(NFB, 128).T
    shared = {"win": win, "wgu": wgu, "wdn": wdn, "wout": wout, "lora": lora, "consts": _consts()}
    for n in ("g_pre_mix", "g_post_mix", "g_pre_ffn", "g_post_ffn", "ln_x_w", "ln_x_b"):
        shared[n] = f(n).reshape(1, -1)
    x = f("x")
    meta = f("meta_tokens")
    SEQ = NT * TILE
    in_maps = []
    for c in range(8):
        b, q = c // 4, c % 4
        seq = np.concatenate([meta, x[b]], 0)
        n_real = (q + 1) * RANGE
        xs = np.zeros((SEQ, D), np.float32)
        xs[SEQ - n_real:] = seq[:n_real]
        v = vecs.copy()
        v[:, V_HK] = 0.0 if q == 0 else 1.0
        m = dict(shared)
        m["xs"] = xs
        m["vecs"] = v
        in_maps.append(m)
    return in_maps


def kernel(**inputs):
    x = np.asarray(inputs["x"])
    B, L, _ = x.shape
    tot = N_META + L
    RANGE = tot // 4
    NT = -(-(tot + 2) // TILE)
    nc = build_nc(NT, RANGE)
    in_maps = prepare(inputs, NT, RANGE)
    dbg = os.environ.get("KCORES")
    if dbg is not None:
        sel = [int(t) for t in dbg.split(",")]
        res1 = run_bass_kernel_spmd(nc, [in_maps[c] for c in sel], core_ids=list(range(len(sel))))
        return {c: {k: np.asarray(v) for k, v in res1.results[i].items()} for i, c in enumerate(sel)}
    res = run_bass_kernel_spmd(nc, in_maps, core_ids=list(range(8)))
    SEQ = NT * TILE
    OC0 = (SEQ - RANGE - 2) // CH
    off = SEQ - RANGE - OC0 * CH
    full = np.zeros((B, tot, D), np.float32)
    for c in range(8):
        b, q = c // 4, c % 4
        o = np.asarray(res.results[c]["out"])
        full[b, q * RANGE:(q + 1) * RANGE] = o[off:off + RANGE]
    return np.ascontiguousarray(full[:, N_META:])
```

```python
import contextlib
import math
import numpy as np
import concourse.bass as bass
import concourse.mybir as mybir
from concourse.bass_utils import run_bass_kernel_spmd

F32 = mybir.dt.float32
BF16 = mybir.dt.bfloat16
ALU = mybir.AluOpType
AF = mybir.ActivationFunctionType
AX = mybir.AxisListType

D = 2048
KC = 16
TILE = 512
CH = 128
RW = 1024
DFF = 5632
NFB = DFF // 128
N_META = 16
RMS_EPS = 1e-6
GN_EPS = 64e-5
DECAY_C = math.exp(-0.5)
STC = 9

V_MU = 0
V_W0 = 28
V_A0 = 36
V_KK = 44
V_KA = 52
V_RK = 60
V_CW = 68
V_CF = 92
V_HK = 92 + 3 * NFB
NV = V_HK + 1
C_ID = 0
C_M4 = 128
C_ML = 640
C_BO = 1152
C_HS = 1280
NCONST = 1282


class Res:
    __slots__ = ("name", "w", "r", "dsem", "dcnt")

    def __init__(self, name):
        self.name = name
        self.w = None
        self.r = {}
        self.dsem = None
        self.dcnt = 0


class V:
    __slots__ = ("ap", "res")

    def __init__(self, ap, res):
        self.ap = ap
        self.res = res

    def __getitem__(self, idx):
        return V(self.ap[idx], self.res)

    def re(self, pat, **kw):
        return V(self.ap.rearrange(pat, **kw), self.res)

    def bitcast(self, dt):
        return V(self.ap.bitcast(dt), self.res)

    def bc(self, shape, axis):
        return V(self.ap.unsqueeze(axis).to_broadcast(list(shape)), self.res)


class Sched:
    ENG = ("pe", "act", "dve", "pool", "sp")

    def __init__(self, nc, stack):
        self.nc = nc
        self.stack = stack
        self.ops = {e: [] for e in self.ENG}
        self.sem = {e: stack.enter_context(nc.semaphore("s_" + e)) for e in self.ENG}
        self.cnt = {e: 0 for e in self.ENG}
        self.seen = {e: {} for e in self.ENG}
        self.allres = []

    def res(self, name):
        r = Res(name)
        self.allres.append(r)
        return r

    def _dsem(self, res):
        if res.dsem is None:
            res.dsem = self.stack.enter_context(self.nc.semaphore("d_" + res.name))
        return res.dsem

    def _deps(self, eng, reads, writes, seen_key=None):
        deps = {}

        def add(t, kind):
            if t is None:
                return
            key, val, en = t
            if en == eng and (kind != "raw" or eng == "pe"):
                return
            if deps.get(key, 0) < val:
                deps[key] = val

        for r in reads:
            add(r.w, "raw")
        for w in writes:
            add(w.w, "waw")
            for key, (val, en) in w.r.items():
                add((key, val, en), "war")
        out = []
        seen = self.seen[seen_key or eng]
        for key, val in deps.items():
            if seen.get(key, 0) >= val:
                continue
            seen[key] = val
            out.append((key, val))
        return out

    def op(self, eng, fn, reads=(), writes=()):
        if _HALT[0]:
            return
        waits = self._deps(eng, reads, writes)
        self.cnt[eng] += 1
        c = self.cnt[eng]
        key = ("c", eng)
        self.ops[eng].append((waits, fn, ("c", c)))
        for r in reads:
            r.r[key] = (c, eng)
        for w in writes:
            w.w = (key, c, eng)
            w.r = {}

    def dma(self, q, fn, reads=(), writes=()):
        if _HALT[0]:
            return
        waits = self._deps("dma", reads, writes, seen_key=q)
        anchor = writes[0] if writes else reads[0]
        sem = self._dsem(anchor)
        anchor.dcnt += 16
        c = anchor.dcnt
        key = ("d", sem)
        self.ops[q].append((waits, fn, ("d", sem)))
        for r in reads:
            r.r[key] = (c, "dma")
        for w in writes:
            w.w = (key, c, "dma")
            w.r = {}

    def barrier(self):
        for e in self.ENG:
            waits = self._deps(e + "_bar", self.allres, self.allres, seen_key=e)
            if waits:
                self.ops[e].append((waits, None, None))

    def emit(self):
        nc = self.nc
        ops = self.ops
        needed = {e: set() for e in self.ENG}
        for e in self.ENG:
            for waits, fn, inc in ops[e]:
                for key, val in waits:
                    if key[0] == "c":
                        needed[key[1]].add(val)
        rank = {e: {v: i + 1 for i, v in enumerate(sorted(needed[e]))} for e in self.ENG}
        sem = self.sem

        def run(engh, lst):
            for waits, fn, inc in lst:
                for key, val in waits:
                    if key[0] == "c":
                        engh.wait_ge(sem[key[1]], rank[key[1]][val])
                    else:
                        engh.wait_ge(key[1], val)
                if fn is not None:
                    ins = fn(engh)
                    if inc[0] == "d":
                        ins.then_inc(inc[1], 16)
                    elif inc[1] in rank[self._cur]:
                        ins.then_inc(sem[self._cur], 1)

        with nc.Block() as block:
            @block.tensor
            def _(e):
                self._cur = "pe"
                run(e, ops["pe"])

            @block.scalar
            def _(e):
                self._cur = "act"
                run(e, ops["act"])

            @block.vector
            def _(e):
                self._cur = "dve"
                run(e, ops["dve"])

            @block.gpsimd
            def _(e):
                self._cur = "pool"
                run(e, ops["pool"])

            @block.sync
            def _(e):
                self._cur = "sp"
                run(e, ops["sp"])


def _rw(reads, writes):
    rs, ws = [], []
    for v in reads:
        if isinstance(v, V) and v.res not in rs:
            rs.append(v.res)
    for v in writes:
        if v.res not in ws:
            ws.append(v.res)
    return rs, ws


def _a(x):
    return x.ap if isinstance(x, V) else x


class KB:
    def __init__(self, nc, S):
        self.nc = nc
        self.S = S

    def mm(self, out, lhsT, rhs, start=True, stop=True):
        rs, ws = _rw([lhsT, rhs], [out])
        self.S.op("pe", lambda e: e.matmul(out.ap, lhsT=lhsT.ap, rhs=rhs.ap, start=start, stop=stop), rs, ws)

    def tr(self, out, in_, ident):
        rs, ws = _rw([in_, ident], [out])
        self.S.op("pe", lambda e: e.transpose(out=out.ap, in_=in_.ap, identity=ident.ap), rs, ws)

    def act(self, out, in_, func, bias=None, scale=None, accum=None):
        rs, ws = _rw([in_, bias, scale], [out] + ([accum] if accum is not None else []))
        kw = {}
        if bias is not None:
            kw["bias"] = _a(bias)
        if scale is not None:
            kw["scale"] = _a(scale)
        if accum is not None:
            kw["accum_out"] = accum.ap
        self.S.op("act", lambda e: e.activation(out=out.ap, in_=in_.ap, func=func, **kw), rs, ws)

    def copy(self, eng, out, in_):
        rs, ws = _rw([in_], [out])
        if eng == "act":
            self.S.op("act", lambda e: e.activation(out=out.ap, in_=in_.ap, func=AF.Copy), rs, ws)
        else:
            self.S.op(eng, lambda e: e.tensor_copy(out=out.ap, in_=in_.ap), rs, ws)

    def tt(self, eng, out, a, b, op):
        rs, ws = _rw([a, b], [out])
        self.S.op(eng, lambda e: e.tensor_tensor(out=out.ap, in0=a.ap, in1=b.ap, op=op), rs, ws)

    def ts(self, eng, out, a, s1, s2=None, op0=ALU.mult, op1=None):
        rs, ws = _rw([a, s1, s2], [out])
        if op1 is None:
            self.S.op(eng, lambda e: e.tensor_scalar(out=out.ap, in0=a.ap, scalar1=_a(s1), scalar2=0.0, op0=op0, op1=ALU.add), rs, ws)
        else:
            self.S.op(eng, lambda e: e.tensor_scalar(out=out.ap, in0=a.ap, scalar1=_a(s1), scalar2=_a(s2), op0=op0, op1=op1), rs, ws)

    def stt(self, eng, out, a, scalar, b, op0, op1):
        rs, ws = _rw([a, scalar, b], [out])
        self.S.op(eng, lambda e: e.scalar_tensor_tensor(out=out.ap, in0=a.ap, scalar=_a(scalar), in1=b.ap, op0=op0, op1=op1), rs, ws)

    def red(self, eng, out, in_, op=ALU.add):
        rs, ws = _rw([in_], [out])
        self.S.op(eng, lambda e: e.tensor_reduce(out=out.ap, in_=in_.ap, axis=AX.X, op=op), rs, ws)

    def scan(self, out, d0, d1):
        rs, ws = _rw([d0, d1], [out])
        self.S.op("dve", lambda e: e.tensor_tensor_scan(out=out.ap, data0=d0.ap, data1=d1.ap, initial=0.0,
                                                        op0=ALU.mult, op1=ALU.add), rs, ws)

    def recip(self, out, in_):
        rs, ws = _rw([in_], [out])
        self.S.op("dve", lambda e: e.reciprocal(out=out.ap, in_=in_.ap), rs, ws)

    def memset(self, eng, out, val):
        rs, ws = _rw([], [out])
        self.S.op(eng, lambda e: e.memset(out.ap, val), rs, ws)

    def dma(self, q, out, in_):
        rs, ws = _rw([in_], [out])
        self.S.dma(q, lambda e: e.dma_start(out=out.ap, in_=in_.ap), rs, ws)


import os
_STOP = int(os.environ.get("KSTOP", "99"))


class _Stop(Exception):
    pass


_HALT = [False]


def _ck(k):
    if _STOP == k:
        _HALT[0] = True


def build_nc(NT, RANGE):
    SEQ = NT * TILE
    OWN0 = SEQ - RANGE - 2
    OC0 = OWN0 // CH
    NCHK = SEQ // CH
    NOC = NCHK - OC0
    HALO = OWN0 - OC0 * CH + 2
    NTOK2 = NOC * CH
    OT0 = OC0 // 4

    nc = bass.Bass("TRN2", target_bir_lowering=False)
    _HALT[0] = False
    dr = {}

    def din(name, shape):
        dr[name] = nc.dram_tensor(name, list(shape), F32, kind="ExternalInput").ap()
        return dr[name]

    xs = din("xs", [SEQ, D])
    win = din("win", [52, 128, KC * 128])
    wgu = din("wgu", [NFB * 2, 128, KC * 128])
    wdn = din("wdn", [NFB, 128, D])
    wout = din("wout", [128, KC * D])
    lora = din("lora", [128, 4 * RW])
    vecs_d = din("vecs", [128, NV])
    const_d = din("consts", [128, NCONST])
    bcv = {n: din(n, [1, w]) for n, w in (("g_pre_mix", D), ("g_post_mix", D), ("g_pre_ffn", D),
                                          ("g_post_ffn", D), ("ln_x_w", RW), ("ln_x_b", RW))}
    out_d = nc.dram_tensor("out", [NTOK2, D], F32, kind="ExternalOutput").ap()
    yT_d = nc.dram_tensor("yT_s", [KC, 128, NTOK2], BF16, kind="Internal").ap()
    n2T_d = nc.dram_tensor("n2T_s", [KC, 128, NTOK2], BF16, kind="Internal").ap()
    h_d = nc.dram_tensor("h_s", [NTOK2, D], F32, kind="Internal").ap()

    with contextlib.ExitStack() as top:
        S = Sched(nc, top)
        K = KB(nc, S)
        R = S.res
        xs_v = V(xs, R("xs"))
        win_v = V(win, R("win"))
        wgu_v = V(wgu, R("wgu"))
        wdn_v = V(wdn, R("wdn"))
        wout_v = V(wout, R("wout"))
        lora_v = V(lora, R("lora"))
        out_v = V(out_d, R("out"))
        yT_v = V(yT_d, R("yT"))
        n2T_v = V(n2T_d, R("n2T"))
        h_v = V(h_d, R("h"))

        banks = []
        for i in range(8):
            t = top.enter_context(nc.psum_tensor("pb%d" % i, [128, 512], F32))
            banks.append(V(t[:, :], R("pb%d" % i)))
        bank_i = [0]

        def psum():
            b = banks[bank_i[0] % 8]
            bank_i[0] += 1
            return b

        def sbt(stack, name, shape, dt):
            t = stack.enter_context(nc.sbuf_tensor("sb_" + name, list(shape), dt))
            return V(t[tuple(slice(None) for _ in shape)], R(name))

        vecs = sbt(top, "vecs", [128, NV], F32)
        vder = sbt(top, "vder", [128, 28 + 8], F32)
        cf = sbt(top, "cf", [128, NCONST], F32)
        cb = sbt(top, "cb", [128, NCONST], BF16)
        ones = sbt(top, "ones", [128, 128], F32)
        K.dma("sp", vecs, V(vecs_d, R("vecs_d")))
        K.dma("sp", cf, V(const_d, R("const_d")))
        K.dma("pool", cb, V(const_d, R("const_d2")))
        K.memset("dve", ones, 1.0)
        K.ts("dve", vder[:, 0:28], vecs[:, V_MU:V_MU + 28], -1.0, 1.0, ALU.mult, ALU.add)
        K.ts("dve", vder[:, 28:36], vecs[:, V_KA:V_KA + 8], -1.0, 1.0, ALU.mult, ALU.add)
        ident_b = cb[:, C_ID:C_ID + 128]
        mask4 = cb[:, C_M4:C_M4 + 512]
        maskL4 = cb[:, C_ML:C_ML + 512]
        blockones = cf[:, C_BO:C_BO + 128]
        headsel_b = cb[:, C_HS:C_HS + 2]
        hk = vecs[:, V_HK:V_HK + 1]

        def rms_rstd(src, rstd, junk, nbanks=None):
            pass

        try:
          _ck(1)
          with contextlib.ExitStack() as p1:
              gpre = sbt(p1, "gpre", [128, D], F32)
              lnw = sbt(p1, "lnw", [128, RW], F32)
              lnb = sbt(p1, "lnb", [128, RW], F32)
              K.dma("sp", gpre, V(bcv["g_pre_mix"].partition_broadcast(128), R("bc0")))
              K.dma("sp", lnw, V(bcv["ln_x_w"].partition_broadcast(128), R("bc1")))
              K.dma("sp", lnb, V(bcv["ln_x_b"].partition_broadcast(128), R("bc2")))
              lw_b = sbt(p1, "lw_b", [128, 4 * RW], BF16)
              K.dma("pool", lw_b, lora_v)
              wring = [sbt(p1, "wr%d" % i, [128, KC, 128], BF16) for i in range(4)]
              wr_i = [0]
              xin = sbt(p1, "xin", [128, D], F32)
              xn = sbt(p1, "xn", [128, D], BF16)
              st4 = sbt(p1, "st4", [128, 8], F32)
              nT = sbt(p1, "nT", [128, KC, TILE], BF16)
              carry = sbt(p1, "carry", [128, 28], F32)
              K.memset("dve", carry, 0.0)
              ccar = sbt(p1, "ccar", [128, 8, 2], F32)
              K.memset("dve", ccar, 0.0)
              tw = sbt(p1, "tw", [128, TILE], BF16)
              adb = sbt(p1, "adb", [128, TILE], BF16)
              sg2 = [sbt(p1, "sg%d" % i, [128, 2, TILE], BF16) for i in range(2)]
              ztmp = sbt(p1, "ztmp", [128, TILE], F32)
              zr = sbt(p1, "zr", [128, TILE], F32)
              zk = sbt(p1, "zk", [128, TILE], F32)
              zv = sbt(p1, "zv", [128, TILE], F32)
              fA = [sbt(p1, "fA%d" % i, [128, TILE], F32) for i in range(8)]
              AR2 = [sbt(p1, "AR%d" % i, [128, 2, 4, 256], BF16) for i in range(2)]
              BK2 = [sbt(p1, "BK%d" % i, [128, 2, 4, 256], BF16) for i in range(2)]
              TF2 = [sbt(p1, "TF%d" % i, [128, 2, 3, TILE], BF16) for i in range(2)]
              rkb2 = [sbt(p1, "rkb%d" % i, [128, 2, TILE], BF16) for i in range(2)]
              PC2 = [sbt(p1, "PCd%d" % i, [128, 2, 4], F32) for i in range(2)]
              TOK = [sbt(p1, "TOK%d" % i, [128, 2, 4, 128], BF16) for i in range(2)]
              SA = [sbt(p1, "SA%d" % i, [128, 4, 512], BF16) for i in range(2)]
              SN = [[sbt(p1, "SN%d_%d" % (i, j), [128, 4, 128], BF16) for j in range(2)] for i in range(2)]
              SL = [[sbt(p1, "SL%d_%d" % (i, j), [128, 4, 128], BF16) for j in range(2)] for i in range(2)]
              WW = [[sbt(p1, "WW%d_%d" % (i, j), [128, 4, 128], BF16) for j in range(2)] for i in range(2)]
              GT = [sbt(p1, "GT%d" % i, [128, 2, 64], BF16) for i in range(2)]
              RH = [sbt(p1, "RH%d" % i, [128, 2, 128], BF16) for i in range(2)]
              Tst = sbt(p1, "Tst", [128, 8, 64], F32)
              Tbf = [sbt(p1, "Tbf%d" % i, [128, 8, 64], BF16) for i in range(2)]
              K.memset("dve", Tst, 0.0)
              K.memset("dve", Tbf[0], 0.0)
              K.memset("dve", Tbf[1], 0.0)
              tb_i = [[0] * 8]
              ysb = sbt(p1, "ysb", [128, 256], F32)
              ysq = sbt(p1, "ysq", [128, 256], F32)
              yn = sbt(p1, "yn", [128, 256], F32)
              bon = sbt(p1, "bon", [128, 256], F32)
              gst = sbt(p1, "gst", [128, 24], F32)
              ytb = sbt(p1, "ytb", [128, 256], BF16)
              yTs = sbt(p1, "yTs", [128, 2, TILE], BF16)
              csb = sbt(p1, "csb", [128, TILE], F32)
              chb = sbt(p1, "chb", [128, TILE + 2], F32)
              ycv = sbt(p1, "ycv", [128, TILE], BF16)

              def load_w(cbi):
                  w = wring[wr_i[0] % 4]
                  wr_i[0] += 1
                  K.dma("pool", w.re("p k j -> p (k j)"), win_v[cbi])
                  return w

              def project(w, t_cols=TILE):
                  pz = psum()
                  for kc in range(KC):
                      K.mm(pz[:, 0:t_cols], w[:, kc, :], nT[:, kc, 0:t_cols], start=(kc == 0), stop=(kc == KC - 1))
                  return pz

              def shift(pz, col, out):
                  mu = vecs[:, V_MU + col:V_MU + col + 1]
                  omu = vder[:, col:col + 1]
                  K.act(ztmp, pz, AF.Copy, scale=omu)
                  K.stt("dve", out[:, 1:TILE], pz[:, 0:TILE - 1], mu, ztmp[:, 1:TILE], ALU.mult, ALU.add)
                  K.stt("dve", out[:, 0:1], carry[:, col:col + 1], mu, ztmp[:, 0:1], ALU.mult, ALU.add)
                  K.copy("act", carry[:, col:col + 1], pz[:, TILE - 1:TILE])

              def front(ti):
                  t0 = ti * TILE
                  sg = sg2[ti % 2]
                  for j in range(4):
                      K.dma("sp", xin, xs_v[t0 + j * CH:t0 + (j + 1) * CH, :])
                      K.memset("dve", st4[:, 0:1], 0.0)
                      K.act(xn, xin, AF.Square, accum=st4[:, 0:1])
                      K.ts("dve", st4[:, 1:2], st4[:, 0:1], 1.0 / D, RMS_EPS, ALU.mult, ALU.add)
                      K.act(st4[:, 2:3], st4[:, 1:2], AF.Sqrt)
                      K.recip(st4[:, 3:4], st4[:, 2:3])
                      K.stt("dve", xn, xin, st4[:, 3:4], gpre, ALU.mult, ALU.mult)
                      yield
                      for a in range(2):
                          pb = psum()
                          pbb = pb.bitcast(BF16)
                          for k8 in range(8):
                              kc = a * 8 + k8
                              K.tr(pbb[:, k8 * 128:(k8 + 1) * 128], xn[:, kc * 128:(kc + 1) * 128], ident_b)
                          K.copy("act" if a == 0 else "dve", nT[:, a * 8:(a + 1) * 8, j * CH:(j + 1) * CH],
                                 pbb.re("p (k t) -> p k t", k=8))
                          yield
                  pz = project(load_w(0))
                  shift(pz, 0, zr)
                  K.act(tw, zr, AF.Tanh)
                  yield
                  pz = project(load_w(1))
                  shift(pz, 1, zr)
                  K.copy("act", adb, zr)
                  yield
                  for gi in range(2):
                      pz = project(load_w(2 + gi))
                      shift(pz, 2 + gi, zr)
                      K.act(sg[:, gi, :], zr, AF.Sigmoid)
                      yield

              def prep(ti, hg):
                  out_tile = ti >= OT0
                  par = (ti * 4 + hg) % 2
                  AR, BK, TF, rkb, PC = AR2[par], BK2[par], TF2[par], rkb2[par], PC2[par]
                  for s in range(2):
                      hp = hg * 2 + s
                      for qi, dst in enumerate((zr, zk, zv)):
                          pz = project(load_w(4 + hp * 3 + qi))
                          shift(pz, 4 + hp * 3 + qi, dst)
                          yield
                      plw = psum()
                      K.mm(plw, lw_b[:, hp * 128:(hp + 1) * 128], tw)
                      pa = psum()
                      K.mm(pa, lw_b[:, RW + hp * 128:RW + (hp + 1) * 128], adb)
                      sw, aa, cs, kk, t1, t2, t3, t4 = fA
                      K.act(sw, plw, AF.Sigmoid, bias=vecs[:, V_W0 + hp:V_W0 + hp + 1])
                      K.act(aa, pa, AF.Sigmoid, bias=vecs[:, V_A0 + hp:V_A0 + hp + 1])
                      yield
                      for c in range(4):
                          K.scan(cs[:, c * CH:(c + 1) * CH], ones, sw[:, c * CH:(c + 1) * CH])
                      K.ts("dve", kk, zk, vecs[:, V_KK + hp:V_KK + hp + 1])
                      K.tt("dve", t1, kk, kk, ALU.mult)
                      yield
                      pss = psum()
                      K.mm(pss, blockones, t1)
                      K.act(t1, pss, AF.Sqrt)
                      K.ts("dve", t1, t1, 1e-12, None, ALU.max)
                      K.recip(t2, t1)
                      K.tt("dve", kk, kk, t2, ALU.mult)
                      yield
                      K.ts("dve", t1, aa, vecs[:, V_KA + hp:V_KA + hp + 1], vder[:, 28 + hp:29 + hp], ALU.mult, ALU.add)
                      K.tt("dve", t1, zk, t1, ALU.mult)
                      K.tt("dve", aa, kk, aa, ALU.mult)
                      yield
                      K.tt("dve", t2, cs, sw, ALU.subtract)
                      K.act(t2, t2, AF.Exp, scale=-DECAY_C)
                      K.stt("dve", AR[:, s, :, 0:128], kk.re("p (c t) -> p c t", c=4), -1.0,
                            t2.re("p (c t) -> p c t", c=4), ALU.mult, ALU.mult)
                      yield
                      K.act(t3, cs, AF.Exp, scale=-DECAY_C)
                      K.tt("dve", AR[:, s, :, 128:256], zr.re("p (c t) -> p c t", c=4),
                           t3.re("p (c t) -> p c t", c=4), ALU.mult)
                      yield
                      K.act(t3, cs, AF.Exp, scale=DECAY_C)
                      K.tt("dve", BK[:, s, :, 0:128], aa.re("p (c t) -> p c t", c=4),
                           t3.re("p (c t) -> p c t", c=4), ALU.mult)
                      K.tt("dve", BK[:, s, :, 128:256], t1.re("p (c t) -> p c t", c=4),
                           t3.re("p (c t) -> p c t", c=4), ALU.mult)
                      yield
                      K.ts("dve", st4[:, 4:8], cs.re("p (c t) -> p c t", c=4)[:, :, CH - 1], -DECAY_C)
                      K.act(PC[:, s, :], st4[:, 4:8], AF.Exp)
                      for c in range(4):
                          K.act(t4[:, c * CH:(c + 1) * CH], cs[:, c * CH:(c + 1) * CH], AF.Exp,
                                scale=DECAY_C, bias=st4[:, 4 + c:5 + c])
                      yield
                      K.tt("dve", TF[:, s, 1, :], aa, t4, ALU.mult)
                      K.tt("dve", TF[:, s, 2, :], t1, t4, ALU.mult)
                      K.copy("act", TF[:, s, 0, :], zv)
                      if out_tile:
                          K.tt("dve", t2, zr, t1, ALU.mult)
                          K.ts("dve", rkb[:, s, :], t2, vecs[:, V_RK + hp:V_RK + hp + 1])
                      yield

              def scan(ti, hg):
                  out_tile = ti >= OT0
                  par = (ti * 4 + hg) % 2
                  AR, BK, TF, rkb, PC = AR2[par], BK2[par], TF2[par], rkb2[par], PC2[par]
                  sg = sg2[ti % 2]
                  for cp in range(2):
                      cl = [2 * cp, 2 * cp + 1]
                      for u, c in enumerate(cl):
                          pb = psum()
                          pbb = pb.bitcast(BF16).re("p (s q t) -> p s q t", s=2, q=4)
                          for s in range(2):
                              for q in range(3):
                                  K.tr(pbb[:, s, q, :], TF[:, s, q, c * CH:(c + 1) * CH], ident_b)
                              K.tr(pbb[:, s, 3, :], AR[:, s, c, 0:128], ident_b)
                          K.copy("act", TOK[u], pbb)
                          yield
                      for u, c in enumerate(cl):
                          for hl in range(4):
                              s, e = hl // 2, hl % 2
                              ps_ = slice(64 * e, 64 * e + 64)
                              pa_ = psum()
                              K.mm(pa_[:, 0:256], BK[ps_, s, c, 0:128], AR[ps_, s, c, :])
                              K.mm(pa_[:, 256:512], BK[ps_, s, c, 128:256], AR[ps_, s, c, :])
                              K.tt("dve", SA[u][:, hl, :], pa_, mask4, ALU.mult)
                          yield
                          ple = [psum(), psum()]
                          for hl in range(4):
                              s, e = hl // 2, hl % 2
                              ps_ = slice(64 * e, 64 * e + 64)
                              K.mm(ple[e][:, s * 128:(s + 1) * 128], AR[ps_, s, c, 0:128], BK[ps_, s, c, 0:128])
                          for e in range(2):
                              K.tt("dve", SL[u][0].re("p (s e) t -> p s e t", e=2)[:, :, e, :],
                                   ple[e][:, 0:256].re("p (s t) -> p s t", s=2),
                                   maskL4[:, 0:256].re("p (s t) -> p s t", s=2), ALU.mult)
                          px = psum()
                          for hl in range(4):
                              s, e = hl // 2, hl % 2
                              K.mm(px[:, hl * 64:(hl + 1) * 64], SA[u][:, hl, 256:384], TOK[u][:, s, 0, 64 * e:64 * e + 64])
                          K.copy("act", WW[u][0][:, :, 64:128], px[:, 0:256].re("p (h v) -> p h v", h=4))
                          K.copy("act", WW[u][0].re("p (s e) k -> p s e k", s=2)[:, :, :, 0:64],
                                 TOK[u][:, :, 3, :].re("p s (e k) -> p s e k", e=2))
                          yield
                      for i in range(7):
                          for u, c in enumerate(cl):
                              n_cur = SA[u][:, :, 0:128] if i == 0 else SN[u][i % 2]
                              l_cur = SL[u][i % 2]
                              w_cur = WW[u][i % 2]
                              pw = psum()
                              for hl in range(4):
                                  K.mm(pw[:, hl * 128:(hl + 1) * 128], n_cur[:, hl, :], w_cur[:, hl, :])
                              K.tt("dve", WW[u][(i + 1) % 2].re("p h t -> p (h t)"), pw,
                                   w_cur.re("p h t -> p (h t)"), ALU.add)
                              if i < 6:
                                  pn = psum()
                                  for hl in range(4):
                                      K.mm(pn[:, hl * 128:(hl + 1) * 128], l_cur[:, hl, :], n_cur[:, hl, :])
                                  K.copy("act", SN[u][(i + 1) % 2].re("p h t -> p (h t)"), pn)
                              if i < 5:
                                  pl_ = psum()
                                  for hl in range(4):
                                      K.mm(pl_[:, hl * 128:(hl + 1) * 128], n_cur[:, hl, :], l_cur[:, hl, :])
                                  K.copy("act", SL[u][(i + 1) % 2].re("p h t -> p (h t)"), pl_)
                              yield
                      for u, c in enumerate(cl):
                          Wf = WW[u][1]
                          chunk_g = ti * 4 + c
                          want_y = chunk_g >= OC0
                          pg = psum()
                          for hl in range(4):
                              s, e = hl // 2, hl % 2
                              K.mm(pg[64 * e:64 * e + 64, s * 64:(s + 1) * 64], Wf[:, hl, 0:64],
                                   TOK[u][:, s, 1, 64 * e:64 * e + 64])
                          K.copy("act", GT[u].re("p s k -> p (s k)"), pg[:, 0:128])
                          tcur = Tbf[tb_i[0][hg] % 2]
                          tnew = Tbf[(tb_i[0][hg] + 1) % 2]
                          if want_y:
                              pr = psum()
                              for hl in range(4):
                                  s, e = hl // 2, hl % 2
                                  K.mm(pr[64 * e:64 * e + 64, s * 128:(s + 1) * 128], Wf[:, hl, 0:64], SA[u][:, hl, 128:256])
                              K.tt("dve", RH[u], pr[:, 0:256].re("p (s t) -> p s t", s=2), AR[:, :, c, 128:256], ALU.add)
                              py = psum()
                              for hl in range(4):
                                  s, e = hl // 2, hl % 2
                                  ps_ = slice(64 * e, 64 * e + 64)
                                  o = py[:, hl * 64:(hl + 1) * 64]
                                  K.mm(o, SA[u][:, hl, 128:256], Wf[:, hl, 64:128], start=True, stop=False)
                                  K.mm(o, SA[u][:, hl, 384:512], TOK[u][:, s, 0, 64 * e:64 * e + 64], start=False, stop=False)
                                  K.mm(o, RH[u][ps_, s, :], tcur[ps_, hg * 2 + s, :], start=False, stop=True)
                          pt = psum()
                          for hl in range(4):
                              s, e = hl // 2, hl % 2
                              ps_ = slice(64 * e, 64 * e + 64)
                              o = pt[ps_, s * 64:(s + 1) * 64]
                              K.mm(o, TOK[u][:, s, 1, 64 * e:64 * e + 64], Wf[:, hl, 64:128], start=True, stop=False)
                              K.mm(o, TOK[u][:, s, 2, 64 * e:64 * e + 64], TOK[u][:, s, 0, 64 * e:64 * e + 64], start=False, stop=False)
                              K.mm(o, GT[u][ps_, s, :], tcur[ps_, hg * 2 + s, :], start=False, stop=True)
                          for s in range(2):
                              K.stt("dve", Tst[:, hg * 2 + s, :], Tst[:, hg * 2 + s, :], PC[:, s, c:c + 1],
                                    pt[:, s * 64:(s + 1) * 64], ALU.mult, ALU.add)
                          K.copy("act", tnew[:, hg * 2:hg * 2 + 2, :], Tst[:, hg * 2:hg * 2 + 2, :])
                          tb_i[0][hg] += 1
                          if want_y:
                              K.copy("act", ysb, py[:, 0:256])
                          yield
                          if not want_y:
                              continue
                          pgt = psum()
                          K.mm(pgt[:, 0:256], sg[:, 0, c * CH:(c + 1) * CH], lw_b[:, 2 * RW + hg * 256:2 * RW + (hg + 1) * 256],
                               start=True, stop=False)
                          K.mm(pgt[:, 0:256], sg[:, 1, c * CH:(c + 1) * CH], lw_b[:, 3 * RW + hg * 256:3 * RW + (hg + 1) * 256],
                               start=False, stop=True)
                          for s in range(2):
                              K.mm(pgt[:, 256 + 2 * s:258 + 2 * s], rkb[:, s, c * CH:(c + 1) * CH], headsel_b)
                          K.copy("act", gst[:, 16:20], pgt[:, 256:260])
                          y3 = ysb.re("p (h v) -> p h v", h=4)
                          K.red("dve", gst[:, 0:4], y3)
                          K.tt("dve", ysq, ysb, ysb, ALU.mult)
                          K.red("dve", gst[:, 4:8], ysq.re("p (h v) -> p h v", h=4))
                          K.ts("dve", gst[:, 0:4], gst[:, 0:4], 1.0 / 64)
                          K.tt("dve", gst[:, 8:12], gst[:, 0:4], gst[:, 0:4], ALU.mult)
                          K.stt("dve", gst[:, 4:8], gst[:, 4:8], 1.0 / 64, gst[:, 8:12], ALU.mult, ALU.subtract)
                          K.ts("dve", gst[:, 4:8], gst[:, 4:8], GN_EPS, None, ALU.add)
                          K.act(gst[:, 8:12], gst[:, 4:8], AF.Sqrt)
                          K.recip(gst[:, 12:16], gst[:, 8:12])
                          yn3 = yn.re("p (h v) -> p h v", h=4)
                          K.tt("dve", yn3, y3, gst[:, 0:4].bc([128, 4, 64], 2), ALU.subtract)
                          K.tt("dve", yn3, yn3, gst[:, 12:16].bc([128, 4, 64], 2), ALU.mult)
                          K.tt("dve", yn, yn, lnw[:, hg * 256:(hg + 1) * 256], ALU.mult)
                          K.tt("dve", yn, yn, lnb[:, hg * 256:(hg + 1) * 256], ALU.add)
                          K.tt("dve", bon.re("p (s e v) -> p s e v", s=2, e=2),
                               TOK[u][:, :, 0, :].re("p s (e v) -> p s e v", e=2),
                               gst[:, 16:20].re("p (s e) -> p s e", s=2).bc([128, 2, 2, 64], 3), ALU.mult)
                          K.tt("dve", yn, yn, bon, ALU.add)
                          K.tt("dve", ytb, yn, pgt[:, 0:256], ALU.mult)
                          pb = psum()
                          pbb = pb.bitcast(BF16)
                          for s in range(2):
                              K.tr(pbb[:, s * 128:(s + 1) * 128], ytb[:, s * 128:(s + 1) * 128], ident_b)
                          K.copy("act", yTs[:, :, c * CH:(c + 1) * CH], pbb[:, 0:256].re("p (s t) -> p s t", s=2))
                          yield
                  if out_tile:
                      c_lo = max(0, OC0 - ti * 4)
                      tk0 = (ti * 4 + c_lo - OC0) * CH
                      ncol = (4 - c_lo) * CH
                      for s in range(2):
                          K.dma("sp", yT_v[8 + hg * 2 + s, :, tk0:tk0 + ncol], yTs[:, s, c_lo * CH:4 * CH])
                  yield

              def conv(ti):
                  c_lo = max(0, OC0 - ti * 4)
                  tk0 = (ti * 4 + c_lo - OC0) * CH
                  ncol = (4 - c_lo) * CH
                  for i in range(8):
                      pB = project(load_w(28 + i * 3 + 0))
                      pC = project(load_w(28 + i * 3 + 1))
                      pH = project(load_w(28 + i * 3 + 2))
                      K.copy("act", csb, pC)
                      K.copy("act", chb[:, 0:2], ccar[:, i, :])
                      K.tt("dve", chb[:, 2:TILE + 2], csb, pH, ALU.mult)
                      K.copy("act", ccar[:, i, :], chb[:, TILE:TILE + 2])
                      K.ts("dve", csb, chb[:, 0:TILE], vecs[:, V_CW + i:V_CW + i + 1])
                      K.stt("dve", csb, chb[:, 1:TILE + 1], vecs[:, V_CW + 8 + i:V_CW + 9 + i], csb, ALU.mult, ALU.add)
                      K.stt("dve", csb, chb[:, 2:TILE + 2], vecs[:, V_CW + 16 + i:V_CW + 17 + i], csb, ALU.mult, ALU.add)
                      K.tt("dve", ycv, csb, pB, ALU.mult)
                      K.dma("sp", yT_v[i, :, tk0:tk0 + ncol], ycv[:, c_lo * CH:4 * CH])
                      yield

              def chain(*gens):
                  for g in gens:
                      yield from g

              def interleave(ga, gb):
                  la, lb = True, True
                  while la or lb:
                      if la:
                          try:
                              next(ga)
                          except StopIteration:
                              la = False
                      if lb:
                          try:
                              next(gb)
                          except StopIteration:
                              lb = False

              PIPE = os.environ.get("KPIPE", "1") == "1"
              units = [(ti, hg) for ti in range(NT) for hg in range(4)]

              def p_stream(ti, hg):
                  gens = []
                  if hg == 0:
                      gens.append(front(ti))
                  gens.append(prep(ti, hg))
                  return gens

              for g in chain(*p_stream(0, 0)):
                  pass
              for ui, (ti, hg) in enumerate(units):
                  nxt = []
                  if hg == 3 and ti >= OT0:
                      nxt.append(conv(ti))
                  if ui + 1 < len(units):
                      nxt += p_stream(*units[ui + 1])
                  if PIPE:
                      interleave(scan(ti, hg), chain(*nxt))
                  else:
                      for g in scan(ti, hg):
                          pass
                      for g in chain(*nxt):
                          pass
          _ck(8)
          S.barrier()

          with contextlib.ExitStack() as p15:
              wo_b = sbt(p15, "wo_b", [128, KC, D], BF16)
              K.dma("pool", wo_b.re("p k d -> p (k d)"), wout_v)
              gpm = sbt(p15, "gpm", [128, D], F32)
              gpf = sbt(p15, "gpf", [128, D], F32)
              K.dma("sp", gpm, V(bcv["g_post_mix"].partition_broadcast(128), R("bc3")))
              K.dma("sp", gpf, V(bcv["g_pre_ffn"].partition_broadcast(128), R("bc4")))
              ych = [sbt(p15, "ych%d" % i, [128, KC, CH], BF16) for i in range(2)]
              xin2 = [sbt(p15, "xin2_%d" % i, [128, D], F32) for i in range(2)]
              hsb = [sbt(p15, "hsb%d" % i, [128, D], F32) for i in range(2)]
              n2 = sbt(p15, "n2", [128, D], BF16)
              n2T = [sbt(p15, "n2T%d" % i, [128, KC, CH], BF16) for i in range(2)]
              st5 = sbt(p15, "st5", [128, 16], F32)
              for oc in range(NOC):
                  yc = ych[oc % 2]
                  xi = xin2[oc % 2]
                  hs = hsb[oc % 2]
                  nt2 = n2T[oc % 2]
                  K.dma("sp", yc, yT_v[:, :, oc * CH:(oc + 1) * CH].re("c p t -> p c t"))
                  K.dma("sp", xi, xs_v[(OC0 + oc) * CH:(OC0 + oc + 1) * CH, :])
                  pm = [psum() for _ in range(4)]
                  for blk in range(4):
                      for chc in range(KC):
                          K.mm(pm[blk], yc[:, chc, :], wo_b[:, chc, blk * 512:(blk + 1) * 512],
                               start=(chc == 0), stop=(chc == KC - 1))
                  K.memset("dve", st5[:, 0:4], 0.0)
                  for blk in range(4):
                      K.act(n2[:, blk * 512:(blk + 1) * 512], pm[blk], AF.Square, accum=st5[:, blk:blk + 1])
                  K.red("dve", st5[:, 4:5], st5[:, 0:4])
                  K.ts("dve", st5[:, 4:5], st5[:, 4:5], 1.0 / D, RMS_EPS, ALU.mult, ALU.add)
                  K.act(st5[:, 5:6], st5[:, 4:5], AF.Sqrt)
                  K.recip(st5[:, 6:7], st5[:, 5:6])
                  for blk in range(4):
                      sl = slice(blk * 512, (blk + 1) * 512)
                      K.stt("dve", hs[:, sl], pm[blk], st5[:, 6:7], gpm[:, sl], ALU.mult, ALU.mult)
                  K.tt("dve", hs, hs, xi, ALU.add)
                  K.dma("sp", h_v[oc * CH:(oc + 1) * CH, :], hs)
                  K.memset("dve", st5[:, 8:9], 0.0)
                  K.act(n2, hs, AF.Square, accum=st5[:, 8:9])
                  K.ts("dve", st5[:, 9:10], st5[:, 8:9], 1.0 / D, RMS_EPS, ALU.mult, ALU.add)
                  K.act(st5[:, 10:11], st5[:, 9:10], AF.Sqrt)
                  K.recip(st5[:, 11:12], st5[:, 10:11])
                  K.stt("dve", n2, hs, st5[:, 11:12], gpf, ALU.mult, ALU.mult)
                  for a in range(2):
                      pb = psum()
                      pbb = pb.bitcast(BF16)
                      for k8 in range(8):
                          kc = a * 8 + k8
                          K.tr(pbb[:, k8 * 128:(k8 + 1) * 128], n2[:, kc * 128:(kc + 1) * 128], ident_b)
                      K.copy("act", nt2[:, a * 8:(a + 1) * 8, :], pbb.re("p (k t) -> p k t", k=8))
                  if oc == 0:
                      K.ts("dve", nt2[:, :, 0:HALO], nt2[:, :, 0:HALO], hk)
                  K.dma("sp", n2T_v[:, :, oc * CH:(oc + 1) * CH].re("c p t -> p c t"), nt2)
          S.barrier()

          with contextlib.ExitStack() as p2:
              gpo = sbt(p2, "gpo", [128, D], F32)
              K.dma("sp", gpo, V(bcv["g_post_ffn"].partition_broadcast(128), R("bc5")))
              facc = sbt(p2, "facc", [128, STC, D], F32)
              n2s = sbt(p2, "n2s", [128, KC, STC * CH], BF16)
              wg_r = [sbt(p2, "wg%d" % i, [128, 2, KC, 128], BF16) for i in range(2)]
              wd_r = [sbt(p2, "wd%d" % i, [128, D], BF16) for i in range(4)]
              gbuf = sbt(p2, "gbuf", [128, STC * CH + 2], F32)
              cvt = sbt(p2, "cvt", [128, TILE], F32)
              slu = sbt(p2, "slu", [128, TILE], F32)
              actT = [sbt(p2, "actT%d" % i, [128, STC * CH], BF16) for i in range(4)]
              hld = sbt(p2, "hld", [128, D], F32)
              st6 = sbt(p2, "st6", [128, 8], F32)
              K.memset("dve", gbuf[:, 0:2], 0.0)
              st_starts = []
              s0 = 0
              while True:
                  st_starts.append(s0)
                  if s0 + STC >= NOC:
                      break
                  s0 += STC - 1
              for sti, cs0 in enumerate(st_starts):
                  nch = min(STC, NOC - cs0)
                  ntk = nch * CH
                  first_out = 0 if sti == 0 else 1
                  K.dma("sp", n2s[:, :, 0:ntk], n2T_v[:, :, cs0 * CH:cs0 * CH + ntk].re("c p t -> p c t"))
                  ttiles = [(a, min(TILE, ntk - a)) for a in range(0, ntk, TILE)]
                  for fb0 in range(0, NFB, 2):
                      ats, wds = [], []
                      for g2 in range(2):
                          fb = fb0 + g2
                          wgb = wg_r[fb % 2]
                          wdb = wd_r[fb % 4]
                          at = actT[fb % 4]
                          ats.append(at)
                          wds.append(wdb)
                          K.dma("pool", wgb.re("p g k j -> p g (k j)"), wgu_v[2 * fb:2 * fb + 2].re("g p f -> p g f"))
                          K.dma("pool", wdb, wdn_v[fb])
                          for (a0, tn) in ttiles:
                              pgg = psum()
                              pu = psum()
                              for kc in range(KC):
                                  K.mm(pgg[:, 0:tn], wgb[:, 0, kc, :], n2s[:, kc, a0:a0 + tn], start=(kc == 0), stop=(kc == KC - 1))
                              for kc in range(KC):
                                  K.mm(pu[:, 0:tn], wgb[:, 1, kc, :], n2s[:, kc, a0:a0 + tn], start=(kc == 0), stop=(kc == KC - 1))
                              K.copy("act", gbuf[:, 2 + a0:2 + a0 + tn], pgg[:, 0:tn])
                              K.ts("dve", cvt[:, 0:tn], gbuf[:, a0:a0 + tn], vecs[:, V_CF + fb:V_CF + fb + 1])
                              K.stt("dve", cvt[:, 0:tn], gbuf[:, a0 + 1:a0 + 1 + tn], vecs[:, V_CF + NFB + fb:V_CF + NFB + fb + 1],
                                    cvt[:, 0:tn], ALU.mult, ALU.add)
                              K.stt("dve", cvt[:, 0:tn], gbuf[:, a0 + 2:a0 + 2 + tn], vecs[:, V_CF + 2 * NFB + fb:V_CF + 2 * NFB + fb + 1],
                                    cvt[:, 0:tn], ALU.mult, ALU.add)
                              K.act(slu[:, 0:tn], cvt[:, 0:tn], AF.Silu)
                              K.tt("dve", at[:, a0:a0 + tn], slu[:, 0:tn], pu[:, 0:tn], ALU.mult)
                      for ci in range(nch):
                          for blk in range(4):
                              pf = psum()
                              K.mm(pf, ats[0][:, ci * CH:(ci + 1) * CH], wds[0][:, blk * 512:(blk + 1) * 512], start=True, stop=False)
                              K.mm(pf, ats[1][:, ci * CH:(ci + 1) * CH], wds[1][:, blk * 512:(blk + 1) * 512], start=False, stop=True)
                              sl = slice(blk * 512, (blk + 1) * 512)
                              if fb0 == 0:
                                  K.copy("act", facc[:, ci, sl], pf)
                              else:
                                  K.tt("dve", facc[:, ci, sl], facc[:, ci, sl], pf, ALU.add)
                  for ci in range(first_out, nch):
                      oc = cs0 + ci
                      K.dma("sp", hld, h_v[oc * CH:(oc + 1) * CH, :])
                      K.memset("dve", st6[:, 0:1], 0.0)
                      K.act(n2s.re("p k t -> p (k t)")[:, 0:D], facc[:, ci, :], AF.Square, accum=st6[:, 0:1])
                      K.ts("dve", st6[:, 1:2], st6[:, 0:1], 1.0 / D, RMS_EPS, ALU.mult, ALU.add)
                      K.act(st6[:, 2:3], st6[:, 1:2], AF.Sqrt)
                      K.recip(st6[:, 3:4], st6[:, 2:3])
                      K.stt("dve", facc[:, ci, :], facc[:, ci, :], st6[:, 3:4], gpo, ALU.mult, ALU.mult)
                      K.tt("dve", facc[:, ci, :], facc[:, ci, :], hld, ALU.add)
                      K.dma("sp", out_v[oc * CH:(oc + 1) * CH, :], facc[:, ci, :])
          S.barrier()
        except _Stop:
            pass
        S.barrier()
        S.emit()
    return nc


def _blk(w):
    return np.ascontiguousarray(w.reshape(KC, 128, 128).transpose(1, 0, 2)).reshape(128, KC * 128)


def _pad_cols(w, n):
    out = np.zeros((w.shape[0], n), w.dtype)
    out[:, :w.shape[1]] = w
    return out


def _consts():
    c = np.zeros((128, NCONST), np.float32)
    j = np.arange(128)[:, None]
    t = np.arange(128)[None, :]
    c[:, C_ID:C_ID + 128] = np.eye(128)
    strict = (t > j).astype(np.float32)
    incl = (t >= j).astype(np.float32)
    c[:, C_M4:C_M4 + 512] = np.concatenate([strict, incl, strict, incl], 1)
    low = (j > t).astype(np.float32)
    c[:, C_ML:C_ML + 512] = np.concatenate([low] * 4, 1)
    c[:, C_BO:C_BO + 128] = (j // 64 == t // 64).astype(np.float32)
    c[:, C_HS:C_HS + 2] = (j // 64 == np.arange(2)[None, :]).astype(np.float32)
    return c


def prepare(inputs, NT, RANGE):
    f = lambda k: np.asarray(inputs[k], np.float32)
    w_in = f("w_in")
    RB = 3072
    cols = []
    cols.append(_pad_cols(w_in[:, RB + 3072:RB + 3168], 128))
    cols.append(_pad_cols(w_in[:, RB + 3168:RB + 3264], 128))
    cols.append(w_in[:, RB + 3264:RB + 3392])
    cols.append(w_in[:, RB + 3392:RB + 3520])
    for i in range(8):
        for base in (0, 1024, 2048):
            cols.append(w_in[:, RB + base + i * 128:RB + base + (i + 1) * 128])
    for i in range(8):
        for base in (0, 1024, 2048):
            cols.append(w_in[:, base + i * 128:base + (i + 1) * 128])
    win = np.stack([_blk(c) for c in cols], 0)
    wgu_full = f("w_gate_up")
    wgu = np.stack([_blk(wgu_full[:, g * DFF + fb * 128:g * DFF + (fb + 1) * 128]) for fb in range(NFB) for g in range(2)], 0)
    wdn = np.ascontiguousarray(f("w_down").reshape(NFB, 128, D))
    wout = np.ascontiguousarray(f("w_out").reshape(KC, 128, D).transpose(1, 0, 2)).reshape(128, KC * D)
    lora = np.zeros((128, 4 * RW), np.float32)
    lora[:96, 0:RW] = f("w_decay_up")
    lora[:96, RW:2 * RW] = f("a_up")
    gup = f("g_up")
    lora[:, 2 * RW:3 * RW] = gup[0:128]
    lora[:, 3 * RW:4 * RW] = gup[128:256]
    mu = f("mu_rwkv")
    vecs = np.zeros((128, NV), np.float32)
    vecs[:96, V_MU + 0] = mu[3072:3168]
    vecs[:96, V_MU + 1] = mu[3168:3264]
    vecs[:, V_MU + 2] = mu[3264:3392]
    vecs[:, V_MU + 3] = mu[3392:3520]
    for i in range(8):
        for qi, base in enumerate((0, 1024, 2048)):
            vecs[:, V_MU + 4 + i * 3 + qi] = mu[base + i * 128:base + (i + 1) * 128]
    for name, col in (("w0", V_W0), ("a0", V_A0), ("k_k", V_KK), ("k_a", V_KA)):
        vecs[:, col:col + 8] = f(name).reshape(8, 128).T
    vecs[:, V_RK:V_RK + 8] = f("r_k").reshape(8, 128).T
    cw = f("conv_w")
    for tap in range(3):
        vecs[:, V_CW + tap * 8:V_CW + tap * 8 + 8] = cw[tap].reshape(8, 128).T
    cfw = f("conv_ffn")
    for tap in range(3):
        vecs[:, V_CF + tap * NFB:V_CF + (tap + 1) * NFB] = cfw[tap].reshape(NFB, 128).T
    shared = {"win": win, "wgu": wgu, "wdn": wdn, "wout": wout, "lora": lora, "consts": _consts()}
    for n in ("g_pre_mix", "g_post_mix", "g_pre_ffn", "g_post_ffn", "ln_x_w", "ln_x_b"):
        shared[n] = f(n).reshape(1, -1)
    x = f("x")
    meta = f("meta_tokens")
    SEQ = NT * TILE
    in_maps = []
    for c in range(8):
        b, q = c // 4, c % 4
        seq = np.concatenate([meta, x[b]], 0)
        n_real = (q + 1) * RANGE
        xs = np.zeros((SEQ, D), np.float32)
        xs[SEQ - n_real:] = seq[:n_real]
        v = vecs.copy()
        v[:, V_HK] = 0.0 if q == 0 else 1.0
        m = dict(shared)
        m["xs"] = xs
        m["vecs"] = v
        in_maps.append(m)
    return in_maps


def kernel(**inputs):
    x = np.asarray(inputs["x"])
    B, L, _ = x.shape
    tot = N_META + L
    RANGE = tot // 4
    NT = -(-(tot + 2) // TILE)
    nc = build_nc(NT, RANGE)
    in_maps = prepare(inputs, NT, RANGE)
    dbg = os.environ.get("KCORES")
    if dbg is not None:
        sel = [int(t) for t in dbg.split(",")]
        res1 = run_bass_kernel_spmd(nc, [in_maps[c] for c in sel], core_ids=list(range(len(sel))))
        return {c: {k: np.asarray(v) for k, v in res1.results[i].items()} for i, c in enumerate(sel)}
    res = run_bass_kernel_spmd(nc, in_maps, core_ids=list(range(8)))
    SEQ = NT * TILE
    OC0 = (SEQ - RANGE - 2) // CH
    off = SEQ - RANGE - OC0 * CH
    full = np.zeros((B, tot, D), np.float32)
    for c in range(8):
        b, q = c // 4, c % 4
        o = np.asarray(res.results[c]["out"])
        full[b, q * RANGE:(q + 1) * RANGE] = o[off:off + RANGE]
    return np.ascontiguousarray(full[:, N_META:])
```

```python
import contextlib
import math
import numpy as np
import concourse.bass as bass
import concourse.mybir as mybir
from concourse.bass_utils import run_bass_kernel_spmd

F32 = mybir.dt.float32
BF16 = mybir.dt.bfloat16
ALU = mybir.AluOpType
AF = mybir.ActivationFunctionType
AX = mybir.AxisListType

D = 2048
KC = 16
TILE = 512
CH = 128
RW = 1024
DFF = 5632
NFB = DFF // 128
N_META = 16
RMS_EPS = 1e-6
GN_EPS = 64e-5
DECAY_C = math.exp(-0.5)
STC = 9

V_MU = 0
V_W0 = 28
V_A0 = 36
V_KK = 44
V_KA = 52
V_RK = 60
V_CW = 68
V_CF = 92
V_HK = 92 + 3 * NFB
NV = V_HK + 1
C_ID = 0
C_M4 = 128
C_ML = 640
C_BO = 1152
C_HS = 1280
NCONST = 1282


class Res:
    __slots__ = ("name", "w", "r", "dsem", "dcnt")

    def __init__(self, name):
        self.name = name
        self.w = None
        self.r = {}
        self.dsem = None
        self.dcnt = 0


class V:
    __slots__ = ("ap", "res")

    def __init__(self, ap, res):
        self.ap = ap
        self.res = res

    def __getitem__(self, idx):
        return V(self.ap[idx], self.res)

    def re(self, pat, **kw):
        return V(self.ap.rearrange(pat, **kw), self.res)

    def bitcast(self, dt):
        return V(self.ap.bitcast(dt), self.res)

    def bc(self, shape, axis):
        return V(self.ap.unsqueeze(axis).to_broadcast(list(shape)), self.res)


class Sched:
    ENG = ("pe", "act", "dve", "pool", "sp")

    def __init__(self, nc, stack):
        self.nc = nc
        self.stack = stack
        self.ops = {e: [] for e in self.ENG}
        self.sem = {e: stack.enter_context(nc.semaphore("s_" + e)) for e in self.ENG}
        self.cnt = {e: 0 for e in self.ENG}
        self.seen = {e: {} for e in self.ENG}
        self.allres = []

    def res(self, name):
        r = Res(name)
        self.allres.append(r)
        return r

    def _dsem(self, res):
        if res.dsem is None:
            res.dsem = self.stack.enter_context(self.nc.semaphore("d_" + res.name))
        return res.dsem

    def _deps(self, eng, reads, writes, seen_key=None):
        deps = {}

        def add(t, kind):
            if t is None:
                return
            key, val, en = t
            if en == eng and (kind != "raw" or eng == "pe"):
                return
            if deps.get(key, 0) < val:
                deps[key] = val

        for r in reads:
            add(r.w, "raw")
        for w in writes:
            add(w.w, "waw")
            for key, (val, en) in w.r.items():
                add((key, val, en), "war")
        out = []
        seen = self.seen[seen_key or eng]
        for key, val in deps.items():
            if seen.get(key, 0) >= val:
                continue
            seen[key] = val
            out.append((key, val))
        return out

    def op(self, eng, fn, reads=(), writes=()):
        if _HALT[0]:
            return
        waits = self._deps(eng, reads, writes)
        self.cnt[eng] += 1
        c = self.cnt[eng]
        key = ("c", eng)
        self.ops[eng].append((waits, fn, ("c", c)))
        for r in reads:
            r.r[key] = (c, eng)
        for w in writes:
            w.w = (key, c, eng)
            w.r = {}

    def dma(self, q, fn, reads=(), writes=()):
        if _HALT[0]:
            return
        waits = self._deps("dma", reads, writes, seen_key=q)
        anchor = writes[0] if writes else reads[0]
        sem = self._dsem(anchor)
        anchor.dcnt += 16
        c = anchor.dcnt
        key = ("d", sem)
        self.ops[q].append((waits, fn, ("d", sem)))
        for r in reads:
            r.r[key] = (c, "dma")
        for w in writes:
            w.w = (key, c, "dma")
            w.r = {}

    def barrier(self):
        for e in self.ENG:
            waits = self._deps(e + "_bar", self.allres, self.allres, seen_key=e)
            if waits:
                self.ops[e].append((waits, None, None))

    def emit(self):
        nc = self.nc
        ops = self.ops
        needed = {e: set() for e in self.ENG}
        for e in self.ENG:
            for waits, fn, inc in ops[e]:
                for key, val in waits:
                    if key[0] == "c":
                        needed[key[1]].add(val)
        rank = {e: {v: i + 1 for i, v in enumerate(sorted(needed[e]))} for e in self.ENG}
        sem = self.sem

        def run(engh, lst):
            for waits, fn, inc in lst:
                for key, val in waits:
                    if key[0] == "c":
                        engh.wait_ge(sem[key[1]], rank[key[1]][val])
                    else:
                        engh.wait_ge(key[1], val)
                if fn is not None:
                    ins = fn(engh)
                    if inc[0] == "d":
                        ins.then_inc(inc[1], 16)
                    elif inc[1] in rank[self._cur]:
                        ins.then_inc(sem[self._cur], 1)

        with nc.Block() as block:
            @block.tensor
            def _(e):
                self._cur = "pe"
                run(e, ops["pe"])

            @block.scalar
            def _(e):
                self._cur = "act"
                run(e, ops["act"])

            @block.vector
            def _(e):
                self._cur = "dve"
                run(e, ops["dve"])

            @block.gpsimd
            def _(e):
                self._cur = "pool"
                run(e, ops["pool"])

            @block.sync
            def _(e):
                self._cur = "sp"
                run(e, ops["sp"])


def _rw(reads, writes):
    rs, ws = [], []
    for v in reads:
        if isinstance(v, V) and v.res not in rs:
            rs.append(v.res)
    for v in writes:
        if v.res not in ws:
            ws.append(v.res)
    return rs, ws


def _a(x):
    return x.ap if isinstance(x, V) else x


class KB:
    def __init__(self, nc, S):
        self.nc = nc
        self.S = S

    def mm(self, out, lhsT, rhs, start=True, stop=True):
        rs, ws = _rw([lhsT, rhs], [out])
        self.S.op("pe", lambda e: e.matmul(out.ap, lhsT=lhsT.ap, rhs=rhs.ap, start=start, stop=stop), rs, ws)

    def tr(self, out, in_, ident):
        rs, ws = _rw([in_, ident], [out])
        self.S.op("pe", lambda e: e.transpose(out=out.ap, in_=in_.ap, identity=ident.ap), rs, ws)

    def act(self, out, in_, func, bias=None, scale=None, accum=None):
        rs, ws = _rw([in_, bias, scale], [out] + ([accum] if accum is not None else []))
        kw = {}
        if bias is not None:
            kw["bias"] = _a(bias)
        if scale is not None:
            kw["scale"] = _a(scale)
        if accum is not None:
            kw["accum_out"] = accum.ap
        self.S.op("act", lambda e: e.activation(out=out.ap, in_=in_.ap, func=func, **kw), rs, ws)

    def copy(self, eng, out, in_):
        rs, ws = _rw([in_], [out])
        if eng == "act":
            self.S.op("act", lambda e: e.activation(out=out.ap, in_=in_.ap, func=AF.Copy), rs, ws)
        else:
            self.S.op(eng, lambda e: e.tensor_copy(out=out.ap, in_=in_.ap), rs, ws)

    def tt(self, eng, out, a, b, op):
        rs, ws = _rw([a, b], [out])
        self.S.op(eng, lambda e: e.tensor_tensor(out=out.ap, in0=a.ap, in1=b.ap, op=op), rs, ws)

    def ts(self, eng, out, a, s1, s2=None, op0=ALU.mult, op1=None):
        rs, ws = _rw([a, s1, s2], [out])
        if op1 is None:
            self.S.op(eng, lambda e: e.tensor_scalar(out=out.ap, in0=a.ap, scalar1=_a(s1), scalar2=0.0, op0=op0, op1=ALU.add), rs, ws)
        else:
            self.S.op(eng, lambda e: e.tensor_scalar(out=out.ap, in0=a.ap, scalar1=_a(s1), scalar2=_a(s2), op0=op0, op1=op1), rs, ws)

    def stt(self, eng, out, a, scalar, b, op0, op1):
        rs, ws = _rw([a, scalar, b], [out])
        self.S.op(eng, lambda e: e.scalar_tensor_tensor(out=out.ap, in0=a.ap, scalar=_a(scalar), in1=b.ap, op0=op0, op1=op1), rs, ws)

    def red(self, eng, out, in_, op=ALU.add):
        rs, ws = _rw([in_], [out])
        self.S.op(eng, lambda e: e.tensor_reduce(out=out.ap, in_=in_.ap, axis=AX.X, op=op), rs, ws)

    def scan(self, out, d0, d1):
        rs, ws = _rw([d0, d1], [out])
        self.S.op("dve", lambda e: e.tensor_tensor_scan(out=out.ap, data0=d0.ap, data1=d1.ap, initial=0.0,
                                                        op0=ALU.mult, op1=ALU.add), rs, ws)

    def recip(self, out, in_):
        rs, ws = _rw([in_], [out])
        self.S.op("dve", lambda e: e.reciprocal(out=out.ap, in_=in_.ap), rs, ws)

    def memset(self, eng, out, val):
        rs, ws = _rw([], [out])
        self.S.op(eng, lambda e: e.memset(out.ap, val), rs, ws)

    def dma(self, q, out, in_):
        rs, ws = _rw([in_], [out])
        self.S.dma(q, lambda e: e.dma_start(out=out.ap, in_=in_.ap), rs, ws)


import os
_STOP = int(os.environ.get("KSTOP", "99"))


class _Stop(Exception):
    pass


_HALT = [False]


def _ck(k):
    if _STOP == k:
        _HALT[0] = True


def build_nc(NT, RANGE):
    SEQ = NT * TILE
    OWN0 = SEQ - RANGE - 2
    OC0 = OWN0 // CH
    NCHK = SEQ // CH
    NOC = NCHK - OC0
    HALO = OWN0 - OC0 * CH + 2
    NTOK2 = NOC * CH
    OT0 = OC0 // 4

    nc = bass.Bass("TRN2", target_bir_lowering=False)
    _HALT[0] = False
    dr = {}

    def din(name, shape):
        dr[name] = nc.dram_tensor(name, list(shape), F32, kind="ExternalInput").ap()
        return dr[name]

    xs = din("xs", [SEQ, D])
    win = din("win", [52, 128, KC * 128])
    wgu = din("wgu", [NFB * 2, 128, KC * 128])
    wdn = din("wdn", [NFB, 128, D])
    wout = din("wout", [128, KC * D])
    lora = din("lora", [128, 4 * RW])
    vecs_d = din("vecs", [128, NV])
    const_d = din("consts", [128, NCONST])
    bcv = {n: din(n, [1, w]) for n, w in (("g_pre_mix", D), ("g_post_mix", D), ("g_pre_ffn", D),
                                          ("g_post_ffn", D), ("ln_x_w", RW), ("ln_x_b", RW))}
    out_d = nc.dram_tensor("out", [NTOK2, D], F32, kind="ExternalOutput").ap()
    yT_d = nc.dram_tensor("yT_s", [KC, 128, NTOK2], BF16, kind="Internal").ap()
    n2T_d = nc.dram_tensor("n2T_s", [KC, 128, NTOK2], BF16, kind="Internal").ap()
    h_d = nc.dram_tensor("h_s", [NTOK2, D], F32, kind="Internal").ap()

    with contextlib.ExitStack() as top:
        S = Sched(nc, top)
        K = KB(nc, S)
        R = S.res
        xs_v = V(xs, R("xs"))
        win_v = V(win, R("win"))
        wgu_v = V(wgu, R("wgu"))
        wdn_v = V(wdn, R("wdn"))
        wout_v = V(wout, R("wout"))
        lora_v = V(lora, R("lora"))
        out_v = V(out_d, R("out"))
        yT_v = V(yT_d, R("yT"))
        n2T_v = V(n2T_d, R("n2T"))
        h_v = V(h_d, R("h"))

        banks = []
        for i in range(8):
            t = top.enter_context(nc.psum_tensor("pb%d" % i, [128, 512], F32))
            banks.append(V(t[:, :], R("pb%d" % i)))
        bank_i = [0]

        def psum():
            b = banks[bank_i[0] % 8]
            bank_i[0] += 1
            return b

        def sbt(stack, name, shape, dt):
            t = stack.enter_context(nc.sbuf_tensor("sb_" + name, list(shape), dt))
            return V(t[tuple(slice(None) for _ in shape)], R(name))

        vecs = sbt(top, "vecs", [128, NV], F32)
        vder = sbt(top, "vder", [128, 28 + 8], F32)
        cf = sbt(top, "cf", [128, NCONST], F32)
        cb = sbt(top, "cb", [128, NCONST], BF16)
        ones = sbt(top, "ones", [128, 128], F32)
        K.dma("sp", vecs, V(vecs_d, R("vecs_d")))
        K.dma("sp", cf, V(const_d, R("const_d")))
        K.dma("pool", cb, V(const_d, R("const_d2")))
        K.memset("dve", ones, 1.0)
        K.ts("dve", vder[:, 0:28], vecs[:, V_MU:V_MU + 28], -1.0, 1.0, ALU.mult, ALU.add)
        K.ts("dve", vder[:, 28:36], vecs[:, V_KA:V_KA + 8], -1.0, 1.0, ALU.mult, ALU.add)
        ident_b = cb[:, C_ID:C_ID + 128]
        mask4 = cb[:, C_M4:C_M4 + 512]
        maskL4 = cb[:, C_ML:C_ML + 512]
        blockones = cf[:, C_BO:C_BO + 128]
        headsel_b = cb[:, C_HS:C_HS + 2]
        hk = vecs[:, V_HK:V_HK + 1]

        def rms_rstd(src, rstd, junk, nbanks=None):
            pass

        try:
          _ck(1)
          with contextlib.ExitStack() as p1:
              gpre = sbt(p1, "gpre", [128, D], F32)
              lnw = sbt(p1, "lnw", [128, RW], F32)
              lnb = sbt(p1, "lnb", [128, RW], F32)
              K.dma("sp", gpre, V(bcv["g_pre_mix"].partition_broadcast(128), R("bc0")))
              K.dma("sp", lnw, V(bcv["ln_x_w"].partition_broadcast(128), R("bc1")))
              K.dma("sp", lnb, V(bcv["ln_x_b"].partition_broadcast(128), R("bc2")))
              lw_b = sbt(p1, "lw_b", [128, 4 * RW], BF16)
              K.dma("pool", lw_b, lora_v)
              wring = [sbt(p1, "wr%d" % i, [128, KC, 128], BF16) for i in range(4)]
              wr_i = [0]
              xin = sbt(p1, "xin", [128, D], F32)
              xn = sbt(p1, "xn", [128, D], BF16)
              st4 = sbt(p1, "st4", [128, 8], F32)
              nT = sbt(p1, "nT", [128, KC, TILE], BF16)
              carry = sbt(p1, "carry", [128, 28], F32)
              K.memset("dve", carry, 0.0)
              ccar = sbt(p1, "ccar", [128, 8, 2], F32)
              K.memset("dve", ccar, 0.0)
              tw = sbt(p1, "tw", [128, TILE], BF16)
              adb = sbt(p1, "adb", [128, TILE], BF16)
              sg2 = [sbt(p1, "sg%d" % i, [128, 2, TILE], BF16) for i in range(2)]
              ztmp = sbt(p1, "ztmp", [128, TILE], F32)
              zr = sbt(p1, "zr", [128, TILE], F32)
              zk = sbt(p1, "zk", [128, TILE], F32)
              zv = sbt(p1, "zv", [128, TILE], F32)
              fA = [sbt(p1, "fA%d" % i, [128, TILE], F32) for i in range(8)]
              AR2 = [sbt(p1, "AR%d" % i, [128, 2, 4, 256], BF16) for i in range(2)]
              BK2 = [sbt(p1, "BK%d" % i, [128, 2, 4, 256], BF16) for i in range(2)]
              TF2 = [sbt(p1, "TF%d" % i, [128, 2, 3, TILE], BF16) for i in range(2)]
              rkb2 = [sbt(p1, "rkb%d" % i, [128, 2, TILE], BF16) for i in range(2)]
              PC2 = [sbt(p1, "PCd%d" % i, [128, 2, 4], F32) for i in range(2)]
              TOK = [sbt(p1, "TOK%d" % i, [128, 2, 4, 128], BF16) for i in range(2)]
              SA = [sbt(p1, "SA%d" % i, [128, 4, 512], BF16) for i in range(2)]
              SN = [[sbt(p1, "SN%d_%d" % (i, j), [128, 4, 128], BF16) for j in range(2)] for i in range(2)]
              SL = [[sbt(p1, "SL%d_%d" % (i, j), [128, 4, 128], BF16) for j in range(2)] for i in range(2)]
              WW = [[sbt(p1, "WW%d_%d" % (i, j), [128, 4, 128], BF16) for j in range(2)] for i in range(2)]
              GT = [sbt(p1, "GT%d" % i, [128, 2, 64], BF16) for i in range(2)]
              RH = [sbt(p1, "RH%d" % i, [128, 2, 128], BF16) for i in range(2)]
              Tst = sbt(p1, "Tst", [128, 8, 64], F32)
              Tbf = [sbt(p1, "Tbf%d" % i, [128, 8, 64], BF16) for i in range(2)]
              K.memset("dve", Tst, 0.0)
              K.memset("dve", Tbf[0], 0.0)
              K.memset("dve", Tbf[1], 0.0)
              tb_i = [[0] * 8]
              ysb = sbt(p1, "ysb", [128, 256], F32)
              ysq = sbt(p1, "ysq", [128, 256], F32)
              yn = sbt(p1, "yn", [128, 256], F32)
              bon = sbt(p1, "bon", [128, 256], F32)
              gst = sbt(p1, "gst", [128, 24], F32)
              ytb = sbt(p1, "ytb", [128, 256], BF16)
              yTs = sbt(p1, "yTs", [128, 2, TILE], BF16)
              csb = sbt(p1, "csb", [128, TILE], F32)
              chb = sbt(p1, "chb", [128, TILE + 2], F32)
              ycv = sbt(p1, "ycv", [128, TILE], BF16)

              def load_w(cbi):
                  w = wring[wr_i[0] % 4]
                  wr_i[0] += 1
                  K.dma("pool", w.re("p k j -> p (k j)"), win_v[cbi])
                  return w

              def project(w, t_cols=TILE):
                  pz = psum()
                  for kc in range(KC):
                      K.mm(pz[:, 0:t_cols], w[:, kc, :], nT[:, kc, 0:t_cols], start=(kc == 0), stop=(kc == KC - 1))
                  return pz

              def shift(pz, col, out):
                  mu = vecs[:, V_MU + col:V_MU + col + 1]
                  omu = vder[:, col:col + 1]
                  K.act(ztmp, pz, AF.Copy, scale=omu)
                  K.stt("dve", out[:, 1:TILE], pz[:, 0:TILE - 1], mu, ztmp[:, 1:TILE], ALU.mult, ALU.add)
                  K.stt("dve", out[:, 0:1], carry[:, col:col + 1], mu, ztmp[:, 0:1], ALU.mult, ALU.add)
                  K.copy("act", carry[:, col:col + 1], pz[:, TILE - 1:TILE])

              def front(ti):
                  t0 = ti * TILE
                  sg = sg2[ti % 2]
                  for j in range(4):
                      K.dma("sp", xin, xs_v[t0 + j * CH:t0 + (j + 1) * CH, :])
                      K.memset("dve", st4[:, 0:1], 0.0)
                      K.act(xn, xin, AF.Square, accum=st4[:, 0:1])
                      K.ts("dve", st4[:, 1:2], st4[:, 0:1], 1.0 / D, RMS_EPS, ALU.mult, ALU.add)
                      K.act(st4[:, 2:3], st4[:, 1:2], AF.Sqrt)
                      K.recip(st4[:, 3:4], st4[:, 2:3])
                      K.stt("dve", xn, xin, st4[:, 3:4], gpre, ALU.mult, ALU.mult)
                      yield
                      for a in range(2):
                          pb = psum()
                          pbb = pb.bitcast(BF16)
                          for k8 in range(8):
                              kc = a * 8 + k8
                              K.tr(pbb[:, k8 * 128:(k8 + 1) * 128], xn[:, kc * 128:(kc + 1) * 128], ident_b)
                          K.copy("act" if a == 0 else "dve", nT[:, a * 8:(a + 1) * 8, j * CH:(j + 1) * CH],
                                 pbb.re("p (k t) -> p k t", k=8))
                          yield
                  pz = project(load_w(0))
                  shift(pz, 0, zr)
                  K.act(tw, zr, AF.Tanh)
                  yield
                  pz = project(load_w(1))
                  shift(pz, 1, zr)
                  K.copy("act", adb, zr)
                  yield
                  for gi in range(2):
                      if ti < OT0 - 1:
                          break
                      pz = project(load_w(2 + gi))
                      shift(pz, 2 + gi, zr)
                      K.act(sg[:, gi, :], zr, AF.Sigmoid)
                      yield

              def prep(ti, hg):
                  out_tile = ti >= OT0
                  par = (ti * 4 + hg) % 2
                  AR, BK, TF, rkb, PC = AR2[par], BK2[par], TF2[par], rkb2[par], PC2[par]
                  for s in range(2):
                      hp = hg * 2 + s
                      for qi, dst in enumerate((zr, zk, zv)):
                          if qi == 0 and ti < OT0 - 1:
                              continue
                          pz = project(load_w(4 + hp * 3 + qi))
                          shift(pz, 4 + hp * 3 + qi, dst)
                          yield
                      plw = psum()
                      K.mm(plw, lw_b[:, hp * 128:(hp + 1) * 128], tw)
                      pa = psum()
                      K.mm(pa, lw_b[:, RW + hp * 128:RW + (hp + 1) * 128], adb)
                      sw, aa, cs, kk, t1, t2, t3, t4 = fA
                      K.act(sw, plw, AF.Sigmoid, bias=vecs[:, V_W0 + hp:V_W0 + hp + 1])
                      K.act(aa, pa, AF.Sigmoid, bias=vecs[:, V_A0 + hp:V_A0 + hp + 1])
                      yield
                      for c in range(4):
                          K.scan(cs[:, c * CH:(c + 1) * CH], ones, sw[:, c * CH:(c + 1) * CH])
                      K.ts("dve", kk, zk, vecs[:, V_KK + hp:V_KK + hp + 1])
                      K.tt("dve", t1, kk, kk, ALU.mult)
                      yield
                      pss = psum()
                      K.mm(pss, blockones, t1)
                      K.act(t1, pss, AF.Sqrt)
                      K.ts("dve", t1, t1, 1e-12, None, ALU.max)
                      K.recip(t2, t1)
                      K.tt("dve", kk, kk, t2, ALU.mult)
                      yield
                      K.ts("dve", t1, aa, vecs[:, V_KA + hp:V_KA + hp + 1], vder[:, 28 + hp:29 + hp], ALU.mult, ALU.add)
                      K.tt("dve", t1, zk, t1, ALU.mult)
                      K.tt("dve", aa, kk, aa, ALU.mult)
                      yield
                      K.tt("dve", t2, cs, sw, ALU.subtract)
                      K.act(t2, t2, AF.Exp, scale=-DECAY_C)
                      K.stt("dve", AR[:, s, :, 0:128], kk.re("p (c t) -> p c t", c=4), -1.0,
                            t2.re("p (c t) -> p c t", c=4), ALU.mult, ALU.mult)
                      yield
                      if out_tile:
                          K.act(t3, cs, AF.Exp, scale=-DECAY_C)
                          K.tt("dve", AR[:, s, :, 128:256], zr.re("p (c t) -> p c t", c=4),
                               t3.re("p (c t) -> p c t", c=4), ALU.mult)
                      yield
                      K.act(t3, cs, AF.Exp, scale=DECAY_C)
                      K.tt("dve", BK[:, s, :, 0:128], aa.re("p (c t) -> p c t", c=4),
                           t3.re("p (c t) -> p c t", c=4), ALU.mult)
                      K.tt("dve", BK[:, s, :, 128:256], t1.re("p (c t) -> p c t", c=4),
                           t3.re("p (c t) -> p c t", c=4), ALU.mult)
                      yield
                      K.ts("dve", st4[:, 4:8], cs.re("p (c t) -> p c t", c=4)[:, :, CH - 1], -DECAY_C)
                      K.act(PC[:, s, :], st4[:, 4:8], AF.Exp)
                      for c in range(4):
                          K.act(t4[:, c * CH:(c + 1) * CH], cs[:, c * CH:(c + 1) * CH], AF.Exp,
                                scale=DECAY_C, bias=st4[:, 4 + c:5 + c])
                      yield
                      K.tt("dve", TF[:, s, 1, :], aa, t4, ALU.mult)
                      K.tt("dve", TF[:, s, 2, :], t1, t4, ALU.mult)
                      K.copy("act", TF[:, s, 0, :], zv)
                      if out_tile:
                          K.tt("dve", t2, zr, t1, ALU.mult)
                          K.ts("dve", rkb[:, s, :], t2, vecs[:, V_RK + hp:V_RK + hp + 1])
                      yield

              def scan(ti, hg):
                  out_tile = ti >= OT0
                  par = (ti * 4 + hg) % 2
                  AR, BK, TF, rkb, PC = AR2[par], BK2[par], TF2[par], rkb2[par], PC2[par]
                  sg = sg2[ti % 2]
                  for cp in range(2):
                      cl = [2 * cp, 2 * cp + 1]
                      for u, c in enumerate(cl):
                          pb = psum()
                          pbb = pb.bitcast(BF16).re("p (s q t) -> p s q t", s=2, q=4)
                          for s in range(2):
                              for q in range(3):
                                  K.tr(pbb[:, s, q, :], TF[:, s, q, c * CH:(c + 1) * CH], ident_b)
                              K.tr(pbb[:, s, 3, :], AR[:, s, c, 0:128], ident_b)
                          K.copy("act", TOK[u], pbb)
                          yield
                      for u, c in enumerate(cl):
                          for hl in range(4):
                              s, e = hl // 2, hl % 2
                              ps_ = slice(64 * e, 64 * e + 64)
                              pa_ = psum()
                              if out_tile:
                                  K.mm(pa_[:, 0:256], BK[ps_, s, c, 0:128], AR[ps_, s, c, :])
                                  K.mm(pa_[:, 256:512], BK[ps_, s, c, 128:256], AR[ps_, s, c, :])
                                  K.tt("dve", SA[u][:, hl, :], pa_, mask4, ALU.mult)
                              else:
                                  K.mm(pa_[:, 0:128], BK[ps_, s, c, 0:128], AR[ps_, s, c, 0:128])
                                  K.mm(pa_[:, 256:384], BK[ps_, s, c, 128:256], AR[ps_, s, c, 0:128])
                                  K.tt("dve", SA[u][:, hl, :].re("p (a b) -> p a b", a=2)[:, :, 0:128],
                                       pa_.re("p (a b) -> p a b", a=2)[:, :, 0:128],
                                       mask4.re("p (a b) -> p a b", a=2)[:, :, 0:128], ALU.mult)
                          yield
                          ple = [psum(), psum()]
                          for hl in range(4):
                              s, e = hl // 2, hl % 2
                              ps_ = slice(64 * e, 64 * e + 64)
                              K.mm(ple[e][:, s * 128:(s + 1) * 128], AR[ps_, s, c, 0:128], BK[ps_, s, c, 0:128])
                          for e in range(2):
                              K.tt("dve", SL[u][0].re("p (s e) t -> p s e t", e=2)[:, :, e, :],
                                   ple[e][:, 0:256].re("p (s t) -> p s t", s=2),
                                   maskL4[:, 0:256].re("p (s t) -> p s t", s=2), ALU.mult)
                          px = psum()
                          for hl in range(4):
                              s, e = hl // 2, hl % 2
                              K.mm(px[:, hl * 64:(hl + 1) * 64], SA[u][:, hl, 256:384], TOK[u][:, s, 0, 64 * e:64 * e + 64])
                          K.copy("act", WW[u][0][:, :, 64:128], px[:, 0:256].re("p (h v) -> p h v", h=4))
                          K.copy("act", WW[u][0].re("p (s e) k -> p s e k", s=2)[:, :, :, 0:64],
                                 TOK[u][:, :, 3, :].re("p s (e k) -> p s e k", e=2))
                          yield
                      for i in range(7):
                          for u, c in enumerate(cl):
                              n_cur = SA[u][:, :, 0:128] if i == 0 else SN[u][i % 2]
                              l_cur = SL[u][i % 2]
                              w_cur = WW[u][i % 2]
                              pw = psum()
                              for hl in range(4):
                                  K.mm(pw[:, hl * 128:(hl + 1) * 128], n_cur[:, hl, :], w_cur[:, hl, :])
                              K.tt("dve", WW[u][(i + 1) % 2].re("p h t -> p (h t)"), pw,
                                   w_cur.re("p h t -> p (h t)"), ALU.add)
                              if i < 6:
                                  pn = psum()
                                  for hl in range(4):
                                      K.mm(pn[:, hl * 128:(hl + 1) * 128], l_cur[:, hl, :], n_cur[:, hl, :])
                                  K.copy("act", SN[u][(i + 1) % 2].re("p h t -> p (h t)"), pn)
                              if i < 5:
                                  pl_ = psum()
                                  for hl in range(4):
                                      K.mm(pl_[:, hl * 128:(hl + 1) * 128], n_cur[:, hl, :], l_cur[:, hl, :])
                                  K.copy("act", SL[u][(i + 1) % 2].re("p h t -> p (h t)"), pl_)
                              yield
                      for u, c in enumerate(cl):
                          Wf = WW[u][1]
                          chunk_g = ti * 4 + c
                          want_y = chunk_g >= OC0
                          pg = psum()
                          for hl in range(4):
                              s, e = hl // 2, hl % 2
                              K.mm(pg[64 * e:64 * e + 64, s * 64:(s + 1) * 64], Wf[:, hl, 0:64],
                                   TOK[u][:, s, 1, 64 * e:64 * e + 64])
                          K.copy("act", GT[u].re("p s k -> p (s k)"), pg[:, 0:128])
                          tcur = Tbf[tb_i[0][hg] % 2]
                          tnew = Tbf[(tb_i[0][hg] + 1) % 2]
                          if want_y:
                              pr = psum()
                              for hl in range(4):
                                  s, e = hl // 2, hl % 2
                                  K.mm(pr[64 * e:64 * e + 64, s * 128:(s + 1) * 128], Wf[:, hl, 0:64], SA[u][:, hl, 128:256])
                              K.tt("dve", RH[u], pr[:, 0:256].re("p (s t) -> p s t", s=2), AR[:, :, c, 128:256], ALU.add)
                              py = psum()
                              for hl in range(4):
                                  s, e = hl // 2, hl % 2
                                  ps_ = slice(64 * e, 64 * e + 64)
                                  o = py[:, hl * 64:(hl + 1) * 64]
                                  K.mm(o, SA[u][:, hl, 128:256], Wf[:, hl, 64:128], start=True, stop=False)
                                  K.mm(o, SA[u][:, hl, 384:512], TOK[u][:, s, 0, 64 * e:64 * e + 64], start=False, stop=False)
                                  K.mm(o, RH[u][ps_, s, :], tcur[ps_, hg * 2 + s, :], start=False, stop=True)
                          pt = psum()
                          for hl in range(4):
                              s, e = hl // 2, hl % 2
                              ps_ = slice(64 * e, 64 * e + 64)
                              o = pt[ps_, s * 64:(s + 1) * 64]
                              K.mm(o, TOK[u][:, s, 1, 64 * e:64 * e + 64], Wf[:, hl, 64:128], start=True, stop=False)
                              K.mm(o, TOK[u][:, s, 2, 64 * e:64 * e + 64], TOK[u][:, s, 0, 64 * e:64 * e + 64], start=False, stop=False)
                              K.mm(o, GT[u][ps_, s, :], tcur[ps_, hg * 2 + s, :], start=False, stop=True)
                          for s in range(2):
                              K.stt("dve", Tst[:, hg * 2 + s, :], Tst[:, hg * 2 + s, :], PC[:, s, c:c + 1],
                                    pt[:, s * 64:(s + 1) * 64], ALU.mult, ALU.add)
                          K.copy("act", tnew[:, hg * 2:hg * 2 + 2, :], Tst[:, hg * 2:hg * 2 + 2, :])
                          tb_i[0][hg] += 1
                          if want_y:
                              K.copy("act", ysb, py[:, 0:256])
                          yield
                          if not want_y:
                              continue
                          pgt = psum()
                          K.mm(pgt[:, 0:256], sg[:, 0, c * CH:(c + 1) * CH], lw_b[:, 2 * RW + hg * 256:2 * RW + (hg + 1) * 256],
                               start=True, stop=False)
                          K.mm(pgt[:, 0:256], sg[:, 1, c * CH:(c + 1) * CH], lw_b[:, 3 * RW + hg * 256:3 * RW + (hg + 1) * 256],
                               start=False, stop=True)
                          for s in range(2):
                              K.mm(pgt[:, 256 + 2 * s:258 + 2 * s], rkb[:, s, c * CH:(c + 1) * CH], headsel_b)
                          K.copy("act", gst[:, 16:20], pgt[:, 256:260])
                          y3 = ysb.re("p (h v) -> p h v", h=4)
                          K.red("dve", gst[:, 0:4], y3)
                          K.tt("dve", ysq, ysb, ysb, ALU.mult)
                          K.red("dve", gst[:, 4:8], ysq.re("p (h v) -> p h v", h=4))
                          K.ts("dve", gst[:, 0:4], gst[:, 0:4], 1.0 / 64)
                          K.tt("dve", gst[:, 8:12], gst[:, 0:4], gst[:, 0:4], ALU.mult)
                          K.stt("dve", gst[:, 4:8], gst[:, 4:8], 1.0 / 64, gst[:, 8:12], ALU.mult, ALU.subtract)
                          K.ts("dve", gst[:, 4:8], gst[:, 4:8], GN_EPS, None, ALU.add)
                          K.act(gst[:, 8:12], gst[:, 4:8], AF.Sqrt)
                          K.recip(gst[:, 12:16], gst[:, 8:12])
                          yn3 = yn.re("p (h v) -> p h v", h=4)
                          K.tt("dve", yn3, y3, gst[:, 0:4].bc([128, 4, 64], 2), ALU.subtract)
                          K.tt("dve", yn3, yn3, gst[:, 12:16].bc([128, 4, 64], 2), ALU.mult)
                          K.tt("dve", yn, yn, lnw[:, hg * 256:(hg + 1) * 256], ALU.mult)
                          K.tt("dve", yn, yn, lnb[:, hg * 256:(hg + 1) * 256], ALU.add)
                          K.tt("dve", bon.re("p (s e v) -> p s e v", s=2, e=2),
                               TOK[u][:, :, 0, :].re("p s (e v) -> p s e v", e=2),
                               gst[:, 16:20].re("p (s e) -> p s e", s=2).bc([128, 2, 2, 64], 3), ALU.mult)
                          K.tt("dve", yn, yn, bon, ALU.add)
                          K.tt("dve", ytb, yn, pgt[:, 0:256], ALU.mult)
                          pb = psum()
                          pbb = pb.bitcast(BF16)
                          for s in range(2):
                              K.tr(pbb[:, s * 128:(s + 1) * 128], ytb[:, s * 128:(s + 1) * 128], ident_b)
                          K.copy("act", yTs[:, :, c * CH:(c + 1) * CH], pbb[:, 0:256].re("p (s t) -> p s t", s=2))
                          yield
                  if out_tile:
                      c_lo = max(0, OC0 - ti * 4)
                      tk0 = (ti * 4 + c_lo - OC0) * CH
                      ncol = (4 - c_lo) * CH
                      for s in range(2):
                          K.dma("sp", yT_v[8 + hg * 2 + s, :, tk0:tk0 + ncol], yTs[:, s, c_lo * CH:4 * CH])
                  yield

              def conv(ti):
                  c_lo = max(0, OC0 - ti * 4)
                  tk0 = (ti * 4 + c_lo - OC0) * CH
                  ncol = (4 - c_lo) * CH
                  for i in range(8):
                      pB = project(load_w(28 + i * 3 + 0))
                      pC = project(load_w(28 + i * 3 + 1))
                      pH = project(load_w(28 + i * 3 + 2))
                      K.copy("act", csb, pC)
                      K.copy("act", chb[:, 0:2], ccar[:, i, :])
                      K.tt("dve", chb[:, 2:TILE + 2], csb, pH, ALU.mult)
                      K.copy("act", ccar[:, i, :], chb[:, TILE:TILE + 2])
                      K.ts("dve", csb, chb[:, 0:TILE], vecs[:, V_CW + i:V_CW + i + 1])
                      K.stt("dve", csb, chb[:, 1:TILE + 1], vecs[:, V_CW + 8 + i:V_CW + 9 + i], csb, ALU.mult, ALU.add)
                      K.stt("dve", csb, chb[:, 2:TILE + 2], vecs[:, V_CW + 16 + i:V_CW + 17 + i], csb, ALU.mult, ALU.add)
                      K.tt("dve", ycv, csb, pB, ALU.mult)
                      K.dma("sp", yT_v[i, :, tk0:tk0 + ncol], ycv[:, c_lo * CH:4 * CH])
                      yield

              def chain(*gens):
                  for g in gens:
                      yield from g

              def interleave(ga, gb):
                  la, lb = True, True
                  while la or lb:
                      if la:
                          try:
                              next(ga)
                          except StopIteration:
                              la = False
                      if lb:
                          try:
                              next(gb)
                          except StopIteration:
                              lb = False

              PIPE = os.environ.get("KPIPE", "1") == "1"
              units = [(ti, hg) for ti in range(NT) for hg in range(4)]

              def p_stream(ti, hg):
                  gens = []
                  if hg == 0:
                      gens.append(front(ti))
                  gens.append(prep(ti, hg))
                  return gens

              for g in chain(*p_stream(0, 0)):
                  pass
              for ui, (ti, hg) in enumerate(units):
                  nxt = []
                  if hg == 3 and ti >= OT0:
                      nxt.append(conv(ti))
                  if ui + 1 < len(units):
                      nxt += p_stream(*units[ui + 1])
                  if PIPE:
                      interleave(scan(ti, hg), chain(*nxt))
                  else:
                      for g in scan(ti, hg):
                          pass
                      for g in chain(*nxt):
                          pass
          _ck(8)
          S.barrier()

          with contextlib.ExitStack() as p15:
              wo_b = sbt(p15, "wo_b", [128, KC, D], BF16)
              K.dma("pool", wo_b.re("p k d -> p (k d)"), wout_v)
              gpm = sbt(p15, "gpm", [128, D], F32)
              gpf = sbt(p15, "gpf", [128, D], F32)
              K.dma("sp", gpm, V(bcv["g_post_mix"].partition_broadcast(128), R("bc3")))
              K.dma("sp", gpf, V(bcv["g_pre_ffn"].partition_broadcast(128), R("bc4")))
              ych = [sbt(p15, "ych%d" % i, [128, KC, CH], BF16) for i in range(2)]
              xin2 = [sbt(p15, "xin2_%d" % i, [128, D], F32) for i in range(2)]
              hsb = [sbt(p15, "hsb%d" % i, [128, D], F32) for i in range(2)]
              n2 = sbt(p15, "n2", [128, D], BF16)
              n2T = [sbt(p15, "n2T%d" % i, [128, KC, CH], BF16) for i in range(2)]
              st5 = sbt(p15, "st5", [128, 16], F32)
              for oc in range(NOC):
                  yc = ych[oc % 2]
                  xi = xin2[oc % 2]
                  hs = hsb[oc % 2]
                  nt2 = n2T[oc % 2]
                  K.dma("sp", yc, yT_v[:, :, oc * CH:(oc + 1) * CH].re("c p t -> p c t"))
                  K.dma("sp", xi, xs_v[(OC0 + oc) * CH:(OC0 + oc + 1) * CH, :])
                  pm = [psum() for _ in range(4)]
                  for blk in range(4):
                      for chc in range(KC):
                          K.mm(pm[blk], yc[:, chc, :], wo_b[:, chc, blk * 512:(blk + 1) * 512],
                               start=(chc == 0), stop=(chc == KC - 1))
                  K.memset("dve", st5[:, 0:4], 0.0)
                  for blk in range(4):
                      K.act(n2[:, blk * 512:(blk + 1) * 512], pm[blk], AF.Square, accum=st5[:, blk:blk + 1])
                  K.red("dve", st5[:, 4:5], st5[:, 0:4])
                  K.ts("dve", st5[:, 4:5], st5[:, 4:5], 1.0 / D, RMS_EPS, ALU.mult, ALU.add)
                  K.act(st5[:, 5:6], st5[:, 4:5], AF.Sqrt)
                  K.recip(st5[:, 6:7], st5[:, 5:6])
                  for blk in range(4):
                      sl = slice(blk * 512, (blk + 1) * 512)
                      K.stt("dve", hs[:, sl], pm[blk], st5[:, 6:7], gpm[:, sl], ALU.mult, ALU.mult)
                  K.tt("dve", hs, hs, xi, ALU.add)
                  K.dma("sp", h_v[oc * CH:(oc + 1) * CH, :], hs)
                  K.memset("dve", st5[:, 8:9], 0.0)
                  K.act(n2, hs, AF.Square, accum=st5[:, 8:9])
                  K.ts("dve", st5[:, 9:10], st5[:, 8:9], 1.0 / D, RMS_EPS, ALU.mult, ALU.add)
                  K.act(st5[:, 10:11], st5[:, 9:10], AF.Sqrt)
                  K.recip(st5[:, 11:12], st5[:, 10:11])
                  K.stt("dve", n2, hs, st5[:, 11:12], gpf, ALU.mult, ALU.mult)
                  for a in range(2):
                      pb = psum()
                      pbb = pb.bitcast(BF16)
                      for k8 in range(8):
                          kc = a * 8 + k8
                          K.tr(pbb[:, k8 * 128:(k8 + 1) * 128], n2[:, kc * 128:(kc + 1) * 128], ident_b)
                      K.copy("act", nt2[:, a * 8:(a + 1) * 8, :], pbb.re("p (k t) -> p k t", k=8))
                  if oc == 0:
                      K.ts("dve", nt2[:, :, 0:HALO], nt2[:, :, 0:HALO], hk)
                  K.dma("sp", n2T_v[:, :, oc * CH:(oc + 1) * CH].re("c p t -> p c t"), nt2)
          S.barrier()

          with contextlib.ExitStack() as p2:
              gpo = sbt(p2, "gpo", [128, D], F32)
              K.dma("sp", gpo, V(bcv["g_post_ffn"].partition_broadcast(128), R("bc5")))
              facc = sbt(p2, "facc", [128, STC, D], F32)
              n2s = sbt(p2, "n2s", [128, KC, STC * CH], BF16)
              wg_r = [sbt(p2, "wg%d" % i, [128, 2, KC, 128], BF16) for i in range(2)]
              wd_r = [sbt(p2, "wd%d" % i, [128, D], BF16) for i in range(4)]
              gbuf = sbt(p2, "gbuf", [128, STC * CH + 2], F32)
              cvt = sbt(p2, "cvt", [128, TILE], F32)
              slu = sbt(p2, "slu", [128, TILE], F32)
              actT = [sbt(p2, "actT%d" % i, [128, STC * CH], BF16) for i in range(4)]
              hld = sbt(p2, "hld", [128, D], F32)
              st6 = sbt(p2, "st6", [128, 8], F32)
              K.memset("dve", gbuf[:, 0:2], 0.0)
              st_starts = []
              s0 = 0
              while True:
                  st_starts.append(s0)
                  if s0 + STC >= NOC:
                      break
                  s0 += STC - 1
              for sti, cs0 in enumerate(st_starts):
                  nch = min(STC, NOC - cs0)
                  ntk = nch * CH
                  first_out = 0 if sti == 0 else 1
                  K.dma("sp", n2s[:, :, 0:ntk], n2T_v[:, :, cs0 * CH:cs0 * CH + ntk].re("c p t -> p c t"))
                  ttiles = [(a, min(TILE, ntk - a)) for a in range(0, ntk, TILE)]
                  for fb0 in range(0, NFB, 2):
                      ats, wds = [], []
                      for g2 in range(2):
                          fb = fb0 + g2
                          wgb = wg_r[fb % 2]
                          wdb = wd_r[fb % 4]
                          at = actT[fb % 4]
                          ats.append(at)
                          wds.append(wdb)
                          K.dma("pool", wgb.re("p g k j -> p g (k j)"), wgu_v[2 * fb:2 * fb + 2].re("g p f -> p g f"))
                          K.dma("pool", wdb, wdn_v[fb])
                          for (a0, tn) in ttiles:
                              pgg = psum()
                              pu = psum()
                              for kc in range(KC):
                                  K.mm(pgg[:, 0:tn], wgb[:, 0, kc, :], n2s[:, kc, a0:a0 + tn], start=(kc == 0), stop=(kc == KC - 1))
                              for kc in range(KC):
                                  K.mm(pu[:, 0:tn], wgb[:, 1, kc, :], n2s[:, kc, a0:a0 + tn], start=(kc == 0), stop=(kc == KC - 1))
                              K.copy("act", gbuf[:, 2 + a0:2 + a0 + tn], pgg[:, 0:tn])
                              K.ts("dve", cvt[:, 0:tn], gbuf[:, a0:a0 + tn], vecs[:, V_CF + fb:V_CF + fb + 1])
                              K.stt("dve", cvt[:, 0:tn], gbuf[:, a0 + 1:a0 + 1 + tn], vecs[:, V_CF + NFB + fb:V_CF + NFB + fb + 1],
                                    cvt[:, 0:tn], ALU.mult, ALU.add)
                              K.stt("dve", cvt[:, 0:tn], gbuf[:, a0 + 2:a0 + 2 + tn], vecs[:, V_CF + 2 * NFB + fb:V_CF + 2 * NFB + fb + 1],
                                    cvt[:, 0:tn], ALU.mult, ALU.add)
                              K.act(slu[:, 0:tn], cvt[:, 0:tn], AF.Silu)
                              K.tt("dve", at[:, a0:a0 + tn], slu[:, 0:tn], pu[:, 0:tn], ALU.mult)
                      for ci in range(nch):
                          for blk in range(4):
                              pf = psum()
                              K.mm(pf, ats[0][:, ci * CH:(ci + 1) * CH], wds[0][:, blk * 512:(blk + 1) * 512], start=True, stop=False)
                              K.mm(pf, ats[1][:, ci * CH:(ci + 1) * CH], wds[1][:, blk * 512:(blk + 1) * 512], start=False, stop=True)
                              sl = slice(blk * 512, (blk + 1) * 512)
                              if fb0 == 0:
                                  K.copy("act", facc[:, ci, sl], pf)
                              else:
                                  K.tt("dve", facc[:, ci, sl], facc[:, ci, sl], pf, ALU.add)
                  for ci in range(first_out, nch):
                      oc = cs0 + ci
                      K.dma("sp", hld, h_v[oc * CH:(oc + 1) * CH, :])
                      K.memset("dve", st6[:, 0:1], 0.0)
                      K.act(n2s.re("p k t -> p (k t)")[:, 0:D], facc[:, ci, :], AF.Square, accum=st6[:, 0:1])
                      K.ts("dve", st6[:, 1:2], st6[:, 0:1], 1.0 / D, RMS_EPS, ALU.mult, ALU.add)
                      K.act(st6[:, 2:3], st6[:, 1:2], AF.Sqrt)
                      K.recip(st6[:, 3:4], st6[:, 2:3])
                      K.stt("dve", facc[:, ci, :], facc[:, ci, :], st6[:, 3:4], gpo, ALU.mult, ALU.mult)
                      K.tt("dve", facc[:, ci, :], facc[:, ci, :], hld, ALU.add)
                      K.dma("sp", out_v[oc * CH:(oc + 1) * CH, :], facc[:, ci, :])
          S.barrier()
        except _Stop:
            pass
        S.barrier()
        S.emit()
    return nc


def _blk(w):
    return np.ascontiguousarray(w.reshape(KC, 128, 128).transpose(1, 0, 2)).reshape(128, KC * 128)


def _pad_cols(w, n):
    out = np.zeros((w.shape[0], n), w.dtype)
    out[:, :w.shape[1]] = w
    return out


def _consts():
    c = np.zeros((128, NCONST), np.float32)
    j = np.arange(128)[:, None]
    t = np.arange(128)[None, :]
    c[:, C_ID:C_ID + 128] = np.eye(128)
    strict = (t > j).astype(np.float32)
    incl = (t >= j).astype(np.float32)
    c[:, C_M4:C_M4 + 512] = np.concatenate([strict, incl, strict, incl], 1)
    low = (j > t).astype(np.float32)
    c[:, C_ML:C_ML + 512] = np.concatenate([low] * 4, 1)
    c[:, C_BO:C_BO + 128] = (j // 64 == t // 64).astype(np.float32)
    c[:, C_HS:C_HS + 2] = (j // 64 == np.arange(2)[None, :]).astype(np.float32)
    return c


def prepare(inputs, NT, RANGE):
    f = lambda k: np.asarray(inputs[k], np.float32)
    w_in = f("w_in")
    RB = 3072
    cols = []
    cols.append(_pad_cols(w_in[:, RB + 3072:RB + 3168], 128))
    cols.append(_pad_cols(w_in[:, RB + 3168:RB + 3264], 128))
    cols.append(w_in[:, RB + 3264:RB + 3392])
    cols.append(w_in[:, RB + 3392:RB + 3520])
    for i in range(8):
        for base in (0, 1024, 2048):
            cols.append(w_in[:, RB + base + i * 128:RB + base + (i + 1) * 128])
    for i in range(8):
        for base in (0, 1024, 2048):
            cols.append(w_in[:, base + i * 128:base + (i + 1) * 128])
    win = np.stack([_blk(c) for c in cols], 0)
    wgu_full = f("w_gate_up")
    wgu = np.stack([_blk(wgu_full[:, g * DFF + fb * 128:g * DFF + (fb + 1) * 128]) for fb in range(NFB) for g in range(2)], 0)
    wdn = np.ascontiguousarray(f("w_down").reshape(NFB, 128, D))
    wout = np.ascontiguousarray(f("w_out").reshape(KC, 128, D).transpose(1, 0, 2)).reshape(128, KC * D)
    lora = np.zeros((128, 4 * RW), np.float32)
    lora[:96, 0:RW] = f("w_decay_up")
    lora[:96, RW:2 * RW] = f("a_up")
    gup = f("g_up")
    lora[:, 2 * RW:3 * RW] = gup[0:128]
    lora[:, 3 * RW:4 * RW] = gup[128:256]
    mu = f("mu_rwkv")
    vecs = np.zeros((128, NV), np.float32)
    vecs[:96, V_MU + 0] = mu[3072:3168]
    vecs[:96, V_MU + 1] = mu[3168:3264]
    vecs[:, V_MU + 2] = mu[3264:3392]
    vecs[:, V_MU + 3] = mu[3392:3520]
    for i in range(8):
        for qi, base in enumerate((0, 1024, 2048)):
            vecs[:, V_MU + 4 + i * 3 + qi] = mu[base + i * 128:base + (i + 1) * 128]
    for name, col in (("w0", V_W0), ("a0", V_A0), ("k_k", V_KK), ("k_a", V_KA)):
        vecs[:, col:col + 8] = f(name).reshape(8, 128).T
    vecs[:, V_RK:V_RK + 8] = f("r_k").reshape(8, 128).T
    cw = f("conv_w")
    for tap in range(3):
        vecs[:, V_CW + tap * 8:V_CW + tap * 8 + 8] = cw[tap].reshape(8, 128).T
    cfw = f("conv_ffn")
    for tap in range(3):
        vecs[:, V_CF + tap * NFB:V_CF + (tap + 1) * NFB] = cfw[tap].reshape(NFB, 128).T
    shared = {"win": win, "wgu": wgu, "wdn": wdn, "wout": wout, "lora": lora, "consts": _consts()}
    for n in ("g_pre_mix", "g_post_mix", "g_pre_ffn", "g_post_ffn", "ln_x_w", "ln_x_b"):
        shared[n] = f(n).reshape(1, -1)
    x = f("x")
    meta = f("meta_tokens")
    SEQ = NT * TILE
    in_maps = []
    for c in range(8):
        b, q = c // 4, c % 4
        seq = np.concatenate([meta, x[b]], 0)
        n_real = (q + 1) * RANGE
        xs = np.zeros((SEQ, D), np.float32)
        xs[SEQ - n_real:] = seq[:n_real]
        v = vecs.copy()
        v[:, V_HK] = 0.0 if q == 0 else 1.0
        m = dict(shared)
        m["xs"] = xs
        m["vecs"] = v
        in_maps.append(m)
    return in_maps


def kernel(**inputs):
    x = np.asarray(inputs["x"])
    B, L, _ = x.shape
    tot = N_META + L
    RANGE = tot // 4
    NT = -(-(tot + 2) // TILE)
    nc = build_nc(NT, RANGE)
    in_maps = prepare(inputs, NT, RANGE)
    dbg = os.environ.get("KCORES")
    if dbg is not None:
        sel = [int(t) for t in dbg.split(",")]
        res1 = run_bass_kernel_spmd(nc, [in_maps[c] for c in sel], core_ids=list(range(len(sel))))
        return {c: {k: np.asarray(v) for k, v in res1.results[i].items()} for i, c in enumerate(sel)}
    res = run_bass_kernel_spmd(nc, in_maps, core_ids=list(range(8)))
    SEQ = NT * TILE
    OC0 = (SEQ - RANGE - 2) // CH
    off = SEQ - RANGE - OC0 * CH
    full = np.zeros((B, tot, D), np.float32)
    for c in range(8):
        b, q = c // 4, c % 4
        o = np.asarray(res.results[c]["out"])
        full[b, q * RANGE:(q + 1) * RANGE] = o[off:off + RANGE]
    return np.ascontiguousarray(full[:, N_META:])
```

```python
import contextlib
import math
import numpy as np
import concourse.bass as bass
import concourse.mybir as mybir
from concourse.bass_utils import run_bass_kernel_spmd

F32 = mybir.dt.float32
BF16 = mybir.dt.bfloat16
ALU = mybir.AluOpType
AF = mybir.ActivationFunctionType
AX = mybir.AxisListType

D = 2048
KC = 16
TILE = 512
CH = 128
RW = 1024
DFF = 5632
NFB = DFF // 128
N_META = 16
RMS_EPS = 1e-6
GN_EPS = 64e-5
DECAY_C = math.exp(-0.5)
STC = 9

V_MU = 0
V_W0 = 28
V_A0 = 36
V_KK = 44
V_KA = 52
V_RK = 60
V_CW = 68
V_CF = 92
V_HK = 92 + 3 * NFB
NV = V_HK + 1
C_ID = 0
C_M4 = 128
C_ML = 640
C_BO = 1152
C_HS = 1280
NCONST = 1282


class Res:
    __slots__ = ("name", "w", "r", "dsem", "dcnt")

    def __init__(self, name):
        self.name = name
        self.w = None
        self.r = {}
        self.dsem = None
        self.dcnt = 0


class V:
    __slots__ = ("ap", "res")

    def __init__(self, ap, res):
        self.ap = ap
        self.res = res

    def __getitem__(self, idx):
        return V(self.ap[idx], self.res)

    def re(self, pat, **kw):
        return V(self.ap.rearrange(pat, **kw), self.res)

    def bitcast(self, dt):
        return V(self.ap.bitcast(dt), self.res)

    def bc(self, shape, axis):
        return V(self.ap.unsqueeze(axis).to_broadcast(list(shape)), self.res)


class Sched:
    ENG = ("pe", "act", "dve", "pool", "sp")

    def __init__(self, nc, stack):
        self.nc = nc
        self.stack = stack
        self.ops = {e: [] for e in self.ENG}
        self.sem = {e: stack.enter_context(nc.semaphore("s_" + e)) for e in self.ENG}
        self.cnt = {e: 0 for e in self.ENG}
        self.seen = {e: {} for e in self.ENG}
        self.allres = []

    def res(self, name):
        r = Res(name)
        self.allres.append(r)
        return r

    def _dsem(self, res):
        if res.dsem is None:
            res.dsem = self.stack.enter_context(self.nc.semaphore("d_" + res.name))
        return res.dsem

    def _deps(self, eng, reads, writes, seen_key=None):
        deps = {}

        def add(t, kind):
            if t is None:
                return
            key, val, en = t
            if en == eng and (kind != "raw" or eng == "pe"):
                return
            if deps.get(key, 0) < val:
                deps[key] = val

        for r in reads:
            add(r.w, "raw")
        for w in writes:
            add(w.w, "waw")
            for key, (val, en) in w.r.items():
                add((key, val, en), "war")
        out = []
        seen = self.seen[seen_key or eng]
        for key, val in deps.items():
            if seen.get(key, 0) >= val:
                continue
            seen[key] = val
            out.append((key, val))
        return out

    def op(self, eng, fn, reads=(), writes=()):
        if _HALT[0]:
            return
        waits = self._deps(eng, reads, writes)
        self.cnt[eng] += 1
        c = self.cnt[eng]
        key = ("c", eng)
        self.ops[eng].append((waits, fn, ("c", c)))
        for r in reads:
            r.r[key] = (c, eng)
        for w in writes:
            w.w = (key, c, eng)
            w.r = {}

    def dma(self, q, fn, reads=(), writes=()):
        if _HALT[0]:
            return
        waits = self._deps("dma", reads, writes, seen_key=q)
        anchor = writes[0] if writes else reads[0]
        sem = self._dsem(anchor)
        anchor.dcnt += 16
        c = anchor.dcnt
        key = ("d", sem)
        self.ops[q].append((waits, fn, ("d", sem)))
        for r in reads:
            r.r[key] = (c, "dma")
        for w in writes:
            w.w = (key, c, "dma")
            w.r = {}

    def barrier(self):
        for e in self.ENG:
            waits = self._deps(e + "_bar", self.allres, self.allres, seen_key=e)
            if waits:
                self.ops[e].append((waits, None, None))

    def emit(self):
        nc = self.nc
        ops = self.ops
        needed = {e: set() for e in self.ENG}
        for e in self.ENG:
            for waits, fn, inc in ops[e]:
                for key, val in waits:
                    if key[0] == "c":
                        needed[key[1]].add(val)
        rank = {e: {v: i + 1 for i, v in enumerate(sorted(needed[e]))} for e in self.ENG}
        sem = self.sem

        def run(engh, lst):
            for waits, fn, inc in lst:
                for key, val in waits:
                    if key[0] == "c":
                        engh.wait_ge(sem[key[1]], rank[key[1]][val])
                    else:
                        engh.wait_ge(key[1], val)
                if fn is not None:
                    ins = fn(engh)
                    if inc[0] == "d":
                        ins.then_inc(inc[1], 16)
                    elif inc[1] in rank[self._cur]:
                        ins.then_inc(sem[self._cur], 1)

        with nc.Block() as block:
            @block.tensor
            def _(e):
                self._cur = "pe"
                run(e, ops["pe"])

            @block.scalar
            def _(e):
                self._cur = "act"
                run(e, ops["act"])

            @block.vector
            def _(e):
                self._cur = "dve"
                run(e, ops["dve"])

            @block.gpsimd
            def _(e):
                self._cur = "pool"
                run(e, ops["pool"])

            @block.sync
            def _(e):
                self._cur = "sp"
                run(e, ops["sp"])


def _rw(reads, writes):
    rs, ws = [], []
    for v in reads:
        if isinstance(v, V) and v.res not in rs:
            rs.append(v.res)
    for v in writes:
        if v.res not in ws:
            ws.append(v.res)
    return rs, ws


def _a(x):
    return x.ap if isinstance(x, V) else x


class KB:
    def __init__(self, nc, S):
        self.nc = nc
        self.S = S

    def mm(self, out, lhsT, rhs, start=True, stop=True):
        rs, ws = _rw([lhsT, rhs], [out])
        self.S.op("pe", lambda e: e.matmul(out.ap, lhsT=lhsT.ap, rhs=rhs.ap, start=start, stop=stop), rs, ws)

    def tr(self, out, in_, ident):
        rs, ws = _rw([in_, ident], [out])
        self.S.op("pe", lambda e: e.transpose(out=out.ap, in_=in_.ap, identity=ident.ap), rs, ws)

    def act(self, out, in_, func, bias=None, scale=None, accum=None):
        rs, ws = _rw([in_, bias, scale], [out] + ([accum] if accum is not None else []))
        kw = {}
        if bias is not None:
            kw["bias"] = _a(bias)
        if scale is not None:
            kw["scale"] = _a(scale)
        if accum is not None:
            kw["accum_out"] = accum.ap
        self.S.op("act", lambda e: e.activation(out=out.ap, in_=in_.ap, func=func, **kw), rs, ws)

    def copy(self, eng, out, in_):
        rs, ws = _rw([in_], [out])
        if eng == "act":
            self.S.op("act", lambda e: e.activation(out=out.ap, in_=in_.ap, func=AF.Copy), rs, ws)
        else:
            self.S.op(eng, lambda e: e.tensor_copy(out=out.ap, in_=in_.ap), rs, ws)

    def tt(self, eng, out, a, b, op):
        rs, ws = _rw([a, b], [out])
        self.S.op(eng, lambda e: e.tensor_tensor(out=out.ap, in0=a.ap, in1=b.ap, op=op), rs, ws)

    def ts(self, eng, out, a, s1, s2=None, op0=ALU.mult, op1=None):
        rs, ws = _rw([a, s1, s2], [out])
        if op1 is None:
            self.S.op(eng, lambda e: e.tensor_scalar(out=out.ap, in0=a.ap, scalar1=_a(s1), scalar2=0.0, op0=op0, op1=ALU.add), rs, ws)
        else:
            self.S.op(eng, lambda e: e.tensor_scalar(out=out.ap, in0=a.ap, scalar1=_a(s1), scalar2=_a(s2), op0=op0, op1=op1), rs, ws)

    def stt(self, eng, out, a, scalar, b, op0, op1):
        rs, ws = _rw([a, scalar, b], [out])
        self.S.op(eng, lambda e: e.scalar_tensor_tensor(out=out.ap, in0=a.ap, scalar=_a(scalar), in1=b.ap, op0=op0, op1=op1), rs, ws)

    def red(self, eng, out, in_, op=ALU.add):
        rs, ws = _rw([in_], [out])
        self.S.op(eng, lambda e: e.tensor_reduce(out=out.ap, in_=in_.ap, axis=AX.X, op=op), rs, ws)

    def scan(self, out, d0, d1):
        rs, ws = _rw([d0, d1], [out])
        self.S.op("dve", lambda e: e.tensor_tensor_scan(out=out.ap, data0=d0.ap, data1=d1.ap, initial=0.0,
                                                        op0=ALU.mult, op1=ALU.add), rs, ws)

    def recip(self, out, in_):
        rs, ws = _rw([in_], [out])
        self.S.op("dve", lambda e: e.reciprocal(out=out.ap, in_=in_.ap), rs, ws)

    def memset(self, eng, out, val):
        rs, ws = _rw([], [out])
        self.S.op(eng, lambda e: e.memset(out.ap, val), rs, ws)

    def dma(self, q, out, in_):
        rs, ws = _rw([in_], [out])
        self.S.dma(q, lambda e: e.dma_start(out=out.ap, in_=in_.ap), rs, ws)


import os
_STOP = int(os.environ.get("KSTOP", "99"))


class _Stop(Exception):
    pass


_HALT = [False]


def _ck(k):
    if _STOP == k:
        _HALT[0] = True


def build_nc(NT, RANGE):
    SEQ = NT * TILE
    OWN0 = SEQ - RANGE - 2
    OC0 = OWN0 // CH
    NCHK = SEQ // CH
    NOC = NCHK - OC0
    HALO = OWN0 - OC0 * CH + 2
    NTOK2 = NOC * CH
    OT0 = OC0 // 4

    nc = bass.Bass("TRN2", target_bir_lowering=False)
    _HALT[0] = False
    dr = {}

    def din(name, shape):
        dr[name] = nc.dram_tensor(name, list(shape), F32, kind="ExternalInput").ap()
        return dr[name]

    xs = din("xs", [SEQ, D])
    win = din("win", [52, 128, KC * 128])
    wgu = din("wgu", [NFB * 2, 128, KC * 128])
    wdn = din("wdn", [NFB, 128, D])
    wout = din("wout", [128, KC * D])
    lora = din("lora", [128, 4 * RW])
    vecs_d = din("vecs", [128, NV])
    const_d = din("consts", [128, NCONST])
    bcv = {n: din(n, [1, w]) for n, w in (("g_pre_mix", D), ("g_post_mix", D), ("g_pre_ffn", D),
                                          ("g_post_ffn", D), ("ln_x_w", RW), ("ln_x_b", RW))}
    out_d = nc.dram_tensor("out", [NTOK2, D], F32, kind="ExternalOutput").ap()
    yT_d = nc.dram_tensor("yT_s", [KC, 128, NTOK2], BF16, kind="Internal").ap()
    n2T_d = nc.dram_tensor("n2T_s", [KC, 128, NTOK2], BF16, kind="Internal").ap()
    h_d = nc.dram_tensor("h_s", [NTOK2, D], F32, kind="Internal").ap()

    with contextlib.ExitStack() as top:
        S = Sched(nc, top)
        K = KB(nc, S)
        R = S.res
        xs_v = V(xs, R("xs"))
        win_v = V(win, R("win"))
        wgu_v = V(wgu, R("wgu"))
        wdn_v = V(wdn, R("wdn"))
        wout_v = V(wout, R("wout"))
        lora_v = V(lora, R("lora"))
        out_v = V(out_d, R("out"))
        yT_v = V(yT_d, R("yT"))
        n2T_v = V(n2T_d, R("n2T"))
        h_v = V(h_d, R("h"))

        banks = []
        for i in range(8):
            t = top.enter_context(nc.psum_tensor("pb%d" % i, [128, 512], F32))
            banks.append(V(t[:, :], R("pb%d" % i)))
        bank_i = [0]

        def psum():
            b = banks[bank_i[0] % 8]
            bank_i[0] += 1
            return b

        def sbt(stack, name, shape, dt):
            t = stack.enter_context(nc.sbuf_tensor("sb_" + name, list(shape), dt))
            return V(t[tuple(slice(None) for _ in shape)], R(name))

        vecs = sbt(top, "vecs", [128, NV], F32)
        vder = sbt(top, "vder", [128, 28 + 8], F32)
        cf = sbt(top, "cf", [128, NCONST], F32)
        cb = sbt(top, "cb", [128, NCONST], BF16)
        ones = sbt(top, "ones", [128, 128], F32)
        K.dma("sp", vecs, V(vecs_d, R("vecs_d")))
        K.dma("sp", cf, V(const_d, R("const_d")))
        K.dma("pool", cb, V(const_d, R("const_d2")))
        K.memset("dve", ones, 1.0)
        K.ts("dve", vder[:, 0:28], vecs[:, V_MU:V_MU + 28], -1.0, 1.0, ALU.mult, ALU.add)
        K.ts("dve", vder[:, 28:36], vecs[:, V_KA:V_KA + 8], -1.0, 1.0, ALU.mult, ALU.add)
        ident_b = cb[:, C_ID:C_ID + 128]
        mask4 = cb[:, C_M4:C_M4 + 512]
        maskL4 = cb[:, C_ML:C_ML + 512]
        blockones = cf[:, C_BO:C_BO + 128]
        headsel_b = cb[:, C_HS:C_HS + 2]
        hk = vecs[:, V_HK:V_HK + 1]

        def rms_rstd(src, rstd, junk, nbanks=None):
            pass

        try:
          _ck(1)
          with contextlib.ExitStack() as p1:
              gpre = sbt(p1, "gpre", [128, D], F32)
              lnw = sbt(p1, "lnw", [128, RW], F32)
              lnb = sbt(p1, "lnb", [128, RW], F32)
              K.dma("sp", gpre, V(bcv["g_pre_mix"].partition_broadcast(128), R("bc0")))
              K.dma("sp", lnw, V(bcv["ln_x_w"].partition_broadcast(128), R("bc1")))
              K.dma("sp", lnb, V(bcv["ln_x_b"].partition_broadcast(128), R("bc2")))
              lw_b = sbt(p1, "lw_b", [128, 4 * RW], BF16)
              K.dma("pool", lw_b, lora_v)
              wring = [sbt(p1, "wr%d" % i, [128, KC, 128], BF16) for i in range(4)]
              wr_i = [0]
              xin_r = [sbt(p1, "xin%d" % i, [128, D], F32) for i in range(2)]
              xn = sbt(p1, "xn", [128, D], BF16)
              st4 = sbt(p1, "st4", [128, 8], F32)
              nT = sbt(p1, "nT", [128, KC, TILE], BF16)
              carry = sbt(p1, "carry", [128, 28], F32)
              K.memset("dve", carry, 0.0)
              ccar = sbt(p1, "ccar", [128, 8, 2], F32)
              K.memset("dve", ccar, 0.0)
              tw = sbt(p1, "tw", [128, TILE], BF16)
              adb = sbt(p1, "adb", [128, TILE], BF16)
              sg2 = [sbt(p1, "sg%d" % i, [128, 2, TILE], BF16) for i in range(2)]
              ztmp = sbt(p1, "ztmp", [128, TILE], F32)
              zr = sbt(p1, "zr", [128, TILE], F32)
              zk = sbt(p1, "zk", [128, TILE], F32)
              zv = sbt(p1, "zv", [128, TILE], F32)
              fA = [sbt(p1, "fA%d" % i, [128, TILE], F32) for i in range(8)]
              AR2 = [sbt(p1, "AR%d" % i, [128, 2, 4, 256], BF16) for i in range(2)]
              BK2 = [sbt(p1, "BK%d" % i, [128, 2, 4, 256], BF16) for i in range(2)]
              TF2 = [sbt(p1, "TF%d" % i, [128, 2, 3, TILE], BF16) for i in range(2)]
              rkb2 = [sbt(p1, "rkb%d" % i, [128, 2, TILE], BF16) for i in range(2)]
              PC2 = [sbt(p1, "PCd%d" % i, [128, 2, 4], F32) for i in range(2)]
              TOK = [sbt(p1, "TOK%d" % i, [128, 2, 4, 128], BF16) for i in range(2)]
              SA = [sbt(p1, "SA%d" % i, [128, 4, 512], BF16) for i in range(2)]
              SN = [[sbt(p1, "SN%d_%d" % (i, j), [128, 4, 128], BF16) for j in range(2)] for i in range(2)]
              SL = [[sbt(p1, "SL%d_%d" % (i, j), [128, 4, 128], BF16) for j in range(2)] for i in range(2)]
              WW = [[sbt(p1, "WW%d_%d" % (i, j), [128, 4, 128], BF16) for j in range(2)] for i in range(2)]
              GT = [sbt(p1, "GT%d" % i, [128, 2, 64], BF16) for i in range(2)]
              RH = [sbt(p1, "RH%d" % i, [128, 2, 128], BF16) for i in range(2)]
              Tst = sbt(p1, "Tst", [128, 8, 64], F32)
              Tbf = [sbt(p1, "Tbf%d" % i, [128, 8, 64], BF16) for i in range(2)]
              K.memset("dve", Tst, 0.0)
              K.memset("dve", Tbf[0], 0.0)
              K.memset("dve", Tbf[1], 0.0)
              tb_i = [[0] * 8]
              ysb = sbt(p1, "ysb", [128, 256], F32)
              ysq = sbt(p1, "ysq", [128, 256], F32)
              yn = sbt(p1, "yn", [128, 256], F32)
              bon = sbt(p1, "bon", [128, 256], F32)
              gst = sbt(p1, "gst", [128, 24], F32)
              ytb = sbt(p1, "ytb", [128, 256], BF16)
              yTs = sbt(p1, "yTs", [128, 2, TILE], BF16)
              csb = sbt(p1, "csb", [128, TILE], F32)
              chb = sbt(p1, "chb", [128, TILE + 2], F32)
              ycv = sbt(p1, "ycv", [128, TILE], BF16)

              def load_w(cbi):
                  w = wring[wr_i[0] % 4]
                  wr_i[0] += 1
                  K.dma("pool", w.re("p k j -> p (k j)"), win_v[cbi])
                  return w

              def project(w, t_cols=TILE):
                  pz = psum()
                  for kc in range(KC):
                      K.mm(pz[:, 0:t_cols], w[:, kc, :], nT[:, kc, 0:t_cols], start=(kc == 0), stop=(kc == KC - 1))
                  return pz

              def shift(pz, col, out):
                  mu = vecs[:, V_MU + col:V_MU + col + 1]
                  omu = vder[:, col:col + 1]
                  K.act(ztmp, pz, AF.Copy, scale=omu)
                  K.stt("dve", out[:, 1:TILE], pz[:, 0:TILE - 1], mu, ztmp[:, 1:TILE], ALU.mult, ALU.add)
                  K.stt("dve", out[:, 0:1], carry[:, col:col + 1], mu, ztmp[:, 0:1], ALU.mult, ALU.add)
                  K.copy("act", carry[:, col:col + 1], pz[:, TILE - 1:TILE])

              def front(ti):
                  t0 = ti * TILE
                  sg = sg2[ti % 2]
                  for j in range(4):
                      xin = xin_r[j % 2]
                      K.dma("sp", xin, xs_v[t0 + j * CH:t0 + (j + 1) * CH, :])
                      K.memset("dve", st4[:, 0:1], 0.0)
                      K.act(xn, xin, AF.Square, accum=st4[:, 0:1])
                      K.ts("dve", st4[:, 1:2], st4[:, 0:1], 1.0 / D, RMS_EPS, ALU.mult, ALU.add)
                      K.act(st4[:, 2:3], st4[:, 1:2], AF.Sqrt)
                      K.recip(st4[:, 3:4], st4[:, 2:3])
                      K.stt("dve", xn, xin, st4[:, 3:4], gpre, ALU.mult, ALU.mult)
                      yield
                      for a in range(2):
                          pb = psum()
                          pbb = pb.bitcast(BF16)
                          for k8 in range(8):
                              kc = a * 8 + k8
                              K.tr(pbb[:, k8 * 128:(k8 + 1) * 128], xn[:, kc * 128:(kc + 1) * 128], ident_b)
                          K.copy("act" if a == 0 else "dve", nT[:, a * 8:(a + 1) * 8, j * CH:(j + 1) * CH],
                                 pbb.re("p (k t) -> p k t", k=8))
                          yield
                  pz = project(load_w(0))
                  shift(pz, 0, zr)
                  K.act(tw, zr, AF.Tanh)
                  yield
                  pz = project(load_w(1))
                  shift(pz, 1, zr)
                  K.copy("act", adb, zr)
                  yield
                  for gi in range(2):
                      if ti < OT0 - 1:
                          break
                      pz = project(load_w(2 + gi))
                      shift(pz, 2 + gi, zr)
                      K.act(sg[:, gi, :], zr, AF.Sigmoid)
                      yield

              def prep(ti, hg):
                  out_tile = ti >= OT0
                  par = (ti * 4 + hg) % 2
                  AR, BK, TF, rkb, PC = AR2[par], BK2[par], TF2[par], rkb2[par], PC2[par]
                  for s in range(2):
                      hp = hg * 2 + s
                      for qi, dst in enumerate((zr, zk, zv)):
                          if qi == 0 and ti < OT0 - 1:
                              continue
                          pz = project(load_w(4 + hp * 3 + qi))
                          shift(pz, 4 + hp * 3 + qi, dst)
                          yield
                      plw = psum()
                      K.mm(plw, lw_b[:, hp * 128:(hp + 1) * 128], tw)
                      pa = psum()
                      K.mm(pa, lw_b[:, RW + hp * 128:RW + (hp + 1) * 128], adb)
                      sw, aa, cs, kk, t1, t2, t3, t4 = fA
                      K.act(sw, plw, AF.Sigmoid, bias=vecs[:, V_W0 + hp:V_W0 + hp + 1])
                      K.act(aa, pa, AF.Sigmoid, bias=vecs[:, V_A0 + hp:V_A0 + hp + 1])
                      yield
                      for c in range(4):
                          K.scan(cs[:, c * CH:(c + 1) * CH], ones, sw[:, c * CH:(c + 1) * CH])
                      K.ts("dve", kk, zk, vecs[:, V_KK + hp:V_KK + hp + 1])
                      K.tt("dve", t1, kk, kk, ALU.mult)
                      yield
                      pss = psum()
                      K.mm(pss, blockones, t1)
                      K.act(t1, pss, AF.Sqrt)
                      K.ts("dve", t1, t1, 1e-12, None, ALU.max)
                      K.recip(t2, t1)
                      K.tt("dve", kk, kk, t2, ALU.mult)
                      yield
                      K.ts("dve", t1, aa, vecs[:, V_KA + hp:V_KA + hp + 1], vder[:, 28 + hp:29 + hp], ALU.mult, ALU.add)
                      K.tt("dve", t1, zk, t1, ALU.mult)
                      K.tt("dve", aa, kk, aa, ALU.mult)
                      yield
                      K.tt("dve", t2, cs, sw, ALU.subtract)
                      K.act(t2, t2, AF.Exp, scale=-DECAY_C)
                      K.stt("dve", AR[:, s, :, 0:128], kk.re("p (c t) -> p c t", c=4), -1.0,
                            t2.re("p (c t) -> p c t", c=4), ALU.mult, ALU.mult)
                      yield
                      if out_tile:
                          K.act(t3, cs, AF.Exp, scale=-DECAY_C)
                          K.tt("dve", AR[:, s, :, 128:256], zr.re("p (c t) -> p c t", c=4),
                               t3.re("p (c t) -> p c t", c=4), ALU.mult)
                      yield
                      K.act(t3, cs, AF.Exp, scale=DECAY_C)
                      K.tt("dve", BK[:, s, :, 0:128], aa.re("p (c t) -> p c t", c=4),
                           t3.re("p (c t) -> p c t", c=4), ALU.mult)
                      K.tt("dve", BK[:, s, :, 128:256], t1.re("p (c t) -> p c t", c=4),
                           t3.re("p (c t) -> p c t", c=4), ALU.mult)
                      yield
                      K.ts("dve", st4[:, 4:8], cs.re("p (c t) -> p c t", c=4)[:, :, CH - 1], -DECAY_C)
                      K.act(PC[:, s, :], st4[:, 4:8], AF.Exp)
                      for c in range(4):
                          K.act(t4[:, c * CH:(c + 1) * CH], cs[:, c * CH:(c + 1) * CH], AF.Exp,
                                scale=DECAY_C, bias=st4[:, 4 + c:5 + c])
                      yield
                      K.tt("dve", TF[:, s, 1, :], aa, t4, ALU.mult)
                      K.tt("dve", TF[:, s, 2, :], t1, t4, ALU.mult)
                      K.copy("act", TF[:, s, 0, :], zv)
                      if out_tile:
                          K.tt("dve", t2, zr, t1, ALU.mult)
                          K.ts("dve", rkb[:, s, :], t2, vecs[:, V_RK + hp:V_RK + hp + 1])
                      yield

              def scan(ti, hg):
                  out_tile = ti >= OT0
                  par = (ti * 4 + hg) % 2
                  AR, BK, TF, rkb, PC = AR2[par], BK2[par], TF2[par], rkb2[par], PC2[par]
                  sg = sg2[ti % 2]
                  for cp in range(2):
                      cl = [2 * cp, 2 * cp + 1]
                      for u, c in enumerate(cl):
                          pb = psum()
                          pbb = pb.bitcast(BF16).re("p (s q t) -> p s q t", s=2, q=4)
                          for s in range(2):
                              for q in range(3):
                                  K.tr(pbb[:, s, q, :], TF[:, s, q, c * CH:(c + 1) * CH], ident_b)
                              K.tr(pbb[:, s, 3, :], AR[:, s, c, 0:128], ident_b)
                          K.copy("act", TOK[u], pbb)
                          yield
                      for u, c in enumerate(cl):
                          for hl in range(4):
                              s, e = hl // 2, hl % 2
                              ps_ = slice(64 * e, 64 * e + 64)
                              pa_ = psum()
                              if out_tile:
                                  K.mm(pa_[:, 0:256], BK[ps_, s, c, 0:128], AR[ps_, s, c, :])
                                  K.mm(pa_[:, 256:512], BK[ps_, s, c, 128:256], AR[ps_, s, c, :])
                                  K.tt("dve", SA[u][:, hl, :], pa_, mask4, ALU.mult)
                              else:
                                  K.mm(pa_[:, 0:128], BK[ps_, s, c, 0:128], AR[ps_, s, c, 0:128])
                                  K.mm(pa_[:, 256:384], BK[ps_, s, c, 128:256], AR[ps_, s, c, 0:128])
                                  K.tt("dve", SA[u][:, hl, :].re("p (a b) -> p a b", a=2)[:, :, 0:128],
                                       pa_.re("p (a b) -> p a b", a=2)[:, :, 0:128],
                                       mask4.re("p (a b) -> p a b", a=2)[:, :, 0:128], ALU.mult)
                          yield
                          ple = [psum(), psum()]
                          for hl in range(4):
                              s, e = hl // 2, hl % 2
                              ps_ = slice(64 * e, 64 * e + 64)
                              K.mm(ple[e][:, s * 128:(s + 1) * 128], AR[ps_, s, c, 0:128], BK[ps_, s, c, 0:128])
                          for e in range(2):
                              K.tt("dve", SL[u][0].re("p (s e) t -> p s e t", e=2)[:, :, e, :],
                                   ple[e][:, 0:256].re("p (s t) -> p s t", s=2),
                                   maskL4[:, 0:256].re("p (s t) -> p s t", s=2), ALU.mult)
                          px = psum()
                          for hl in range(4):
                              s, e = hl // 2, hl % 2
                              K.mm(px[:, hl * 64:(hl + 1) * 64], SA[u][:, hl, 256:384], TOK[u][:, s, 0, 64 * e:64 * e + 64])
                          K.copy("act", WW[u][0][:, :, 64:128], px[:, 0:256].re("p (h v) -> p h v", h=4))
                          K.copy("act", WW[u][0].re("p (s e) k -> p s e k", s=2)[:, :, :, 0:64],
                                 TOK[u][:, :, 3, :].re("p s (e k) -> p s e k", e=2))
                          yield
                      for i in range(7):
                          for u, c in enumerate(cl):
                              n_cur = SA[u][:, :, 0:128] if i == 0 else SN[u][i % 2]
                              l_cur = SL[u][i % 2]
                              w_cur = WW[u][i % 2]
                              pw = psum()
                              for hl in range(4):
                                  K.mm(pw[:, hl * 128:(hl + 1) * 128], n_cur[:, hl, :], w_cur[:, hl, :])
                              K.tt("dve", WW[u][(i + 1) % 2].re("p h t -> p (h t)"), pw,
                                   w_cur.re("p h t -> p (h t)"), ALU.add)
                              if i < 6:
                                  pn = psum()
                                  for hl in range(4):
                                      K.mm(pn[:, hl * 128:(hl + 1) * 128], l_cur[:, hl, :], n_cur[:, hl, :])
                                  K.copy("act", SN[u][(i + 1) % 2].re("p h t -> p (h t)"), pn)
                              if i < 5:
                                  pl_ = psum()
                                  for hl in range(4):
                                      K.mm(pl_[:, hl * 128:(hl + 1) * 128], n_cur[:, hl, :], l_cur[:, hl, :])
                                  K.copy("act", SL[u][(i + 1) % 2].re("p h t -> p (h t)"), pl_)
                              yield
                      for u, c in enumerate(cl):
                          Wf = WW[u][1]
                          chunk_g = ti * 4 + c
                          want_y = chunk_g >= OC0
                          pg = psum()
                          for hl in range(4):
                              s, e = hl // 2, hl % 2
                              K.mm(pg[64 * e:64 * e + 64, s * 64:(s + 1) * 64], Wf[:, hl, 0:64],
                                   TOK[u][:, s, 1, 64 * e:64 * e + 64])
                          K.copy("act", GT[u].re("p s k -> p (s k)"), pg[:, 0:128])
                          tcur = Tbf[tb_i[0][hg] % 2]
                          tnew = Tbf[(tb_i[0][hg] + 1) % 2]
                          if want_y:
                              pr = psum()
                              for hl in range(4):
                                  s, e = hl // 2, hl % 2
                                  K.mm(pr[64 * e:64 * e + 64, s * 128:(s + 1) * 128], Wf[:, hl, 0:64], SA[u][:, hl, 128:256])
                              K.tt("dve", RH[u], pr[:, 0:256].re("p (s t) -> p s t", s=2), AR[:, :, c, 128:256], ALU.add)
                              py = psum()
                              for hl in range(4):
                                  s, e = hl // 2, hl % 2
                                  ps_ = slice(64 * e, 64 * e + 64)
                                  o = py[:, hl * 64:(hl + 1) * 64]
                                  K.mm(o, SA[u][:, hl, 128:256], Wf[:, hl, 64:128], start=True, stop=False)
                                  K.mm(o, SA[u][:, hl, 384:512], TOK[u][:, s, 0, 64 * e:64 * e + 64], start=False, stop=False)
                                  K.mm(o, RH[u][ps_, s, :], tcur[ps_, hg * 2 + s, :], start=False, stop=True)
                          pt = psum()
                          for hl in range(4):
                              s, e = hl // 2, hl % 2
                              ps_ = slice(64 * e, 64 * e + 64)
                              o = pt[ps_, s * 64:(s + 1) * 64]
                              K.mm(o, TOK[u][:, s, 1, 64 * e:64 * e + 64], Wf[:, hl, 64:128], start=True, stop=False)
                              K.mm(o, TOK[u][:, s, 2, 64 * e:64 * e + 64], TOK[u][:, s, 0, 64 * e:64 * e + 64], start=False, stop=False)
                              K.mm(o, GT[u][ps_, s, :], tcur[ps_, hg * 2 + s, :], start=False, stop=True)
                          for s in range(2):
                              K.stt("dve", Tst[:, hg * 2 + s, :], Tst[:, hg * 2 + s, :], PC[:, s, c:c + 1],
                                    pt[:, s * 64:(s + 1) * 64], ALU.mult, ALU.add)
                          K.copy("act", tnew[:, hg * 2:hg * 2 + 2, :], Tst[:, hg * 2:hg * 2 + 2, :])
                          tb_i[0][hg] += 1
                          if want_y:
                              K.copy("act", ysb, py[:, 0:256])
                          yield
                          if not want_y:
                              continue
                          pgt = psum()
                          K.mm(pgt[:, 0:256], sg[:, 0, c * CH:(c + 1) * CH], lw_b[:, 2 * RW + hg * 256:2 * RW + (hg + 1) * 256],
                               start=True, stop=False)
                          K.mm(pgt[:, 0:256], sg[:, 1, c * CH:(c + 1) * CH], lw_b[:, 3 * RW + hg * 256:3 * RW + (hg + 1) * 256],
                               start=False, stop=True)
                          for s in range(2):
                              K.mm(pgt[:, 256 + 2 * s:258 + 2 * s], rkb[:, s, c * CH:(c + 1) * CH], headsel_b)
                          K.copy("act", gst[:, 16:20], pgt[:, 256:260])
                          y3 = ysb.re("p (h v) -> p h v", h=4)
                          K.red("dve", gst[:, 0:4], y3)
                          K.tt("dve", ysq, ysb, ysb, ALU.mult)
                          K.red("dve", gst[:, 4:8], ysq.re("p (h v) -> p h v", h=4))
                          K.ts("dve", gst[:, 0:4], gst[:, 0:4], 1.0 / 64)
                          K.tt("dve", gst[:, 8:12], gst[:, 0:4], gst[:, 0:4], ALU.mult)
                          K.stt("dve", gst[:, 4:8], gst[:, 4:8], 1.0 / 64, gst[:, 8:12], ALU.mult, ALU.subtract)
                          K.ts("dve", gst[:, 4:8], gst[:, 4:8], GN_EPS, None, ALU.add)
                          K.act(gst[:, 8:12], gst[:, 4:8], AF.Sqrt)
                          K.recip(gst[:, 12:16], gst[:, 8:12])
                          yn3 = yn.re("p (h v) -> p h v", h=4)
                          K.tt("dve", yn3, y3, gst[:, 0:4].bc([128, 4, 64], 2), ALU.subtract)
                          K.tt("dve", yn3, yn3, gst[:, 12:16].bc([128, 4, 64], 2), ALU.mult)
                          K.tt("dve", yn, yn, lnw[:, hg * 256:(hg + 1) * 256], ALU.mult)
                          K.tt("dve", yn, yn, lnb[:, hg * 256:(hg + 1) * 256], ALU.add)
                          K.tt("dve", bon.re("p (s e v) -> p s e v", s=2, e=2),
                               TOK[u][:, :, 0, :].re("p s (e v) -> p s e v", e=2),
                               gst[:, 16:20].re("p (s e) -> p s e", s=2).bc([128, 2, 2, 64], 3), ALU.mult)
                          K.tt("dve", yn, yn, bon, ALU.add)
                          K.tt("dve", ytb, yn, pgt[:, 0:256], ALU.mult)
                          pb = psum()
                          pbb = pb.bitcast(BF16)
                          for s in range(2):
                              K.tr(pbb[:, s * 128:(s + 1) * 128], ytb[:, s * 128:(s + 1) * 128], ident_b)
                          K.copy("act", yTs[:, :, c * CH:(c + 1) * CH], pbb[:, 0:256].re("p (s t) -> p s t", s=2))
                          yield
                  if out_tile:
                      c_lo = max(0, OC0 - ti * 4)
                      tk0 = (ti * 4 + c_lo - OC0) * CH
                      ncol = (4 - c_lo) * CH
                      for s in range(2):
                          K.dma("sp", yT_v[8 + hg * 2 + s, :, tk0:tk0 + ncol], yTs[:, s, c_lo * CH:4 * CH])
                  yield

              def conv(ti):
                  c_lo = max(0, OC0 - ti * 4)
                  tk0 = (ti * 4 + c_lo - OC0) * CH
                  ncol = (4 - c_lo) * CH
                  for i in range(8):
                      pB = project(load_w(28 + i * 3 + 0))
                      pC = project(load_w(28 + i * 3 + 1))
                      pH = project(load_w(28 + i * 3 + 2))
                      K.copy("act", csb, pC)
                      K.copy("act", chb[:, 0:2], ccar[:, i, :])
                      K.tt("dve", chb[:, 2:TILE + 2], csb, pH, ALU.mult)
                      K.copy("act", ccar[:, i, :], chb[:, TILE:TILE + 2])
                      K.ts("dve", csb, chb[:, 0:TILE], vecs[:, V_CW + i:V_CW + i + 1])
                      K.stt("dve", csb, chb[:, 1:TILE + 1], vecs[:, V_CW + 8 + i:V_CW + 9 + i], csb, ALU.mult, ALU.add)
                      K.stt("dve", csb, chb[:, 2:TILE + 2], vecs[:, V_CW + 16 + i:V_CW + 17 + i], csb, ALU.mult, ALU.add)
                      K.tt("dve", ycv, csb, pB, ALU.mult)
                      K.dma("sp", yT_v[i, :, tk0:tk0 + ncol], ycv[:, c_lo * CH:4 * CH])
                      yield

              def chain(*gens):
                  for g in gens:
                      yield from g

              def interleave(ga, gb):
                  la, lb = True, True
                  while la or lb:
                      if la:
                          try:
                              next(ga)
                          except StopIteration:
                              la = False
                      if lb:
                          try:
                              next(gb)
                          except StopIteration:
                              lb = False

              PIPE = os.environ.get("KPIPE", "1") == "1"
              units = [(ti, hg) for ti in range(NT) for hg in range(4)]

              def p_stream(ti, hg):
                  gens = []
                  if hg == 0:
                      gens.append(front(ti))
                  gens.append(prep(ti, hg))
                  return gens

              for g in chain(*p_stream(0, 0)):
                  pass
              for ui, (ti, hg) in enumerate(units):
                  nxt = []
                  if hg == 3 and ti >= OT0:
                      nxt.append(conv(ti))
                  if ui + 1 < len(units):
                      nxt += p_stream(*units[ui + 1])
                  if PIPE:
                      interleave(scan(ti, hg), chain(*nxt))
                  else:
                      for g in scan(ti, hg):
                          pass
                      for g in chain(*nxt):
                          pass
          _ck(8)
          S.barrier()

          with contextlib.ExitStack() as p15:
              wo_b = sbt(p15, "wo_b", [128, KC, D], BF16)
              K.dma("pool", wo_b.re("p k d -> p (k d)"), wout_v)
              gpm = sbt(p15, "gpm", [128, D], F32)
              gpf = sbt(p15, "gpf", [128, D], F32)
              K.dma("sp", gpm, V(bcv["g_post_mix"].partition_broadcast(128), R("bc3")))
              K.dma("sp", gpf, V(bcv["g_pre_ffn"].partition_broadcast(128), R("bc4")))
              ych = [sbt(p15, "ych%d" % i, [128, KC, CH], BF16) for i in range(2)]
              xin2 = [sbt(p15, "xin2_%d" % i, [128, D], F32) for i in range(2)]
              hsb = [sbt(p15, "hsb%d" % i, [128, D], F32) for i in range(2)]
              n2 = sbt(p15, "n2", [128, D], BF16)
              n2T = [sbt(p15, "n2T%d" % i, [128, KC, CH], BF16) for i in range(2)]
              st5 = sbt(p15, "st5", [128, 16], F32)
              def p15_load(oc):
                  K.dma("sp", ych[oc % 2], yT_v[:, :, oc * CH:(oc + 1) * CH].re("c p t -> p c t"))
                  K.dma("sp", xin2[oc % 2], xs_v[(OC0 + oc) * CH:(OC0 + oc + 1) * CH, :])

              p15_load(0)
              for oc in range(NOC):
                  yc = ych[oc % 2]
                  xi = xin2[oc % 2]
                  hs = hsb[oc % 2]
                  nt2 = n2T[oc % 2]
                  if oc + 1 < NOC:
                      p15_load(oc + 1)
                  pm = [psum() for _ in range(4)]
                  for blk in range(4):
                      for chc in range(KC):
                          K.mm(pm[blk], yc[:, chc, :], wo_b[:, chc, blk * 512:(blk + 1) * 512],
                               start=(chc == 0), stop=(chc == KC - 1))
                  K.memset("dve", st5[:, 0:4], 0.0)
                  for blk in range(4):
                      K.act(n2[:, blk * 512:(blk + 1) * 512], pm[blk], AF.Square, accum=st5[:, blk:blk + 1])
                  K.red("dve", st5[:, 4:5], st5[:, 0:4])
                  K.ts("dve", st5[:, 4:5], st5[:, 4:5], 1.0 / D, RMS_EPS, ALU.mult, ALU.add)
                  K.act(st5[:, 5:6], st5[:, 4:5], AF.Sqrt)
                  K.recip(st5[:, 6:7], st5[:, 5:6])
                  for blk in range(4):
                      sl = slice(blk * 512, (blk + 1) * 512)
                      K.stt("dve", hs[:, sl], pm[blk], st5[:, 6:7], gpm[:, sl], ALU.mult, ALU.mult)
                  K.tt("dve", hs, hs, xi, ALU.add)
                  K.dma("sp", h_v[oc * CH:(oc + 1) * CH, :], hs)
                  K.memset("dve", st5[:, 8:9], 0.0)
                  K.act(n2, hs, AF.Square, accum=st5[:, 8:9])
                  K.ts("dve", st5[:, 9:10], st5[:, 8:9], 1.0 / D, RMS_EPS, ALU.mult, ALU.add)
                  K.act(st5[:, 10:11], st5[:, 9:10], AF.Sqrt)
                  K.recip(st5[:, 11:12], st5[:, 10:11])
                  K.stt("dve", n2, hs, st5[:, 11:12], gpf, ALU.mult, ALU.mult)
                  for a in range(2):
                      pb = psum()
                      pbb = pb.bitcast(BF16)
                      for k8 in range(8):
                          kc = a * 8 + k8
                          K.tr(pbb[:, k8 * 128:(k8 + 1) * 128], n2[:, kc * 128:(kc + 1) * 128], ident_b)
                      K.copy("act", nt2[:, a * 8:(a + 1) * 8, :], pbb.re("p (k t) -> p k t", k=8))
                  if oc == 0:
                      K.ts("dve", nt2[:, :, 0:HALO], nt2[:, :, 0:HALO], hk)
                  K.dma("sp", n2T_v[:, :, oc * CH:(oc + 1) * CH].re("c p t -> p c t"), nt2)
          S.barrier()

          with contextlib.ExitStack() as p2:
              gpo = sbt(p2, "gpo", [128, D], F32)
              K.dma("sp", gpo, V(bcv["g_post_ffn"].partition_broadcast(128), R("bc5")))
              facc = sbt(p2, "facc", [128, STC, D], F32)
              n2s = sbt(p2, "n2s", [128, KC, STC * CH], BF16)
              wg_r = [sbt(p2, "wg%d" % i, [128, 2, KC, 128], BF16) for i in range(2)]
              wd_r = [sbt(p2, "wd%d" % i, [128, D], BF16) for i in range(4)]
              gbuf = sbt(p2, "gbuf", [128, STC * CH + 2], F32)
              cvt = sbt(p2, "cvt", [128, TILE], F32)
              slu = sbt(p2, "slu", [128, TILE], F32)
              actT = [sbt(p2, "actT%d" % i, [128, STC * CH], BF16) for i in range(4)]
              hld = sbt(p2, "hld", [128, D], F32)
              st6 = sbt(p2, "st6", [128, 8], F32)
              K.memset("dve", gbuf[:, 0:2], 0.0)
              st_starts = []
              s0 = 0
              while True:
                  st_starts.append(s0)
                  if s0 + STC >= NOC:
                      break
                  s0 += STC - 1
              for sti, cs0 in enumerate(st_starts):
                  nch = min(STC, NOC - cs0)
                  ntk = nch * CH
                  first_out = 0 if sti == 0 else 1
                  K.dma("sp", n2s[:, :, 0:ntk], n2T_v[:, :, cs0 * CH:cs0 * CH + ntk].re("c p t -> p c t"))
                  ttiles = [(a, min(TILE, ntk - a)) for a in range(0, ntk, TILE)]
                  for fb0 in range(0, NFB, 2):
                      ats, wds = [], []
                      for g2 in range(2):
                          fb = fb0 + g2
                          wgb = wg_r[fb % 2]
                          wdb = wd_r[fb % 4]
                          at = actT[fb % 4]
                          ats.append(at)
                          wds.append(wdb)
                          K.dma("pool", wgb.re("p g k j -> p g (k j)"), wgu_v[2 * fb:2 * fb + 2].re("g p f -> p g f"))
                          K.dma("pool", wdb, wdn_v[fb])
                          for (a0, tn) in ttiles:
                              pgg = psum()
                              pu = psum()
                              for kc in range(KC):
                                  K.mm(pgg[:, 0:tn], wgb[:, 0, kc, :], n2s[:, kc, a0:a0 + tn], start=(kc == 0), stop=(kc == KC - 1))
                              for kc in range(KC):
                                  K.mm(pu[:, 0:tn], wgb[:, 1, kc, :], n2s[:, kc, a0:a0 + tn], start=(kc == 0), stop=(kc == KC - 1))
                              K.copy("act", gbuf[:, 2 + a0:2 + a0 + tn], pgg[:, 0:tn])
                              K.ts("dve", cvt[:, 0:tn], gbuf[:, a0:a0 + tn], vecs[:, V_CF + fb:V_CF + fb + 1])
                              K.stt("dve", cvt[:, 0:tn], gbuf[:, a0 + 1:a0 + 1 + tn], vecs[:, V_CF + NFB + fb:V_CF + NFB + fb + 1],
                                    cvt[:, 0:tn], ALU.mult, ALU.add)
                              K.stt("dve", cvt[:, 0:tn], gbuf[:, a0 + 2:a0 + 2 + tn], vecs[:, V_CF + 2 * NFB + fb:V_CF + 2 * NFB + fb + 1],
                                    cvt[:, 0:tn], ALU.mult, ALU.add)
                              K.act(slu[:, 0:tn], cvt[:, 0:tn], AF.Silu)
                              K.tt("dve", at[:, a0:a0 + tn], slu[:, 0:tn], pu[:, 0:tn], ALU.mult)
                      for ci in range(nch):
                          for blk in range(4):
                              pf = psum()
                              K.mm(pf, ats[0][:, ci * CH:(ci + 1) * CH], wds[0][:, blk * 512:(blk + 1) * 512], start=True, stop=False)
                              K.mm(pf, ats[1][:, ci * CH:(ci + 1) * CH], wds[1][:, blk * 512:(blk + 1) * 512], start=False, stop=True)
                              sl = slice(blk * 512, (blk + 1) * 512)
                              if fb0 == 0:
                                  K.copy("act", facc[:, ci, sl], pf)
                              else:
                                  K.tt("dve", facc[:, ci, sl], facc[:, ci, sl], pf, ALU.add)
                  for ci in range(first_out, nch):
                      oc = cs0 + ci
                      K.dma("sp", hld, h_v[oc * CH:(oc + 1) * CH, :])
                      K.memset("dve", st6[:, 0:1], 0.0)
                      K.act(n2s.re("p k t -> p (k t)")[:, 0:D], facc[:, ci, :], AF.Square, accum=st6[:, 0:1])
                      K.ts("dve", st6[:, 1:2], st6[:, 0:1], 1.0 / D, RMS_EPS, ALU.mult, ALU.add)
                      K.act(st6[:, 2:3], st6[:, 1:2], AF.Sqrt)
                      K.recip(st6[:, 3:4], st6[:, 2:3])
                      K.stt("dve", facc[:, ci, :], facc[:, ci, :], st6[:, 3:4], gpo, ALU.mult, ALU.mult)
                      K.tt("dve", facc[:, ci, :], facc[:, ci, :], hld, ALU.add)
                      K.dma("sp", out_v[oc * CH:(oc + 1) * CH, :], facc[:, ci, :])
          S.barrier()
        except _Stop:
            pass
        S.barrier()
        S.emit()
    return nc


def _blk(w):
    return np.ascontiguousarray(w.reshape(KC, 128, 128).transpose(1, 0, 2)).reshape(128, KC * 128)


def _pad_cols(w, n):
    out = np.zeros((w.shape[0], n), w.dtype)
    out[:, :w.shape[1]] = w
    return out


def _consts():
    c = np.zeros((128, NCONST), np.float32)
    j = np.arange(128)[:, None]
    t = np.arange(128)[None, :]
    c[:, C_ID:C_ID + 128] = np.eye(128)
    strict = (t > j).astype(np.float32)
    incl = (t >= j).astype(np.float32)
    c[:, C_M4:C_M4 + 512] = np.concatenate([strict, incl, strict, incl], 1)
    low = (j > t).astype(np.float32)
    c[:, C_ML:C_ML + 512] = np.concatenate([low] * 4, 1)
    c[:, C_BO:C_BO + 128] = (j // 64 == t // 64).astype(np.float32)
    c[:, C_HS:C_HS + 2] = (j // 64 == np.arange(2)[None, :]).astype(np.float32)
    return c


def prepare(inputs, NT, RANGE):
    f = lambda k: np.asarray(inputs[k], np.float32)
    w_in = f("w_in")
    RB = 3072
    cols = []
    cols.append(_pad_cols(w_in[:, RB + 3072:RB + 3168], 128))
    cols.append(_pad_cols(w_in[:, RB + 3168:RB + 3264], 128))
    cols.append(w_in[:, RB + 3264:RB + 3392])
    cols.append(w_in[:, RB + 3392:RB + 3520])
    for i in range(8):
        for base in (0, 1024, 2048):
            cols.append(w_in[:, RB + base + i * 128:RB + base + (i + 1) * 128])
    for i in range(8):
        for base in (0, 1024, 2048):
            cols.append(w_in[:, base + i * 128:base + (i + 1) * 128])
    win = np.stack([_blk(c) for c in cols], 0)
    wgu_full = f("w_gate_up")
    wgu = np.stack([_blk(wgu_full[:, g * DFF + fb * 128:g * DFF + (fb + 1) * 128]) for fb in range(NFB) for g in range(2)], 0)
    wdn = np.ascontiguousarray(f("w_down").reshape(NFB, 128, D))
    wout = np.ascontiguousarray(f("w_out").reshape(KC, 128, D).transpose(1, 0, 2)).reshape(128, KC * D)
    lora = np.zeros((128, 4 * RW), np.float32)
    lora[:96, 0:RW] = f("w_decay_up")
    lora[:96, RW:2 * RW] = f("a_up")
    gup = f("g_up")
    lora[:, 2 * RW:3 * RW] = gup[0:128]
    lora[:, 3 * RW:4 * RW] = gup[128:256]
    mu = f("mu_rwkv")
    vecs = np.zeros((128, NV), np.float32)
    vecs[:96, V_MU + 0] = mu[3072:3168]
    vecs[:96, V_MU + 1] = mu[3168:3264]
    vecs[:, V_MU + 2] = mu[3264:3392]
    vecs[:, V_MU + 3] = mu[3392:3520]
    for i in range(8):
        for qi, base in enumerate((0, 1024, 2048)):
            vecs[:, V_MU + 4 + i * 3 + qi] = mu[base + i * 128:base + (i + 1) * 128]
    for name, col in (("w0", V_W0), ("a0", V_A0), ("k_k", V_KK), ("k_a", V_KA)):
        vecs[:, col:col + 8] = f(name).reshape(8, 128).T
    vecs[:, V_RK:V_RK + 8] = f("r_k").reshape(8, 128).T
    cw = f("conv_w")
    for tap in range(3):
        vecs[:, V_CW + tap * 8:V_CW + tap * 8 + 8] = cw[tap].reshape(8, 128).T
    cfw = f("conv_ffn")
    for tap in range(3):
        vecs[:, V_CF + tap * NFB:V_CF + (tap + 1) * NFB] = cfw[tap].reshape(NFB, 128).T
    shared = {"win": win, "wgu": wgu, "wdn": wdn, "wout": wout, "lora": lora, "consts": _consts()}
    for n in ("g_pre_mix", "g_post_mix", "g_pre_ffn", "g_post_ffn", "ln_x_w", "ln_x_b"):
        shared[n] = f(n).reshape(1, -1)
    x = f("x")
    meta = f("meta_tokens")
    SEQ = NT * TILE
    in_maps = []
    for c in range(8):
        b, q = c // 4, c % 4
        seq = np.concatenate([meta, x[b]], 0)
        n_real = (q + 1) * RANGE
        xs = np.zeros((SEQ, D), np.float32)
        xs[SEQ - n_real:] = seq[:n_real]
        v = vecs.copy()
        v[:, V_HK] = 0.0 if q == 0 else 1.0
        m = dict(shared)
        m["xs"] = xs
        m["vecs"] = v
        in_maps.append(m)
    return in_maps


def kernel(**inputs):
    x = np.asarray(inputs["x"])
    B, L, _ = x.shape
    tot = N_META + L
    RANGE = tot // 4
    NT = -(-(tot + 2) // TILE)
    nc = build_nc(NT, RANGE)
    in_maps = prepare(inputs, NT, RANGE)
    dbg = os.environ.get("KCORES")
    if dbg is not None:
        sel = [int(t) for t in dbg.split(",")]
        res1 = run_bass_kernel_spmd(nc, [in_maps[c] for c in sel], core_ids=list(range(len(sel))))
        return {c: {k: np.asarray(v) for k, v in res1.results[i].items()} for i, c in enumerate(sel)}
    res = run_bass_kernel_spmd(nc, in_maps, core_ids=list(range(8)))
    SEQ = NT * TILE
    OC0 = (SEQ - RANGE - 2) // CH
    off = SEQ - RANGE - OC0 * CH
    full = np.zeros((B, tot, D), np.float32)
    for c in range(8):
        b, q = c // 4, c % 4
        o = np.asarray(res.results[c]["out"])
        full[b, q * RANGE:(q + 1) * RANGE] = o[off:off + RANGE]
    return np.ascontiguousarray(full[:, N_META:])
```

```python
import contextlib
import math
import numpy as np
import concourse.bass as bass
import concourse.mybir as mybir
from concourse.bass_utils import run_bass_kernel_spmd

F32 = mybir.dt.float32
BF16 = mybir.dt.bfloat16
ALU = mybir.AluOpType
AF = mybir.ActivationFunctionType
AX = mybir.AxisListType

D = 2048
KC = 16
TILE = 512
CH = 128
RW = 1024
DFF = 5632
NFB = DFF // 128
N_META = 16
RMS_EPS = 1e-6
GN_EPS = 64e-5
DECAY_C = math.exp(-0.5)
STC = 9

V_MU = 0
V_W0 = 28
V_A0 = 36
V_KK = 44
V_KA = 52
V_RK = 60
V_CW = 68
V_CF = 92
V_HK = 92 + 3 * NFB
NV = V_HK + 1
C_ID = 0
C_M4 = 128
C_ML = 640
C_BO = 1152
C_HS = 1280
NCONST = 1282


class Res:
    __slots__ = ("name", "w", "r", "dsem", "dcnt")

    def __init__(self, name):
        self.name = name
        self.w = None
        self.r = {}
        self.dsem = None
        self.dcnt = 0


class V:
    __slots__ = ("ap", "res")

    def __init__(self, ap, res):
        self.ap = ap
        self.res = res

    def __getitem__(self, idx):
        return V(self.ap[idx], self.res)

    def re(self, pat, **kw):
        return V(self.ap.rearrange(pat, **kw), self.res)

    def bitcast(self, dt):
        return V(self.ap.bitcast(dt), self.res)

    def bc(self, shape, axis):
        return V(self.ap.unsqueeze(axis).to_broadcast(list(shape)), self.res)


class Sched:
    ENG = ("pe", "act", "dve", "pool", "sp")

    def __init__(self, nc, stack):
        self.nc = nc
        self.stack = stack
        self.ops = {e: [] for e in self.ENG}
        self.sem = {e: stack.enter_context(nc.semaphore("s_" + e)) for e in self.ENG}
        self.cnt = {e: 0 for e in self.ENG}
        self.seen = {e: {} for e in self.ENG}
        self.allres = []

    def res(self, name):
        r = Res(name)
        self.allres.append(r)
        return r

    def _dsem(self, res):
        if res.dsem is None:
            res.dsem = self.stack.enter_context(self.nc.semaphore("d_" + res.name))
        return res.dsem

    def _deps(self, eng, reads, writes, seen_key=None):
        deps = {}

        def add(t, kind):
            if t is None:
                return
            key, val, en = t
            if en == eng and (kind != "raw" or eng == "pe"):
                return
            if deps.get(key, 0) < val:
                deps[key] = val

        for r in reads:
            add(r.w, "raw")
        for w in writes:
            add(w.w, "waw")
            for key, (val, en) in w.r.items():
                add((key, val, en), "war")
        out = []
        seen = self.seen[seen_key or eng]
        for key, val in deps.items():
            if seen.get(key, 0) >= val:
                continue
            seen[key] = val
            out.append((key, val))
        return out

    def op(self, eng, fn, reads=(), writes=()):
        if _HALT[0]:
            return
        waits = self._deps(eng, reads, writes)
        self.cnt[eng] += 1
        c = self.cnt[eng]
        key = ("c", eng)
        self.ops[eng].append((waits, fn, ("c", c)))
        for r in reads:
            r.r[key] = (c, eng)
        for w in writes:
            w.w = (key, c, eng)
            w.r = {}

    def dma(self, q, fn, reads=(), writes=()):
        if _HALT[0]:
            return
        waits = self._deps("dma", reads, writes, seen_key=q)
        anchor = writes[0] if writes else reads[0]
        sem = self._dsem(anchor)
        anchor.dcnt += 16
        c = anchor.dcnt
        key = ("d", sem)
        self.ops[q].append((waits, fn, ("d", sem)))
        for r in reads:
            r.r[key] = (c, "dma")
        for w in writes:
            w.w = (key, c, "dma")
            w.r = {}

    def barrier(self):
        for e in self.ENG:
            waits = self._deps(e + "_bar", self.allres, self.allres, seen_key=e)
            if waits:
                self.ops[e].append((waits, None, None))

    def emit(self):
        nc = self.nc
        ops = self.ops
        needed = {e: set() for e in self.ENG}
        for e in self.ENG:
            for waits, fn, inc in ops[e]:
                for key, val in waits:
                    if key[0] == "c":
                        needed[key[1]].add(val)
        rank = {e: {v: i + 1 for i, v in enumerate(sorted(needed[e]))} for e in self.ENG}
        sem = self.sem

        def run(engh, lst):
            for waits, fn, inc in lst:
                for key, val in waits:
                    if key[0] == "c":
                        engh.wait_ge(sem[key[1]], rank[key[1]][val])
                    else:
                        engh.wait_ge(key[1], val)
                if fn is not None:
                    ins = fn(engh)
                    if inc[0] == "d":
                        ins.then_inc(inc[1], 16)
                    elif inc[1] in rank[self._cur]:
                        ins.then_inc(sem[self._cur], 1)

        with nc.Block() as block:
            @block.tensor
            def _(e):
                self._cur = "pe"
                run(e, ops["pe"])

            @block.scalar
            def _(e):
                self._cur = "act"
                run(e, ops["act"])

            @block.vector
            def _(e):
                self._cur = "dve"
                run(e, ops["dve"])

            @block.gpsimd
            def _(e):
                self._cur = "pool"
                run(e, ops["pool"])

            @block.sync
            def _(e):
                self._cur = "sp"
                run(e, ops["sp"])


def _rw(reads, writes):
    rs, ws = [], []
    for v in reads:
        if isinstance(v, V) and v.res not in rs:
            rs.append(v.res)
    for v in writes:
        if v.res not in ws:
            ws.append(v.res)
    return rs, ws


def _a(x):
    return x.ap if isinstance(x, V) else x


class KB:
    def __init__(self, nc, S):
        self.nc = nc
        self.S = S

    def mm(self, out, lhsT, rhs, start=True, stop=True):
        rs, ws = _rw([lhsT, rhs], [out])
        self.S.op("pe", lambda e: e.matmul(out.ap, lhsT=lhsT.ap, rhs=rhs.ap, start=start, stop=stop), rs, ws)

    def tr(self, out, in_, ident):
        rs, ws = _rw([in_, ident], [out])
        self.S.op("pe", lambda e: e.transpose(out=out.ap, in_=in_.ap, identity=ident.ap), rs, ws)

    def act(self, out, in_, func, bias=None, scale=None, accum=None):
        rs, ws = _rw([in_, bias, scale], [out] + ([accum] if accum is not None else []))
        kw = {}
        if bias is not None:
            kw["bias"] = _a(bias)
        if scale is not None:
            kw["scale"] = _a(scale)
        if accum is not None:
            kw["accum_out"] = accum.ap
        self.S.op("act", lambda e: e.activation(out=out.ap, in_=in_.ap, func=func, **kw), rs, ws)

    def copy(self, eng, out, in_):
        rs, ws = _rw([in_], [out])
        if eng == "act":
            self.S.op("act", lambda e: e.activation(out=out.ap, in_=in_.ap, func=AF.Copy), rs, ws)
        else:
            self.S.op(eng, lambda e: e.tensor_copy(out=out.ap, in_=in_.ap), rs, ws)

    def tt(self, eng, out, a, b, op):
        rs, ws = _rw([a, b], [out])
        self.S.op(eng, lambda e: e.tensor_tensor(out=out.ap, in0=a.ap, in1=b.ap, op=op), rs, ws)

    def ts(self, eng, out, a, s1, s2=None, op0=ALU.mult, op1=None):
        rs, ws = _rw([a, s1, s2], [out])
        if op1 is None:
            self.S.op(eng, lambda e: e.tensor_scalar(out=out.ap, in0=a.ap, scalar1=_a(s1), scalar2=0.0, op0=op0, op1=ALU.add), rs, ws)
        else:
            self.S.op(eng, lambda e: e.tensor_scalar(out=out.ap, in0=a.ap, scalar1=_a(s1), scalar2=_a(s2), op0=op0, op1=op1), rs, ws)

    def stt(self, eng, out, a, scalar, b, op0, op1):
        rs, ws = _rw([a, scalar, b], [out])
        self.S.op(eng, lambda e: e.scalar_tensor_tensor(out=out.ap, in0=a.ap, scalar=_a(scalar), in1=b.ap, op0=op0, op1=op1), rs, ws)

    def red(self, eng, out, in_, op=ALU.add):
        rs, ws = _rw([in_], [out])
        self.S.op(eng, lambda e: e.tensor_reduce(out=out.ap, in_=in_.ap, axis=AX.X, op=op), rs, ws)

    def scan(self, out, d0, d1):
        rs, ws = _rw([d0, d1], [out])
        self.S.op("dve", lambda e: e.tensor_tensor_scan(out=out.ap, data0=d0.ap, data1=d1.ap, initial=0.0,
                                                        op0=ALU.mult, op1=ALU.add), rs, ws)

    def recip(self, out, in_):
        rs, ws = _rw([in_], [out])
        self.S.op("dve", lambda e: e.reciprocal(out=out.ap, in_=in_.ap), rs, ws)

    def memset(self, eng, out, val):
        rs, ws = _rw([], [out])
        self.S.op(eng, lambda e: e.memset(out.ap, val), rs, ws)

    def dma(self, q, out, in_):
        rs, ws = _rw([in_], [out])
        self.S.dma(q, lambda e: e.dma_start(out=out.ap, in_=in_.ap), rs, ws)


import os
_STOP = int(os.environ.get("KSTOP", "99"))


class _Stop(Exception):
    pass


_HALT = [False]


def _ck(k):
    if _STOP == k:
        _HALT[0] = True


def build_nc(NT, RANGE):
    SEQ = NT * TILE
    OWN0 = SEQ - RANGE - 2
    OC0 = OWN0 // CH
    NCHK = SEQ // CH
    NOC = NCHK - OC0
    HALO = OWN0 - OC0 * CH + 2
    NTOK2 = NOC * CH
    OT0 = OC0 // 4

    nc = bass.Bass("TRN2", target_bir_lowering=False)
    _HALT[0] = False
    dr = {}

    def din(name, shape):
        dr[name] = nc.dram_tensor(name, list(shape), F32, kind="ExternalInput").ap()
        return dr[name]

    xs = din("xs", [SEQ, D])
    win = din("win", [52, 128, KC * 128])
    wgu = din("wgu", [NFB * 2, 128, KC * 128])
    wdn = din("wdn", [NFB, 128, D])
    wout = din("wout", [128, KC * D])
    lora = din("lora", [128, 4 * RW])
    vecs_d = din("vecs", [128, NV])
    const_d = din("consts", [128, NCONST])
    bcv = {n: din(n, [1, w]) for n, w in (("g_pre_mix", D), ("g_post_mix", D), ("g_pre_ffn", D),
                                          ("g_post_ffn", D), ("ln_x_w", RW), ("ln_x_b", RW))}
    out_d = nc.dram_tensor("out", [NTOK2, D], F32, kind="ExternalOutput").ap()
    yT_d = nc.dram_tensor("yT_s", [KC, 128, NTOK2], BF16, kind="Internal").ap()
    n2T_d = nc.dram_tensor("n2T_s", [KC, 128, NTOK2], BF16, kind="Internal").ap()
    h_d = nc.dram_tensor("h_s", [NTOK2, D], F32, kind="Internal").ap()

    with contextlib.ExitStack() as top:
        S = Sched(nc, top)
        K = KB(nc, S)
        R = S.res
        xs_v = V(xs, R("xs"))
        win_v = V(win, R("win"))
        wgu_v = V(wgu, R("wgu"))
        wdn_v = V(wdn, R("wdn"))
        wout_v = V(wout, R("wout"))
        lora_v = V(lora, R("lora"))
        out_v = V(out_d, R("out"))
        yT_v = V(yT_d, R("yT"))
        n2T_v = V(n2T_d, R("n2T"))
        h_v = V(h_d, R("h"))

        banks = []
        for i in range(8):
            t = top.enter_context(nc.psum_tensor("pb%d" % i, [128, 512], F32))
            banks.append(V(t[:, :], R("pb%d" % i)))
        bank_i = [0]

        def psum():
            b = banks[bank_i[0] % 8]
            bank_i[0] += 1
            return b

        def sbt(stack, name, shape, dt):
            t = stack.enter_context(nc.sbuf_tensor("sb_" + name, list(shape), dt))
            return V(t[tuple(slice(None) for _ in shape)], R(name))

        vecs = sbt(top, "vecs", [128, NV], F32)
        vder = sbt(top, "vder", [128, 28 + 8], F32)
        cf = sbt(top, "cf", [128, NCONST], F32)
        cb = sbt(top, "cb", [128, NCONST], BF16)
        ones = sbt(top, "ones", [128, 128], F32)
        K.dma("sp", vecs, V(vecs_d, R("vecs_d")))
        K.dma("sp", cf, V(const_d, R("const_d")))
        K.dma("pool", cb, V(const_d, R("const_d2")))
        K.memset("dve", ones, 1.0)
        K.ts("dve", vder[:, 0:28], vecs[:, V_MU:V_MU + 28], -1.0, 1.0, ALU.mult, ALU.add)
        K.ts("dve", vder[:, 28:36], vecs[:, V_KA:V_KA + 8], -1.0, 1.0, ALU.mult, ALU.add)
        ident_b = cb[:, C_ID:C_ID + 128]
        mask4 = cb[:, C_M4:C_M4 + 512]
        maskL4 = cb[:, C_ML:C_ML + 512]
        blockones = cf[:, C_BO:C_BO + 128]
        headsel_b = cb[:, C_HS:C_HS + 2]
        hk = vecs[:, V_HK:V_HK + 1]

        def rms_rstd(src, rstd, junk, nbanks=None):
            pass

        try:
          _ck(1)
          with contextlib.ExitStack() as p1:
              gpre = sbt(p1, "gpre", [128, D], F32)
              lnw = sbt(p1, "lnw", [128, RW], F32)
              lnb = sbt(p1, "lnb", [128, RW], F32)
              K.dma("sp", gpre, V(bcv["g_pre_mix"].partition_broadcast(128), R("bc0")))
              K.dma("sp", lnw, V(bcv["ln_x_w"].partition_broadcast(128), R("bc1")))
              K.dma("sp", lnb, V(bcv["ln_x_b"].partition_broadcast(128), R("bc2")))
              lw_b = sbt(p1, "lw_b", [128, 4 * RW], BF16)
              K.dma("pool", lw_b, lora_v)
              wring = [sbt(p1, "wr%d" % i, [128, KC, 128], BF16) for i in range(4)]
              wr_i = [0]
              xin_r = [sbt(p1, "xin%d" % i, [128, D], F32) for i in range(2)]
              xn = sbt(p1, "xn", [128, D], BF16)
              st4 = sbt(p1, "st4", [128, 8], F32)
              nT = sbt(p1, "nT", [128, KC, TILE], BF16)
              carry = sbt(p1, "carry", [128, 28], F32)
              K.memset("dve", carry, 0.0)
              ccar = sbt(p1, "ccar", [128, 8, 2], F32)
              K.memset("dve", ccar, 0.0)
              tw = sbt(p1, "tw", [128, TILE], BF16)
              adb = sbt(p1, "adb", [128, TILE], BF16)
              sg2 = [sbt(p1, "sg%d" % i, [128, 2, TILE], BF16) for i in range(2)]
              ztmp = sbt(p1, "ztmp", [128, TILE], F32)
              zr = sbt(p1, "zr", [128, TILE], F32)
              zk = sbt(p1, "zk", [128, TILE], F32)
              zv = sbt(p1, "zv", [128, TILE], F32)
              fA = [sbt(p1, "fA%d" % i, [128, TILE], F32) for i in range(8)]
              AR2 = [sbt(p1, "AR%d" % i, [128, 2, 4, 256], BF16) for i in range(2)]
              BK2 = [sbt(p1, "BK%d" % i, [128, 2, 4, 256], BF16) for i in range(2)]
              TF2 = [sbt(p1, "TF%d" % i, [128, 2, 3, TILE], BF16) for i in range(2)]
              rkb2 = [sbt(p1, "rkb%d" % i, [128, 2, TILE], BF16) for i in range(2)]
              PC2 = [sbt(p1, "PCd%d" % i, [128, 2, 4], F32) for i in range(2)]
              TOK = [sbt(p1, "TOK%d" % i, [128, 2, 4, 128], BF16) for i in range(2)]
              SA = [sbt(p1, "SA%d" % i, [128, 4, 512], BF16) for i in range(2)]
              SN = [[sbt(p1, "SN%d_%d" % (i, j), [128, 4, 128], BF16) for j in range(2)] for i in range(2)]
              SL = [[sbt(p1, "SL%d_%d" % (i, j), [128, 4, 128], BF16) for j in range(2)] for i in range(2)]
              WW = [[sbt(p1, "WW%d_%d" % (i, j), [128, 4, 128], BF16) for j in range(2)] for i in range(2)]
              GT = [sbt(p1, "GT%d" % i, [128, 2, 64], BF16) for i in range(2)]
              RH = [sbt(p1, "RH%d" % i, [128, 2, 128], BF16) for i in range(2)]
              Tst = sbt(p1, "Tst", [128, 8, 64], F32)
              Tbf = [sbt(p1, "Tbf%d" % i, [128, 8, 64], BF16) for i in range(2)]
              K.memset("dve", Tst, 0.0)
              K.memset("dve", Tbf[0], 0.0)
              K.memset("dve", Tbf[1], 0.0)
              tb_i = [[0] * 8]
              ysb = sbt(p1, "ysb", [128, 256], F32)
              ysq = sbt(p1, "ysq", [128, 256], F32)
              yn = sbt(p1, "yn", [128, 256], F32)
              bon = sbt(p1, "bon", [128, 256], F32)
              gst = sbt(p1, "gst", [128, 24], F32)
              ytb = sbt(p1, "ytb", [128, 256], BF16)
              yTs = sbt(p1, "yTs", [128, 2, TILE], BF16)
              csb = sbt(p1, "csb", [128, TILE], F32)
              chb = sbt(p1, "chb", [128, TILE + 2], F32)
              ycv = sbt(p1, "ycv", [128, TILE], BF16)

              def load_w(cbi):
                  w = wring[wr_i[0] % 4]
                  wr_i[0] += 1
                  K.dma("pool", w.re("p k j -> p (k j)"), win_v[cbi])
                  return w

              def project(w, t_cols=TILE):
                  pz = psum()
                  for kc in range(KC):
                      K.mm(pz[:, 0:t_cols], w[:, kc, :], nT[:, kc, 0:t_cols], start=(kc == 0), stop=(kc == KC - 1))
                  return pz

              def shift(pz, col, out):
                  mu = vecs[:, V_MU + col:V_MU + col + 1]
                  omu = vder[:, col:col + 1]
                  K.act(ztmp, pz, AF.Copy, scale=omu)
                  K.stt("dve", out[:, 1:TILE], pz[:, 0:TILE - 1], mu, ztmp[:, 1:TILE], ALU.mult, ALU.add)
                  K.stt("dve", out[:, 0:1], carry[:, col:col + 1], mu, ztmp[:, 0:1], ALU.mult, ALU.add)
                  K.copy("act", carry[:, col:col + 1], pz[:, TILE - 1:TILE])

              def front(ti):
                  t0 = ti * TILE
                  sg = sg2[ti % 2]
                  for j in range(4):
                      xin = xin_r[j % 2]
                      K.dma("sp", xin, xs_v[t0 + j * CH:t0 + (j + 1) * CH, :])
                      K.memset("dve", st4[:, 0:1], 0.0)
                      K.act(xn, xin, AF.Square, accum=st4[:, 0:1])
                      K.ts("dve", st4[:, 1:2], st4[:, 0:1], 1.0 / D, RMS_EPS, ALU.mult, ALU.add)
                      K.act(st4[:, 2:3], st4[:, 1:2], AF.Sqrt)
                      K.recip(st4[:, 3:4], st4[:, 2:3])
                      K.stt("dve", xn, xin, st4[:, 3:4], gpre, ALU.mult, ALU.mult)
                      yield
                      for a in range(2):
                          pb = psum()
                          pbb = pb.bitcast(BF16)
                          for k8 in range(8):
                              kc = a * 8 + k8
                              K.tr(pbb[:, k8 * 128:(k8 + 1) * 128], xn[:, kc * 128:(kc + 1) * 128], ident_b)
                          K.copy("act" if a == 0 else "dve", nT[:, a * 8:(a + 1) * 8, j * CH:(j + 1) * CH],
                                 pbb.re("p (k t) -> p k t", k=8))
                          yield
                  pz = project(load_w(0))
                  shift(pz, 0, zr)
                  K.act(tw, zr, AF.Tanh)
                  yield
                  pz = project(load_w(1))
                  shift(pz, 1, zr)
                  K.copy("act", adb, zr)
                  yield
                  for gi in range(2):
                      if ti < OT0 - 1:
                          break
                      pz = project(load_w(2 + gi))
                      shift(pz, 2 + gi, zr)
                      K.act(sg[:, gi, :], zr, AF.Sigmoid)
                      yield

              def prep(ti, hg):
                  out_tile = ti >= OT0
                  par = (ti * 4 + hg) % 2
                  AR, BK, TF, rkb, PC = AR2[par], BK2[par], TF2[par], rkb2[par], PC2[par]
                  for s in range(2):
                      hp = hg * 2 + s
                      for qi, dst in enumerate((zr, zk, zv)):
                          if qi == 0 and ti < OT0 - 1:
                              continue
                          pz = project(load_w(4 + hp * 3 + qi))
                          shift(pz, 4 + hp * 3 + qi, dst)
                          yield
                      plw = psum()
                      K.mm(plw, lw_b[:, hp * 128:(hp + 1) * 128], tw)
                      pa = psum()
                      K.mm(pa, lw_b[:, RW + hp * 128:RW + (hp + 1) * 128], adb)
                      sw, aa, cs, kk, t1, t2, t3, t4 = fA
                      K.act(sw, plw, AF.Sigmoid, bias=vecs[:, V_W0 + hp:V_W0 + hp + 1])
                      K.act(aa, pa, AF.Sigmoid, bias=vecs[:, V_A0 + hp:V_A0 + hp + 1])
                      yield
                      for c in range(4):
                          K.scan(cs[:, c * CH:(c + 1) * CH], ones, sw[:, c * CH:(c + 1) * CH])
                      K.ts("dve", kk, zk, vecs[:, V_KK + hp:V_KK + hp + 1])
                      K.tt("dve", t1, kk, kk, ALU.mult)
                      yield
                      pss = psum()
                      K.mm(pss, blockones, t1)
                      K.act(t1, pss, AF.Sqrt)
                      K.ts("dve", t1, t1, 1e-12, None, ALU.max)
                      K.recip(t2, t1)
                      K.tt("dve", kk, kk, t2, ALU.mult)
                      yield
                      K.ts("dve", t1, aa, vecs[:, V_KA + hp:V_KA + hp + 1], vder[:, 28 + hp:29 + hp], ALU.mult, ALU.add)
                      K.tt("dve", t1, zk, t1, ALU.mult)
                      K.tt("dve", aa, kk, aa, ALU.mult)
                      yield
                      K.tt("dve", t2, cs, sw, ALU.subtract)
                      K.act(t2, t2, AF.Exp, scale=-DECAY_C)
                      K.stt("dve", AR[:, s, :, 0:128], kk.re("p (c t) -> p c t", c=4), -1.0,
                            t2.re("p (c t) -> p c t", c=4), ALU.mult, ALU.mult)
                      yield
                      if out_tile:
                          K.act(t3, cs, AF.Exp, scale=-DECAY_C)
                          K.tt("dve", AR[:, s, :, 128:256], zr.re("p (c t) -> p c t", c=4),
                               t3.re("p (c t) -> p c t", c=4), ALU.mult)
                      yield
                      K.act(t3, cs, AF.Exp, scale=DECAY_C)
                      K.tt("dve", BK[:, s, :, 0:128], aa.re("p (c t) -> p c t", c=4),
                           t3.re("p (c t) -> p c t", c=4), ALU.mult)
                      K.tt("dve", BK[:, s, :, 128:256], t1.re("p (c t) -> p c t", c=4),
                           t3.re("p (c t) -> p c t", c=4), ALU.mult)
                      yield
                      K.ts("dve", st4[:, 4:8], cs.re("p (c t) -> p c t", c=4)[:, :, CH - 1], -DECAY_C)
                      K.act(PC[:, s, :], st4[:, 4:8], AF.Exp)
                      for c in range(4):
                          K.act(t4[:, c * CH:(c + 1) * CH], cs[:, c * CH:(c + 1) * CH], AF.Exp,
                                scale=DECAY_C, bias=st4[:, 4 + c:5 + c])
                      yield
                      K.tt("dve", TF[:, s, 1, :], aa, t4, ALU.mult)
                      K.tt("dve", TF[:, s, 2, :], t1, t4, ALU.mult)
                      K.copy("act", TF[:, s, 0, :], zv)
                      if out_tile:
                          K.tt("dve", t2, zr, t1, ALU.mult)
                          K.ts("dve", rkb[:, s, :], t2, vecs[:, V_RK + hp:V_RK + hp + 1])
                      yield

              def scan(ti, hg):
                  out_tile = ti >= OT0
                  par = (ti * 4 + hg) % 2
                  AR, BK, TF, rkb, PC = AR2[par], BK2[par], TF2[par], rkb2[par], PC2[par]
                  sg = sg2[ti % 2]
                  for cp in range(2):
                      cl = [2 * cp, 2 * cp + 1]
                      for u, c in enumerate(cl):
                          pb = psum()
                          pbb = pb.bitcast(BF16).re("p (s q t) -> p s q t", s=2, q=4)
                          for s in range(2):
                              for q in range(3):
                                  K.tr(pbb[:, s, q, :], TF[:, s, q, c * CH:(c + 1) * CH], ident_b)
                              K.tr(pbb[:, s, 3, :], AR[:, s, c, 0:128], ident_b)
                          K.copy("act", TOK[u], pbb)
                          yield
                      for u, c in enumerate(cl):
                          for hl in range(4):
                              s, e = hl // 2, hl % 2
                              ps_ = slice(64 * e, 64 * e + 64)
                              pa_ = psum()
                              if out_tile:
                                  K.mm(pa_[:, 0:256], BK[ps_, s, c, 0:128], AR[ps_, s, c, :])
                                  K.mm(pa_[:, 256:512], BK[ps_, s, c, 128:256], AR[ps_, s, c, :])
                                  K.tt("dve", SA[u][:, hl, :], pa_, mask4, ALU.mult)
                              else:
                                  K.mm(pa_[:, 0:128], BK[ps_, s, c, 0:128], AR[ps_, s, c, 0:128])
                                  K.mm(pa_[:, 256:384], BK[ps_, s, c, 128:256], AR[ps_, s, c, 0:128])
                                  K.tt("dve", SA[u][:, hl, :].re("p (a b) -> p a b", a=2)[:, :, 0:128],
                                       pa_.re("p (a b) -> p a b", a=2)[:, :, 0:128],
                                       mask4.re("p (a b) -> p a b", a=2)[:, :, 0:128], ALU.mult)
                          yield
                          ple = [psum(), psum()]
                          for hl in range(4):
                              s, e = hl // 2, hl % 2
                              ps_ = slice(64 * e, 64 * e + 64)
                              K.mm(ple[e][:, s * 128:(s + 1) * 128], AR[ps_, s, c, 0:128], BK[ps_, s, c, 0:128])
                          for e in range(2):
                              K.tt("dve", SL[u][0].re("p (s e) t -> p s e t", e=2)[:, :, e, :],
                                   ple[e][:, 0:256].re("p (s t) -> p s t", s=2),
                                   maskL4[:, 0:256].re("p (s t) -> p s t", s=2), ALU.mult)
                          px = psum()
                          for hl in range(4):
                              s, e = hl // 2, hl % 2
                              K.mm(px[:, hl * 64:(hl + 1) * 64], SA[u][:, hl, 256:384], TOK[u][:, s, 0, 64 * e:64 * e + 64])
                          K.copy("act", WW[u][0][:, :, 64:128], px[:, 0:256].re("p (h v) -> p h v", h=4))
                          K.copy("act", WW[u][0].re("p (s e) k -> p s e k", s=2)[:, :, :, 0:64],
                                 TOK[u][:, :, 3, :].re("p s (e k) -> p s e k", e=2))
                          yield
                      for i in range(7):
                          for u, c in enumerate(cl):
                              n_cur = SA[u][:, :, 0:128] if i == 0 else SN[u][i % 2]
                              l_cur = SL[u][i % 2]
                              w_cur = WW[u][i % 2]
                              pw = psum()
                              for hl in range(4):
                                  K.mm(pw[:, hl * 128:(hl + 1) * 128], n_cur[:, hl, :], w_cur[:, hl, :])
                              K.tt("dve", WW[u][(i + 1) % 2].re("p h t -> p (h t)"), pw,
                                   w_cur.re("p h t -> p (h t)"), ALU.add)
                              if i < 6:
                                  pn = psum()
                                  for hl in range(4):
                                      K.mm(pn[:, hl * 128:(hl + 1) * 128], l_cur[:, hl, :], n_cur[:, hl, :])
                                  K.copy("act", SN[u][(i + 1) % 2].re("p h t -> p (h t)"), pn)
                              if i < 5:
                                  pl_ = psum()
                                  for hl in range(4):
                                      K.mm(pl_[:, hl * 128:(hl + 1) * 128], n_cur[:, hl, :], l_cur[:, hl, :])
                                  K.copy("act", SL[u][(i + 1) % 2].re("p h t -> p (h t)"), pl_)
                              yield
                      for u, c in enumerate(cl):
                          Wf = WW[u][1]
                          chunk_g = ti * 4 + c
                          want_y = chunk_g >= OC0
                          pg = psum()
                          for hl in range(4):
                              s, e = hl // 2, hl % 2
                              K.mm(pg[64 * e:64 * e + 64, s * 64:(s + 1) * 64], Wf[:, hl, 0:64],
                                   TOK[u][:, s, 1, 64 * e:64 * e + 64])
                          K.copy("act", GT[u].re("p s k -> p (s k)"), pg[:, 0:128])
                          tcur = Tbf[tb_i[0][hg] % 2]
                          tnew = Tbf[(tb_i[0][hg] + 1) % 2]
                          if want_y:
                              pr = psum()
                              for hl in range(4):
                                  s, e = hl // 2, hl % 2
                                  K.mm(pr[64 * e:64 * e + 64, s * 128:(s + 1) * 128], Wf[:, hl, 0:64], SA[u][:, hl, 128:256])
                              K.tt("dve", RH[u], pr[:, 0:256].re("p (s t) -> p s t", s=2), AR[:, :, c, 128:256], ALU.add)
                              py = psum()
                              for hl in range(4):
                                  s, e = hl // 2, hl % 2
                                  ps_ = slice(64 * e, 64 * e + 64)
                                  o = py[:, hl * 64:(hl + 1) * 64]
                                  K.mm(o, SA[u][:, hl, 128:256], Wf[:, hl, 64:128], start=True, stop=False)
                                  K.mm(o, SA[u][:, hl, 384:512], TOK[u][:, s, 0, 64 * e:64 * e + 64], start=False, stop=False)
                                  K.mm(o, RH[u][ps_, s, :], tcur[ps_, hg * 2 + s, :], start=False, stop=True)
                          pt = psum()
                          for hl in range(4):
                              s, e = hl // 2, hl % 2
                              ps_ = slice(64 * e, 64 * e + 64)
                              o = pt[ps_, s * 64:(s + 1) * 64]
                              K.mm(o, TOK[u][:, s, 1, 64 * e:64 * e + 64], Wf[:, hl, 64:128], start=True, stop=False)
                              K.mm(o, TOK[u][:, s, 2, 64 * e:64 * e + 64], TOK[u][:, s, 0, 64 * e:64 * e + 64], start=False, stop=False)
                              K.mm(o, GT[u][ps_, s, :], tcur[ps_, hg * 2 + s, :], start=False, stop=True)
                          for s in range(2):
                              K.stt("dve", Tst[:, hg * 2 + s, :], Tst[:, hg * 2 + s, :], PC[:, s, c:c + 1],
                                    pt[:, s * 64:(s + 1) * 64], ALU.mult, ALU.add)
                          K.copy("act", tnew[:, hg * 2:hg * 2 + 2, :], Tst[:, hg * 2:hg * 2 + 2, :])
                          tb_i[0][hg] += 1
                          if want_y:
                              K.copy("act", ysb, py[:, 0:256])
                          yield
                          if not want_y:
                              continue
                          pgt = psum()
                          K.mm(pgt[:, 0:256], sg[:, 0, c * CH:(c + 1) * CH], lw_b[:, 2 * RW + hg * 256:2 * RW + (hg + 1) * 256],
                               start=True, stop=False)
                          K.mm(pgt[:, 0:256], sg[:, 1, c * CH:(c + 1) * CH], lw_b[:, 3 * RW + hg * 256:3 * RW + (hg + 1) * 256],
                               start=False, stop=True)
                          for s in range(2):
                              K.mm(pgt[:, 256 + 2 * s:258 + 2 * s], rkb[:, s, c * CH:(c + 1) * CH], headsel_b)
                          K.copy("act", gst[:, 16:20], pgt[:, 256:260])
                          y3 = ysb.re("p (h v) -> p h v", h=4)
                          K.red("dve", gst[:, 0:4], y3)
                          K.tt("dve", ysq, ysb, ysb, ALU.mult)
                          K.red("dve", gst[:, 4:8], ysq.re("p (h v) -> p h v", h=4))
                          K.ts("dve", gst[:, 0:4], gst[:, 0:4], 1.0 / 64)
                          K.tt("dve", gst[:, 8:12], gst[:, 0:4], gst[:, 0:4], ALU.mult)
                          K.stt("dve", gst[:, 4:8], gst[:, 4:8], 1.0 / 64, gst[:, 8:12], ALU.mult, ALU.subtract)
                          K.ts("dve", gst[:, 4:8], gst[:, 4:8], GN_EPS, None, ALU.add)
                          K.act(gst[:, 8:12], gst[:, 4:8], AF.Sqrt)
                          K.recip(gst[:, 12:16], gst[:, 8:12])
                          yn3 = yn.re("p (h v) -> p h v", h=4)
                          K.tt("dve", yn3, y3, gst[:, 0:4].bc([128, 4, 64], 2), ALU.subtract)
                          K.tt("dve", yn3, yn3, gst[:, 12:16].bc([128, 4, 64], 2), ALU.mult)
                          K.tt("dve", yn, yn, lnw[:, hg * 256:(hg + 1) * 256], ALU.mult)
                          K.tt("dve", yn, yn, lnb[:, hg * 256:(hg + 1) * 256], ALU.add)
                          K.tt("dve", bon.re("p (s e v) -> p s e v", s=2, e=2),
                               TOK[u][:, :, 0, :].re("p s (e v) -> p s e v", e=2),
                               gst[:, 16:20].re("p (s e) -> p s e", s=2).bc([128, 2, 2, 64], 3), ALU.mult)
                          K.tt("dve", yn, yn, bon, ALU.add)
                          K.tt("dve", ytb, yn, pgt[:, 0:256], ALU.mult)
                          pb = psum()
                          pbb = pb.bitcast(BF16)
                          for s in range(2):
                              K.tr(pbb[:, s * 128:(s + 1) * 128], ytb[:, s * 128:(s + 1) * 128], ident_b)
                          K.copy("act", yTs[:, :, c * CH:(c + 1) * CH], pbb[:, 0:256].re("p (s t) -> p s t", s=2))
                          yield
                  if out_tile:
                      c_lo = max(0, OC0 - ti * 4)
                      tk0 = (ti * 4 + c_lo - OC0) * CH
                      ncol = (4 - c_lo) * CH
                      for s in range(2):
                          K.dma("sp", yT_v[8 + hg * 2 + s, :, tk0:tk0 + ncol], yTs[:, s, c_lo * CH:4 * CH])
                  yield

              def conv(ti):
                  c_lo = max(0, OC0 - ti * 4)
                  tk0 = (ti * 4 + c_lo - OC0) * CH
                  ncol = (4 - c_lo) * CH
                  for i in range(8):
                      pB = project(load_w(28 + i * 3 + 0))
                      pC = project(load_w(28 + i * 3 + 1))
                      pH = project(load_w(28 + i * 3 + 2))
                      K.copy("act", csb, pC)
                      K.copy("act", chb[:, 0:2], ccar[:, i, :])
                      K.tt("dve", chb[:, 2:TILE + 2], csb, pH, ALU.mult)
                      K.copy("act", ccar[:, i, :], chb[:, TILE:TILE + 2])
                      K.ts("dve", csb, chb[:, 0:TILE], vecs[:, V_CW + i:V_CW + i + 1])
                      K.stt("dve", csb, chb[:, 1:TILE + 1], vecs[:, V_CW + 8 + i:V_CW + 9 + i], csb, ALU.mult, ALU.add)
                      K.stt("dve", csb, chb[:, 2:TILE + 2], vecs[:, V_CW + 16 + i:V_CW + 17 + i], csb, ALU.mult, ALU.add)
                      K.tt("dve", ycv, csb, pB, ALU.mult)
                      K.dma("sp", yT_v[i, :, tk0:tk0 + ncol], ycv[:, c_lo * CH:4 * CH])
                      yield

              def chain(*gens):
                  for g in gens:
                      yield from g

              def count_segments(make):
                  sv = (dict(pool_i) if 'pool_i' in dir() else None, bank_i[0], wr_i[0], list(tb_i[0]))
                  _HALT[0] = True
                  n = sum(1 for _ in make())
                  _HALT[0] = False
                  bank_i[0] = sv[1]
                  wr_i[0] = sv[2]
                  tb_i[0] = sv[3]
                  return n

              def interleave(ga, gb, na, nb):
                  ia = ib = 0
                  la, lb = na > 0, nb > 0
                  while la or lb:
                      take_a = la and (not lb or ia * nb <= ib * na)
                      if take_a:
                          try:
                              next(ga)
                              ia += 1
                          except StopIteration:
                              la = False
                      else:
                          try:
                              next(gb)
                              ib += 1
                          except StopIteration:
                              lb = False

              PIPE = os.environ.get("KPIPE", "1") == "1"
              units = [(ti, hg) for ti in range(NT) for hg in range(4)]

              def p_stream(ti, hg):
                  gens = []
                  if hg == 0:
                      gens.append(front(ti))
                  gens.append(prep(ti, hg))
                  return gens

              for g in chain(*p_stream(0, 0)):
                  pass
              for ui, (ti, hg) in enumerate(units):
                  nxt = []
                  if hg == 3 and ti >= OT0:
                      nxt.append(conv(ti))
                  if ui + 1 < len(units):
                      nxt += p_stream(*units[ui + 1])
                  if PIPE:
                      def mk_nxt(ti=ti, hg=hg, ui=ui):
                          g = []
                          if hg == 3 and ti >= OT0:
                              g.append(conv(ti))
                          if ui + 1 < len(units):
                              g += p_stream(*units[ui + 1])
                          return chain(*g)
                      na = count_segments(lambda: scan(ti, hg))
                      nb = count_segments(mk_nxt)
                      interleave(scan(ti, hg), chain(*nxt), na, nb)
                  else:
                      for g in scan(ti, hg):
                          pass
                      for g in chain(*nxt):
                          pass
          _ck(8)
          S.barrier()

          with contextlib.ExitStack() as p15:
              wo_b = sbt(p15, "wo_b", [128, KC, D], BF16)
              K.dma("pool", wo_b.re("p k d -> p (k d)"), wout_v)
              gpm = sbt(p15, "gpm", [128, D], F32)
              gpf = sbt(p15, "gpf", [128, D], F32)
              K.dma("sp", gpm, V(bcv["g_post_mix"].partition_broadcast(128), R("bc3")))
              K.dma("sp", gpf, V(bcv["g_pre_ffn"].partition_broadcast(128), R("bc4")))
              ych = [sbt(p15, "ych%d" % i, [128, KC, CH], BF16) for i in range(2)]
              xin2 = [sbt(p15, "xin2_%d" % i, [128, D], F32) for i in range(2)]
              hsb = [sbt(p15, "hsb%d" % i, [128, D], F32) for i in range(2)]
              n2 = sbt(p15, "n2", [128, D], BF16)
              n2T = [sbt(p15, "n2T%d" % i, [128, KC, CH], BF16) for i in range(2)]
              st5 = sbt(p15, "st5", [128, 16], F32)
              def p15_load(oc):
                  K.dma("sp", ych[oc % 2], yT_v[:, :, oc * CH:(oc + 1) * CH].re("c p t -> p c t"))
                  K.dma("sp", xin2[oc % 2], xs_v[(OC0 + oc) * CH:(OC0 + oc + 1) * CH, :])

              p15_load(0)
              for oc in range(NOC):
                  yc = ych[oc % 2]
                  xi = xin2[oc % 2]
                  hs = hsb[oc % 2]
                  nt2 = n2T[oc % 2]
                  if oc + 1 < NOC:
                      p15_load(oc + 1)
                  pm = [psum() for _ in range(4)]
                  for blk in range(4):
                      for chc in range(KC):
                          K.mm(pm[blk], yc[:, chc, :], wo_b[:, chc, blk * 512:(blk + 1) * 512],
                               start=(chc == 0), stop=(chc == KC - 1))
                  K.memset("dve", st5[:, 0:4], 0.0)
                  for blk in range(4):
                      K.act(n2[:, blk * 512:(blk + 1) * 512], pm[blk], AF.Square, accum=st5[:, blk:blk + 1])
                  K.red("dve", st5[:, 4:5], st5[:, 0:4])
                  K.ts("dve", st5[:, 4:5], st5[:, 4:5], 1.0 / D, RMS_EPS, ALU.mult, ALU.add)
                  K.act(st5[:, 5:6], st5[:, 4:5], AF.Sqrt)
                  K.recip(st5[:, 6:7], st5[:, 5:6])
                  for blk in range(4):
                      sl = slice(blk * 512, (blk + 1) * 512)
                      K.stt("dve", hs[:, sl], pm[blk], st5[:, 6:7], gpm[:, sl], ALU.mult, ALU.mult)
                  K.tt("dve", hs, hs, xi, ALU.add)
                  K.dma("sp", h_v[oc * CH:(oc + 1) * CH, :], hs)
                  K.memset("dve", st5[:, 8:9], 0.0)
                  K.act(n2, hs, AF.Square, accum=st5[:, 8:9])
                  K.ts("dve", st5[:, 9:10], st5[:, 8:9], 1.0 / D, RMS_EPS, ALU.mult, ALU.add)
                  K.act(st5[:, 10:11], st5[:, 9:10], AF.Sqrt)
                  K.recip(st5[:, 11:12], st5[:, 10:11])
                  K.stt("dve", n2, hs, st5[:, 11:12], gpf, ALU.mult, ALU.mult)
                  for a in range(2):
                      pb = psum()
                      pbb = pb.bitcast(BF16)
                      for k8 in range(8):
                          kc = a * 8 + k8
                          K.tr(pbb[:, k8 * 128:(k8 + 1) * 128], n2[:, kc * 128:(kc + 1) * 128], ident_b)
                      K.copy("act", nt2[:, a * 8:(a + 1) * 8, :], pbb.re("p (k t) -> p k t", k=8))
                  if oc == 0:
                      K.ts("dve", nt2[:, :, 0:HALO], nt2[:, :, 0:HALO], hk)
                  K.dma("sp", n2T_v[:, :, oc * CH:(oc + 1) * CH].re("c p t -> p c t"), nt2)
          S.barrier()

          with contextlib.ExitStack() as p2:
              gpo = sbt(p2, "gpo", [128, D], F32)
              K.dma("sp", gpo, V(bcv["g_post_ffn"].partition_broadcast(128), R("bc5")))
              facc = sbt(p2, "facc", [128, STC, D], F32)
              n2s = sbt(p2, "n2s", [128, KC, STC * CH], BF16)
              wg_r = [sbt(p2, "wg%d" % i, [128, 2, KC, 128], BF16) for i in range(2)]
              wd_r = [sbt(p2, "wd%d" % i, [128, D], BF16) for i in range(4)]
              gbuf = sbt(p2, "gbuf", [128, STC * CH + 2], F32)
              cvt = sbt(p2, "cvt", [128, TILE], F32)
              slu = sbt(p2, "slu", [128, TILE], F32)
              actT = [sbt(p2, "actT%d" % i, [128, STC * CH], BF16) for i in range(4)]
              hld = sbt(p2, "hld", [128, D], F32)
              st6 = sbt(p2, "st6", [128, 8], F32)
              K.memset("dve", gbuf[:, 0:2], 0.0)
              st_starts = []
              s0 = 0
              while True:
                  st_starts.append(s0)
                  if s0 + STC >= NOC:
                      break
                  s0 += STC - 1
              for sti, cs0 in enumerate(st_starts):
                  nch = min(STC, NOC - cs0)
                  ntk = nch * CH
                  first_out = 0 if sti == 0 else 1
                  K.dma("sp", n2s[:, :, 0:ntk], n2T_v[:, :, cs0 * CH:cs0 * CH + ntk].re("c p t -> p c t"))
                  ttiles = [(a, min(TILE, ntk - a)) for a in range(0, ntk, TILE)]
                  for fb0 in range(0, NFB, 2):
                      ats, wds = [], []
                      for g2 in range(2):
                          fb = fb0 + g2
                          wgb = wg_r[fb % 2]
                          wdb = wd_r[fb % 4]
                          at = actT[fb % 4]
                          ats.append(at)
                          wds.append(wdb)
                          K.dma("pool", wgb.re("p g k j -> p g (k j)"), wgu_v[2 * fb:2 * fb + 2].re("g p f -> p g f"))
                          K.dma("pool", wdb, wdn_v[fb])
                          for (a0, tn) in ttiles:
                              pgg = psum()
                              pu = psum()
                              for kc in range(KC):
                                  K.mm(pgg[:, 0:tn], wgb[:, 0, kc, :], n2s[:, kc, a0:a0 + tn], start=(kc == 0), stop=(kc == KC - 1))
                              for kc in range(KC):
                                  K.mm(pu[:, 0:tn], wgb[:, 1, kc, :], n2s[:, kc, a0:a0 + tn], start=(kc == 0), stop=(kc == KC - 1))
                              K.copy("act", gbuf[:, 2 + a0:2 + a0 + tn], pgg[:, 0:tn])
                              K.ts("dve", cvt[:, 0:tn], gbuf[:, a0:a0 + tn], vecs[:, V_CF + fb:V_CF + fb + 1])
                              K.stt("dve", cvt[:, 0:tn], gbuf[:, a0 + 1:a0 + 1 + tn], vecs[:, V_CF + NFB + fb:V_CF + NFB + fb + 1],
                                    cvt[:, 0:tn], ALU.mult, ALU.add)
                              K.stt("dve", cvt[:, 0:tn], gbuf[:, a0 + 2:a0 + 2 + tn], vecs[:, V_CF + 2 * NFB + fb:V_CF + 2 * NFB + fb + 1],
                                    cvt[:, 0:tn], ALU.mult, ALU.add)
                              K.act(slu[:, 0:tn], cvt[:, 0:tn], AF.Silu)
                              K.tt("dve", at[:, a0:a0 + tn], slu[:, 0:tn], pu[:, 0:tn], ALU.mult)
                      for ci in range(nch):
                          for blk in range(4):
                              pf = psum()
                              K.mm(pf, ats[0][:, ci * CH:(ci + 1) * CH], wds[0][:, blk * 512:(blk + 1) * 512], start=True, stop=False)
                              K.mm(pf, ats[1][:, ci * CH:(ci + 1) * CH], wds[1][:, blk * 512:(blk + 1) * 512], start=False, stop=True)
                              sl = slice(blk * 512, (blk + 1) * 512)
                              if fb0 == 0:
                                  K.copy("act", facc[:, ci, sl], pf)
                              else:
                                  K.tt("dve", facc[:, ci, sl], facc[:, ci, sl], pf, ALU.add)
                  for ci in range(first_out, nch):
                      oc = cs0 + ci
                      K.dma("sp", hld, h_v[oc * CH:(oc + 1) * CH, :])
                      K.memset("dve", st6[:, 0:1], 0.0)
                      K.act(n2s.re("p k t -> p (k t)")[:, 0:D], facc[:, ci, :], AF.Square, accum=st6[:, 0:1])
                      K.ts("dve", st6[:, 1:2], st6[:, 0:1], 1.0 / D, RMS_EPS, ALU.mult, ALU.add)
                      K.act(st6[:, 2:3], st6[:, 1:2], AF.Sqrt)
                      K.recip(st6[:, 3:4], st6[:, 2:3])
                      K.stt("dve", facc[:, ci, :], facc[:, ci, :], st6[:, 3:4], gpo, ALU.mult, ALU.mult)
                      K.tt("dve", facc[:, ci, :], facc[:, ci, :], hld, ALU.add)
                      K.dma("sp", out_v[oc * CH:(oc + 1) * CH, :], facc[:, ci, :])
          S.barrier()
        except _Stop:
            pass
        S.barrier()
        S.emit()
    return nc


def _blk(w):
    return np.ascontiguousarray(w.reshape(KC, 128, 128).transpose(1, 0, 2)).reshape(128, KC * 128)


def _pad_cols(w, n):
    out = np.zeros((w.shape[0], n), w.dtype)
    out[:, :w.shape[1]] = w
    return out


def _consts():
    c = np.zeros((128, NCONST), np.float32)
    j = np.arange(128)[:, None]
    t = np.arange(128)[None, :]
    c[:, C_ID:C_ID + 128] = np.eye(128)
    strict = (t > j).astype(np.float32)
    incl = (t >= j).astype(np.float32)
    c[:, C_M4:C_M4 + 512] = np.concatenate([strict, incl, strict, incl], 1)
    low = (j > t).astype(np.float32)
    c[:, C_ML:C_ML + 512] = np.concatenate([low] * 4, 1)
    c[:, C_BO:C_BO + 128] = (j // 64 == t // 64).astype(np.float32)
    c[:, C_HS:C_HS + 2] = (j // 64 == np.arange(2)[None, :]).astype(np.float32)
    return c


def prepare(inputs, NT, RANGE):
    f = lambda k: np.asarray(inputs[k], np.float32)
    w_in = f("w_in")
    RB = 3072
    cols = []
    cols.append(_pad_cols(w_in[:, RB + 3072:RB + 3168], 128))
    cols.append(_pad_cols(w_in[:, RB + 3168:RB + 3264], 128))
    cols.append(w_in[:, RB + 3264:RB + 3392])
    cols.append(w_in[:, RB + 3392:RB + 3520])
    for i in range(8):
        for base in (0, 1024, 2048):
            cols.append(w_in[:, RB + base + i * 128:RB + base + (i + 1) * 128])
    for i in range(8):
        for base in (0, 1024, 2048):
            cols.append(w_in[:, base + i * 128:base + (i + 1) * 128])
    win = np.stack([_blk(c) for c in cols], 0)
    wgu_full = f("w_gate_up")
    wgu = np.stack([_blk(wgu_full[:, g * DFF + fb * 128:g * DFF + (fb + 1) * 128]) for fb in range(NFB) for g in range(2)], 0)
    wdn = np.ascontiguousarray(f("w_down").reshape(NFB, 128, D))
    wout = np.ascontiguousarray(f("w_out").reshape(KC, 128, D).transpose(1, 0, 2)).reshape(128, KC * D)
    lora = np.zeros((128, 4 * RW), np.float32)
    lora[:96, 0:RW] = f("w_decay_up")
    lora[:96, RW:2 * RW] = f("a_up")
    gup = f("g_up")
    lora[:, 2 * RW:3 * RW] = gup[0:128]
    lora[:, 3 * RW:4 * RW] = gup[128:256]
    mu = f("mu_rwkv")
    vecs = np.zeros((128, NV), np.float32)
    vecs[:96, V_MU + 0] = mu[3072:3168]
    vecs[:96, V_MU + 1] = mu[3168:3264]
    vecs[:, V_MU + 2] = mu[3264:3392]
    vecs[:, V_MU + 3] = mu[3392:3520]
    for i in range(8):
        for qi, base in enumerate((0, 1024, 2048)):
            vecs[:, V_MU + 4 + i * 3 + qi] = mu[base + i * 128:base + (i + 1) * 128]
    for name, col in (("w0", V_W0), ("a0", V_A0), ("k_k", V_KK), ("k_a", V_KA)):
        vecs[:, col:col + 8] = f(name).reshape(8, 128).T
    vecs[:, V_RK:V_RK + 8] = f("r_k").reshape(8, 128).T
    cw = f("conv_w")
    for tap in range(3):
        vecs[:, V_CW + tap * 8:V_CW + tap * 8 + 8] = cw[tap].reshape(8, 128).T
    cfw = f("conv_ffn")
    for tap in range(3):
        vecs[:, V_CF + tap * NFB:V_CF + (tap + 1) * NFB] = cfw[tap].reshape(NFB, 128).T
    shared = {"win": win, "wgu": wgu, "wdn": wdn, "wout": wout, "lora": lora, "consts": _consts()}
    for n in ("g_pre_mix", "g_post_mix", "g_pre_ffn", "g_post_ffn", "ln_x_w", "ln_x_b"):
        shared[n] = f(n).reshape(1, -1)
    x = f("x")
    meta = f("meta_tokens")
    SEQ = NT * TILE
    in_maps = []
    for c in range(8):
        b, q = c // 4, c % 4
        seq = np.concatenate([meta, x[b]], 0)
        n_real = (q + 1) * RANGE
        xs = np.zeros((SEQ, D), np.float32)
        xs[SEQ - n_real:] = seq[:n_real]
        v = vecs.copy()
        v[:, V_HK] = 0.0 if q == 0 else 1.0
        m = dict(shared)
        m["xs"] = xs
        m["vecs"] = v
        in_maps.append(m)
    return in_maps


def kernel(**inputs):
    x = np.asarray(inputs["x"])
    B, L, _ = x.shape
    tot = N_META + L
    RANGE = tot // 4
    NT = -(-(tot + 2) // TILE)
    nc = build_nc(NT, RANGE)
    in_maps = prepare(inputs, NT, RANGE)
    dbg = os.environ.get("KCORES")
    if dbg is not None:
        sel = [int(t) for t in dbg.split(",")]
        res1 = run_bass_kernel_spmd(nc, [in_maps[c] for c in sel], core_ids=list(range(len(sel))))
        return {c: {k: np.asarray(v) for k, v in res1.results[i].items()} for i, c in enumerate(sel)}
    res = run_bass_kernel_spmd(nc, in_maps, core_ids=list(range(8)))
    SEQ = NT * TILE
    OC0 = (SEQ - RANGE - 2) // CH
    off = SEQ - RANGE - OC0 * CH
    full = np.zeros((B, tot, D), np.float32)
    for c in range(8):
        b, q = c // 4, c % 4
        o = np.asarray(res.results[c]["out"])
        full[b, q * RANGE:(q + 1) * RANGE] = o[off:off + RANGE]
    return np.ascontiguousarray(full[:, N_META:])
```
